# Optimizing a Trainium2 kernel written in Bass

```python
import jax, jax.numpy as jnp
from jax import lax
import numpy as np

D_MODEL = 1024
BATCH = 4
SEQ = 4096
DEPTH = 1
DEC_BATCH = 128
DEC_SEQ = 8
PAST_LEN = 8192
PAGE_SIZE = 128

N_HEADS = 8
Q_LORA = 384
KV_LORA = 256
NOPE_DIM = 128
ROPE_DIM = 64
V_DIM = D_MODEL // N_HEADS
D_CONV = D_MODEL
CONV_W = 3
D_FF = 4 * D_MODEL
Q_BLOCK = 128
ROPE_THETA = 10000.0
EPS = 1e-6
ATTN_SCALE = (NOPE_DIM + ROPE_DIM) ** -0.5
IN_SPLITS = (Q_LORA, Q_LORA + KV_LORA, Q_LORA + KV_LORA + ROPE_DIM,
             Q_LORA + KV_LORA + ROPE_DIM + D_CONV,
             Q_LORA + KV_LORA + ROPE_DIM + 2 * D_CONV,
             Q_LORA + KV_LORA + ROPE_DIM + 3 * D_CONV)
D_IN = Q_LORA + KV_LORA + ROPE_DIM + 3 * D_CONV + 2 * D_MODEL

kernel_name = "hybrid_mla_shortconv_adaln_decoder_step"


def rmsnorm(x, g):
    xf = x.astype(jnp.float32)
    r = lax.rsqrt(jnp.mean(xf * xf, axis=-1, keepdims=True) + EPS)
    return (xf * r).astype(x.dtype) * g


def rope_angles(pos):
    inv = ROPE_THETA ** (-jnp.arange(0, ROPE_DIM, 2, dtype=jnp.float32) / ROPE_DIM)
    ang = pos.astype(jnp.float32)[:, None] * inv[None, :]
    return jnp.cos(ang), jnp.sin(ang)


def apply_rope(x, cos, sin):
    shape = (1, cos.shape[0]) + (1,) * (x.ndim - 3) + (cos.shape[1],)
    cos = cos.reshape(shape).astype(x.dtype)
    sin = sin.reshape(shape).astype(x.dtype)
    x1, x2 = jnp.split(x, 2, axis=-1)
    return jnp.concatenate([x1 * cos - x2 * sin, x2 * cos + x1 * sin], axis=-1)


def mla_attend(q_lat, q_rope, q_pos, ckv, krope, k_pos):
    s = jnp.einsum('bthc,bsc->bhts', q_lat, ckv) + jnp.einsum('bthr,bsr->bhts', q_rope, krope)
    s = s.astype(jnp.float32) * ATTN_SCALE
    mask = k_pos[None, :] <= q_pos[:, None]
    s = jnp.where(mask[None, None], s, -jnp.inf)
    p = jax.nn.softmax(s, axis=-1).astype(ckv.dtype)
    return jnp.einsum('bhts,bsc->bthc', p, ckv)


def prompt_attend(q_lat, q_rope, ckv, krope, pos):
    b, s = q_lat.shape[:2]
    nb = s // Q_BLOCK
    ql = q_lat.reshape(b, nb, Q_BLOCK, N_HEADS, KV_LORA).transpose(1, 0, 2, 3, 4)
    qr = q_rope.reshape(b, nb, Q_BLOCK, N_HEADS, ROPE_DIM).transpose(1, 0, 2, 3, 4)
    qp = pos.reshape(nb, Q_BLOCK)
    out = lax.map(lambda a: mla_attend(a[0], a[1], a[2], ckv, krope, pos), (ql, qr, qp))
    return out.transpose(1, 0, 2, 3, 4).reshape(b, s, N_HEADS, KV_LORA)


def short_conv(v, buf, conv_w):
    full = jnp.concatenate([buf, v], axis=1)
    t = v.shape[1]
    z = conv_w[0] * full[:, 0:t]
    for k in range(1, CONV_W):
        z = z + conv_w[k] * full[:, k:k + t]
    return z, full[:, full.shape[1] - (CONV_W - 1):]


def decoder_layer(x, c, pos, conv_buf, past_ckv, past_krope, past_pos, p):
    b, t, _ = x.shape
    mod = jnp.einsum('bd,de->be', jax.nn.silu(c), p['w_ada']) + p['b_ada']
    sh1, sc1, g1, sh2, sc2, g2 = [m[:, None, :] for m in jnp.split(mod, 6, axis=-1)]
    u = rmsnorm(x, p['g_attn']) * (1 + sc1) + sh1
    proj = jnp.einsum('btd,de->bte', u, p['w_in'])
    q_a, ckv, kr, b_g, c_g, xin, gates = jnp.split(proj, IN_SPLITS, axis=-1)
    q = jnp.einsum('btq,qe->bte', rmsnorm(q_a, p['g_q']), p['w_q_b'])
    q = q.reshape(b, t, N_HEADS, NOPE_DIM + ROPE_DIM)
    q_nope, q_rope = q[..., :NOPE_DIM], q[..., NOPE_DIM:]
    cos, sin = rope_angles(pos)
    q_rope = apply_rope(q_rope, cos, sin)
    ckv = rmsnorm(ckv, p['g_kv'])
    kr = apply_rope(kr, cos, sin)
    w_kvb = p['w_kv_b'].reshape(KV_LORA, N_HEADS, NOPE_DIM + V_DIM)
    w_uk, w_uv = w_kvb[..., :NOPE_DIM], w_kvb[..., NOPE_DIM:]
    q_lat = jnp.einsum('bthn,chn->bthc', q_nope, w_uk)
    if past_ckv is None:
        o_lat = prompt_attend(q_lat, q_rope, ckv, kr, pos)
    else:
        keys_ckv = jnp.concatenate([past_ckv, ckv], axis=1)
        keys_kr = jnp.concatenate([past_krope, kr], axis=1)
        k_pos = jnp.concatenate([past_pos, pos])
        o_lat = mla_attend(q_lat, q_rope, pos, keys_ckv, keys_kr, k_pos)
    attn = jnp.einsum('bthc,chv->bthv', o_lat, w_uv).reshape(b, t, N_HEADS * V_DIM)
    z, new_buf = short_conv(c_g * xin, conv_buf, p['conv_w'])
    conv_out = b_g * z
    g_a, g_b = jnp.split(jax.nn.sigmoid(gates), 2, axis=-1)
    merged = g_a * attn + g_b * conv_out
    x = x + g1 * jnp.einsum('btd,de->bte', merged, p['w_o'])
    u2 = rmsnorm(x, p['g_mlp']) * (1 + sc2) + sh2
    hdn = jnp.square(jax.nn.relu(jnp.einsum('btd,df->btf', u2, p['w_1'])))
    x = x + g2 * jnp.einsum('btf,fd->btd', hdn, p['w_2'])
    return x, ckv, kr, new_buf


def setup_inputs(seed: int = 0) -> dict:
    key = jax.random.key(seed)
    ks = jax.random.split(key, 24)
    n_pages = PAST_LEN // PAGE_SIZE
    n_used = DEC_BATCH * n_pages
    n_pool = (n_used * 5) // 4
    f32 = jnp.float32
    nrm = lambda k, s, sc: jax.random.normal(k, s, f32) * sc
    page_table = jax.random.permutation(ks[0], n_pool)[:n_used].reshape(DEC_BATCH, n_pages).astype(jnp.int32)
    return {
        "x_prompt": nrm(ks[1], (BATCH, SEQ, D_MODEL), 1.0),
        "x_sample": nrm(ks[2], (DEC_BATCH, DEC_SEQ, D_MODEL), 1.0),
        "cache_ckv": nrm(ks[3], (DEPTH, n_pool, PAGE_SIZE, KV_LORA), 1.0),
        "cache_krope": nrm(ks[4], (DEPTH, n_pool, PAGE_SIZE, ROPE_DIM), 1.0),
        "state_conv": nrm(ks[5], (DEPTH, DEC_BATCH, CONV_W - 1, D_CONV), 0.5),
        "page_table": page_table,
        "c_prompt": nrm(ks[6], (BATCH, D_MODEL), 1.0),
        "c_sample": nrm(ks[7], (DEC_BATCH, D_MODEL), 1.0),
        "w_ada": nrm(ks[8], (DEPTH, D_MODEL, 6 * D_MODEL), 0.5 * D_MODEL ** -0.5),
        "b_ada": nrm(ks[9], (DEPTH, 6 * D_MODEL), 0.02),
        "g_attn": 1.0 + nrm(ks[10], (DEPTH, D_MODEL), 0.02),
        "w_in": nrm(ks[11], (DEPTH, D_MODEL, D_IN), D_MODEL ** -0.5),
        "g_q": 1.0 + nrm(ks[12], (DEPTH, Q_LORA), 0.02),
        "w_q_b": nrm(ks[13], (DEPTH, Q_LORA, N_HEADS * (NOPE_DIM + ROPE_DIM)), Q_LORA ** -0.5),
        "g_kv": 1.0 + nrm(ks[14], (DEPTH, KV_LORA), 0.02),
        "w_kv_b": nrm(ks[15], (DEPTH, KV_LORA, N_HEADS * (NOPE_DIM + V_DIM)), KV_LORA ** -0.5),
        "conv_w": nrm(ks[16], (DEPTH, CONV_W, D_CONV), CONV_W ** -0.5),
        "w_o": nrm(ks[17], (DEPTH, D_MODEL, D_MODEL), D_MODEL ** -0.5),
        "g_mlp": 1.0 + nrm(ks[18], (DEPTH, D_MODEL), 0.02),
        "w_1": nrm(ks[19], (DEPTH, D_MODEL, D_FF), D_MODEL ** -0.5),
        "w_2": nrm(ks[20], (DEPTH, D_FF, D_MODEL), D_FF ** -0.5),
        "g_final": 1.0 + nrm(ks[21], (D_MODEL,), 0.02),
    }


def reference(x_prompt, x_sample, cache_ckv, cache_krope, state_conv, page_table, c_prompt, c_sample,
              w_ada, b_ada, g_attn, w_in, g_q, w_q_b, g_kv, w_kv_b, conv_w, w_o, g_mlp, w_1, w_2, g_final):
    seq = x_prompt.shape[1]
    dec_b, dec_t = x_sample.shape[:2]
    past_len = page_table.shape[1] * PAGE_SIZE
    pos_p = jnp.arange(seq, dtype=jnp.int32)
    pos_s = past_len + jnp.arange(dec_t, dtype=jnp.int32)
    past_pos = jnp.arange(past_len, dtype=jnp.int32)
    hp, hs = x_prompt, x_sample
    ckv_p, kr_p, cv_p, ckv_s, kr_s, cv_s = [], [], [], [], [], []
    for l in range(DEPTH):
        p = {"w_ada": w_ada[l], "b_ada": b_ada[l], "g_attn": g_attn[l], "w_in": w_in[l],
             "g_q": g_q[l], "w_q_b": w_q_b[l], "g_kv": g_kv[l], "w_kv_b": w_kv_b[l],
             "conv_w": conv_w[l], "w_o": w_o[l], "g_mlp": g_mlp[l], "w_1": w_1[l], "w_2": w_2[l]}
        buf0 = jnp.zeros((hp.shape[0], CONV_W - 1, D_CONV), hp.dtype)
        hp, a, bq, cq = decoder_layer(hp, c_prompt, pos_p, buf0, None, None, None, p)
        ckv_p.append(a); kr_p.append(bq); cv_p.append(cq)
        past_ckv = cache_ckv[l][page_table].reshape(dec_b, past_len, KV_LORA)
        past_kr = cache_krope[l][page_table].reshape(dec_b, past_len, ROPE_DIM)
        hs, a, bq, cq = decoder_layer(hs, c_sample, pos_s, state_conv[l], past_ckv, past_kr, past_pos, p)
        ckv_s.append(a); kr_s.append(bq); cv_s.append(cq)
    y_prompt = rmsnorm(hp, g_final)
    y_sample = rmsnorm(hs, g_final)
    return (y_prompt, y_sample, jnp.stack(ckv_p), jnp.stack(kr_p), jnp.stack(cv_p),
            jnp.stack(ckv_s), jnp.stack(kr_s), jnp.stack(cv_s))
```

```python
import contextlib
import numpy as np
import concourse.bass as bass
import concourse.mybir as mybir
from concourse.bass_utils import run_bass_kernel_spmd

F32 = mybir.dt.float32
BF16 = mybir.dt.bfloat16
I32 = mybir.dt.int32
AF = mybir.ActivationFunctionType
ALU = mybir.AluOpType

PE, ACT, DVE, POOL, SP = "pe", "act", "dve", "pool", "sp"
COMPUTE = (PE, ACT, DVE, POOL)
NQ = 8

D = 1024
SEQ = 4096
NT = 32
NOWN = 16
NPOOL = 10240
EPS = 1e-6
SCALE = float((128 + 64) ** -0.5)
BIG = 30000.0
STAGE = 3
NTG = 2
NG = 128 * NTG


class Buf:
    __slots__ = ("name", "writers", "readers")

    def __init__(self, name=""):
        self.name = name
        self.writers = {}
        self.readers = {}


class Op:
    __slots__ = ("eng", "fn", "deps", "idx", "is_dma", "signal", "sem", "val", "qslot")

    def __init__(self, eng, fn, is_dma):
        self.eng = eng
        self.fn = fn
        self.is_dma = is_dma
        self.deps = []
        self.signal = False
        self.sem = None
        self.val = None
        self.qslot = 0


class Prog:
    def __init__(self, nc):
        self.nc = nc
        self.ops = {e: [] for e in (PE, ACT, DVE, POOL, SP)}
        self.ndma = {e: 0 for e in (PE, ACT, DVE, POOL, SP)}

    def _key(self, op):
        return ("dma", id(op)) if op.is_dma else op.eng

    def op(self, eng, meth, args, kwargs, reads=(), writes=(), dma=False):
        fn = (lambda e: getattr(e, meth)(*args, **kwargs))
        o = Op(eng, fn, dma)
        o.idx = len(self.ops[eng])
        deps = []
        for b in reads:
            for w in b.writers.values():
                deps.append(w)
        for b in writes:
            for w in b.writers.values():
                if w.is_dma or dma or w.eng != eng:
                    deps.append(w)
            for r in b.readers.values():
                if r.is_dma or dma or r.eng != eng:
                    deps.append(r)
        o.deps = [d for d in deps if not (d.eng == PE and eng == PE and not d.is_dma and not dma)]
        for b in reads:
            b.readers[self._key(o)] = o
        for b in writes:
            b.writers = {self._key(o): o}
            b.readers = {}
        if dma:
            o.qslot = self.ndma[eng]
            self.ndma[eng] += 1
        self.ops[eng].append(o)
        return o

    def pe(self, meth, *args, reads=(), writes=(), **kw):
        return self.op(PE, meth, args, kw, reads, writes)

    def act(self, meth, *args, reads=(), writes=(), **kw):
        return self.op(ACT, meth, args, kw, reads, writes)

    def dve(self, meth, *args, reads=(), writes=(), **kw):
        return self.op(DVE, meth, args, kw, reads, writes)

    def pool(self, meth, *args, reads=(), writes=(), **kw):
        return self.op(POOL, meth, args, kw, reads, writes)

    def dma(self, meth, *args, reads=(), writes=(), q=SP, **kw):
        return self.op(q, meth, args, kw, reads, writes, dma=True)

    def emit(self):
        nc = self.nc
        for e in self.ops:
            for o in self.ops[e]:
                for d in o.deps:
                    d.signal = True
        with contextlib.ExitStack() as st:
            csem = {e: st.enter_context(nc.semaphore("s_" + e)) for e in COMPUTE}
            qsem = {}
            for e in self.ops:
                if self.ndma[e]:
                    qsem[e] = [st.enter_context(nc.semaphore("q_%s_%d" % (e, i))) for i in range(NQ)]
            for e in self.ops:
                c = 0
                for o in self.ops[e]:
                    if o.is_dma:
                        o.sem = qsem[e][o.qslot % NQ]
                        o.val = 16 * (o.qslot // NQ + 1)
                    elif o.signal:
                        c += 1
                        o.sem, o.val = csem[e], c
            block = st.enter_context(nc.Block())
            handles = {PE: block.tensor, ACT: block.scalar, DVE: block.vector, POOL: block.gpsimd, SP: block.sync}

            def make(e):
                def body(eng):
                    known = {}
                    for o in self.ops[e]:
                        waits = {}
                        for d in o.deps:
                            kk = id(d.sem)
                            if kk not in waits or waits[kk][1] < d.val:
                                waits[kk] = (d.sem, d.val)
                        if o.is_dma and o.qslot >= NQ:
                            s = qsem[e][o.qslot % NQ]
                            v = 16 * (o.qslot // NQ)
                            kk = id(s)
                            if kk not in waits or waits[kk][1] < v:
                                waits[kk] = (s, v)
                        for kk, (s, v) in waits.items():
                            if known.get(kk, 0) >= v:
                                continue
                            eng.wait_ge(s, v)
                            known[kk] = v
                        inst = o.fn(eng)
                        if o.is_dma:
                            inst.then_inc(o.sem, 16)
                        elif o.signal:
                            inst.then_inc(o.sem, 1)
                    if e == SP:
                        for qe in qsem:
                            n = self.ndma[qe]
                            for slot in range(min(NQ, n)):
                                last = ((n - 1 - slot) // NQ) + 1
                                eng.wait_ge(qsem[qe][slot], 16 * last)
                return body

            for e in (SP, POOL, ACT, DVE, PE):
                if self.ops[e] or e == SP:
                    handles[e](make(e))


def build_program():
    nc = bass.Bass("TRN2", target_bir_lowering=False)
    P = Prog(nc)

    def din(name, shape, dt=F32):
        return nc.dram_tensor(name, list(shape), dt, kind="ExternalInput").ap()

    def dout(name, shape):
        return nc.dram_tensor(name, list(shape), F32, kind="ExternalOutput").ap()

    _n = [0]

    def sb(shape, dt=F32, name=None):
        _n[0] += 1
        return nc.alloc_sbuf_tensor(name or ("t%d" % _n[0]), list(shape), dt)

    xs = din("xs", [SEQ, D])
    xo = din("xo", [NOWN * 128, D])
    xh = din("xh", [32, D])
    xd = din("xd", [128, D])
    cc = din("cc", [17, D])
    pt = din("pt", [128, 8], I32)
    ckv_pool = din("ckv_pool", [NPOOL, 128 * 256])
    kr_pool = din("kr_pool", [NPOOL, 128 * 64])
    sconv = din("sconv", [32, D])
    w_ada = din("w_ada", [D, 6 * D])
    badaT = din("badaT", [128, 48])
    bada_g = din("bada_g", [2, D])
    gattnT = din("gattnT", [128, 8])
    gmlpT = din("gmlpT", [128, 8])
    gqT = din("gqT", [128, 3])
    gkv = din("gkv", [256])
    gfin = din("gfin", [D])
    w_kvin = din("w_kvin", [D, 384])
    w_main = din("w_main", [D, 5504])
    w_q = din("w_q", [384, 2048])
    w_ukT = din("w_ukT", [128, 8 * 256])
    w_uv = din("w_uv", [256, 8 * 128])
    convT = din("convT", [128, 8 * 3])
    w_o = din("w_o", [D, D])
    w_1 = din("w_1", [D, 4 * D])
    w_2 = din("w_2", [4 * D, D])
    ident_d = din("ident", [128, 128])
    masks_d = din("masks", [128, 2 * 128])
    hmask_d = din("hmask", [128, 16])
    cs_tok_p = din("cs_tok_p", [SEQ, 128])
    cs_tok_s = din("cs_tok_s", [128, 128])
    cs_feat_p = din("cs_feat_p", [64, 2 * NOWN * 128])
    cs_feat_s = din("cs_feat_s", [64, 2 * 128])
    smask_d = din("smask", [128, 8 * 128])
    augc_d = din("augc", [2, 2 * 128])
    kind_d = din("kind", [2, 128])

    y_p = dout("y_p", [NOWN * 128, D])
    y_s = dout("y_s", [128, D])
    ckv_p = dout("ckv_p", [SEQ, 256])
    kr_p = dout("kr_p", [SEQ, 64])
    conv_p = dout("conv_p", [2, D])
    ckv_s = dout("ckv_s", [128, 256])
    kr_s = dout("kr_s", [128, 64])
    conv_s = dout("conv_s", [32, D])

    ps = [nc.alloc_psum_tensor("ps%d" % i, [128, 512], F32) for i in range(8)]
    bps = [Buf("ps%d" % i) for i in range(8)]
    rr = {"g": 0}

    DEFB = [0, 1, 2, 3, 4, 5, 6, 7]

    def psum(group=DEFB, key="g"):
        i = group[rr.get(key, 0) % len(group)]
        rr[key] = rr.get(key, 0) + 1
        return ps[i], bps[i]

    def load(dst_ap, src_ap, buf, q=SP, reads=()):
        P.dma("dma_start", out=dst_ap, in_=src_ap, reads=list(reads), writes=[buf], q=q)

    ident_f = sb([128, 128]); b_ident_f = Buf()
    ident_b = sb([128, 128], BF16); b_ident_b = Buf()
    ones_b = sb([128, 128], BF16); b_ones = Buf()
    load(ident_f[:], ident_d, b_ident_f)
    P.dve("tensor_copy", ident_b[:], ident_f[:], reads=[b_ident_f], writes=[b_ident_b])
    P.dve("memset", ones_b[:], 1.0, writes=[b_ones])
    masks_f = sb([128, 256]); b_masks_f = Buf()
    masks = sb([128, 256], BF16); b_masks = Buf()
    load(masks_f[:], masks_d, b_masks_f)
    P.dve("tensor_copy", masks[:], masks_f[:], reads=[b_masks_f], writes=[b_masks])
    hmask = sb([128, 16]); b_hmask = Buf()
    load(hmask[:], hmask_d, b_hmask)
    gattn_t = sb([128, 8]); b_gattn = Buf(); load(gattn_t[:], gattnT, b_gattn)
    gmlp_t = sb([128, 8]); b_gmlp = Buf(); load(gmlp_t[:], gmlpT, b_gmlp)
    gq_t = sb([128, 3]); b_gq = Buf(); load(gq_t[:], gqT, b_gq)
    bada_t = sb([128, 48]); b_bada = Buf(); load(bada_t[:], badaT, b_bada)
    convw = sb([128, 24]); b_convw = Buf(); load(convw[:], convT, b_convw)
    gkv_bc = sb([128, 256]); b_gkv = Buf(); load(gkv_bc[:], gkv.partition_broadcast(128), b_gkv)
    gfin_bc = sb([128, D]); b_gfin = Buf(); load(gfin_bc[:], gfin.partition_broadcast(128), b_gfin)
    xn = [sb([128, D]) for _ in range(2)]; b_xn = [Buf(), Buf()]
    YO = xn; b_YO = b_xn
    XG = sb([128, NTG, D], name="XG"); b_XG = [Buf("XG%d" % t) for t in range(NTG)]
    UT = sb([128, 8, NG], BF16, name="UT"); b_UT = [Buf("UT%d" % t) for t in range(NTG)]
    REG = sb([128, 32, NG], BF16, name="REG"); b_REG = [Buf() for _ in range(32)]
    BAD = [XG[:, 0, :], XG[:, 1, :]]
    b_badag, b_badag1 = b_XG[0], b_XG[1]
    load(BAD[0], bada_g[0].partition_broadcast(128), b_badag)
    load(BAD[1], bada_g[1].partition_broadcast(128), b_badag1)

    wkv = sb([128, 8, 384], BF16); b_wkv = Buf()
    load(wkv[:], w_kvin.rearrange("(k p) c -> p k c", p=128), b_wkv, q=POOL)
    wq = sb([128, 3, 2048], BF16); b_wq = Buf()
    load(wq[:], w_q.rearrange("(k p) c -> p k c", p=128), b_wq, q=POOL)
    wuk = sb([128, 8, 256], BF16); b_wuk = Buf()
    load(wuk[:].rearrange("p h c -> p (h c)"), w_ukT, b_wuk, q=POOL)
    wuv = sb([128, 2, 8, 128], BF16); b_wuv = Buf()
    load(wuv[:].rearrange("p m h v -> p m (h v)"), w_uv.rearrange("(m p) c -> p m c", p=128), b_wuv, q=POOL)

    WUNIT = 1024
    NUNIT = 16
    wring = sb([128, NUNIT * WUNIT], BF16, name="wring")
    bunits = [Buf("wu%d" % i) for i in range(NUNIT)]
    wstate = {"pos": 0}

    def wload(src2d, kch, cols, reads=()):
        n = kch * cols
        nu = (n + WUNIT - 1) // WUNIT
        if wstate["pos"] + nu > NUNIT:
            wstate["pos"] = 0
        u0 = wstate["pos"]
        wstate["pos"] += nu
        bufs = bunits[u0:u0 + nu]
        view = wring[:, u0 * WUNIT:u0 * WUNIT + n].rearrange("p (k c) -> p k c", c=cols)
        P.dma("dma_start", out=view, in_=src2d.rearrange("(k p) c -> p k c", p=128),
              reads=list(reads), writes=bufs, q=POOL)
        return view, bufs

    w_main_b = nc.dram_tensor("w_main_b", [D, 5504], BF16, kind="Internal").ap()
    w_o_b = nc.dram_tensor("w_o_b", [D, D], BF16, kind="Internal").ap()
    w_1_b = nc.dram_tensor("w_1_b", [D, 4 * D], BF16, kind="Internal").ap()
    w_2_b = nc.dram_tensor("w_2_b", [4 * D, D], BF16, kind="Internal").ap()
    b_wconv = {"main": Buf(), "o": Buf(), "1": Buf(), "2": Buf()}

    def convert_weights():
        for (src, dst, key) in ((w_main, w_main_b, "main"), (w_o, w_o_b, "o"), (w_1, w_1_b, "1"), (w_2, w_2_b, "2")):
            rows, cols = src.shape
            for r0 in range(0, rows, 128):
                for c0 in range(0, cols, 2048):
                    c1 = min(cols, c0 + 2048)
                    P.dma("dma_start", out=dst[r0:r0 + 128, c0:c1], in_=src[r0:r0 + 128, c0:c1],
                          reads=[b_wconv[key]], writes=[b_wconv[key]], q=POOL)
                    yield None

    conv_gen = convert_weights()

    def convert_some(n):
        for _ in range(n):
            try:
                next(conv_gen)
            except StopIteration:
                return

    RA = sb([128, 3 * SEQ + NT * 258 + 8], BF16, name="RA")
    KT = RA[:, 0:3 * SEQ].rearrange("p (c t) -> p c t", c=3); b_KT = [Buf("KT%d" % t) for t in range(NT)]
    VP = RA[:, 3 * SEQ:3 * SEQ + NT * 258].rearrange("p (t f) -> p t f", f=258); b_VP = [Buf("VP%d" % t) for t in range(NT)]
    ra_off = [0]

    def carve(shape):
        n = 1
        for d_ in shape[1:]:
            n *= d_
        v = RA[:, ra_off[0]:ra_off[0] + n]
        ra_off[0] += n + (n % 2)
        assert ra_off[0] <= 3 * SEQ + NT * 258
        if len(shape) == 2:
            return v
        if len(shape) == 3:
            return v.rearrange("p (a b) -> p a b", b=shape[2])
        return v.rearrange("p (a b c) -> p a b c", b=shape[2], c=shape[3])

    def alias_barrier(new_bufs, old_bufs):
        merged = {}
        for ob in old_bufs:
            for dct in (ob.readers, ob.writers):
                for k, o in dct.items():
                    if k not in merged or merged[k].idx < o.idx:
                        merged[k] = o
        for nb in new_bufs:
            nb.readers.update(merged)
    b_KTaug = Buf()
    P.dve("memset", KT[64:65, 2, :], 1.0, writes=[b_KTaug])
    P.dve("memset", VP[:, :, 256:258], 1.0, writes=b_VP)

    modT = sb([128, 4, 8, 17]); b_modT = Buf()
    A1p = sb([128, 8]); B1p = sb([128, 8]); A2p = sb([128, 8]); B2p = sb([128, 8]); b_modp = Buf()
    A1s = sb([128, 8, 16]); A2s = sb([128, 8, 16]); b_mods = Buf()
    g1p = sb([128, D]); g2p = sb([128, D]); g1s = sb([128, D]); g2s = sb([128, D])
    b_g = {"g1p": Buf(), "g2p": Buf(), "g1s": Buf(), "g2s": Buf()}

    cct = xn[0][0:17, :]; b_cc = b_xn[0]; load(cct[:], cc, b_cc)
    sil = xn[1][0:17, :]; b_sil = b_xn[1]
    P.act("activation", out=sil[:], in_=cct[:], func=AF.Silu, reads=[b_cc], writes=[b_sil])
    silT = sb([128, 8, 17], BF16); b_silT = Buf()
    pt_, bp_ = psum()
    for k in range(8):
        P.pe("transpose", pt_[:, k * 17:(k + 1) * 17], sil[0:17, k * 128:(k + 1) * 128], ident_f[0:17, 0:17],
             reads=[b_sil, b_ident_f], writes=[bp_])
    P.dve("tensor_copy", silT[:].rearrange("p k s -> p (k s)"), pt_[:, 0:8 * 17], reads=[bp_], writes=[b_silT])
    nreg = 1024 // NG
    lhs_p = REG[:, 16:16 + nreg, :].rearrange("p a b -> p (a b)").rearrange("p (k c) -> p k c", c=128)
    lhs_s = REG[:, 16 + nreg:16 + 2 * nreg, :].rearrange("p a b -> p (a b)").rearrange("p (k c) -> p k c", c=128)
    b_lhs = Buf()
    b_lhs_all = [b_lhs] + b_REG[16:16 + 2 * nreg]
    P.dve("tensor_copy", lhs_p[:], silT[:, :, 0:1].to_broadcast([128, 8, 128]), reads=[b_silT], writes=b_lhs_all)
    for k in range(8):
        P.dve("tensor_copy", lhs_s[:, k, :].rearrange("p (s t) -> p s t", t=8),
                                           silT[:, k, 1:17].unsqueeze(2).to_broadcast([128, 16, 8]),
              reads=[b_silT, b_lhs], writes=b_lhs_all)

    for j in range(6):
        wv, wb = wload(w_ada[:, j * D:(j + 1) * D], 8, D)
        if j in (0, 1, 3, 4):
            jj = (0, 1, None, 2, 3)[j]
            po, bpo = psum()
            for m in range(8):
                for k in range(8):
                    P.pe("matmul", po[:, m * 17:(m + 1) * 17], lhsT=wv[:, k, m * 128:(m + 1) * 128],
                                                      rhs=silT[:, k, :], start=(k == 0), stop=(k == 7),
                         reads=wb + [b_silT], writes=[bpo])
            for m in range(8):
                P.dve("tensor_scalar", modT[:, jj, m, :], po[:, m * 17:(m + 1) * 17],
                                                                 bada_t[:, j * 8 + m:j * 8 + m + 1], None, op0=ALU.add,
                      reads=[bpo, b_bada, b_modT], writes=[b_modT])
        else:
            gi = 0 if j == 2 else 1
            for (lh, gt, bn) in ((lhs_p, g1p if gi == 0 else g2p, "g%dp" % (gi + 1)),
                                 (lhs_s, g1s if gi == 0 else g2s, "g%ds" % (gi + 1))):
                for n in range(2):
                    po, bpo = psum()
                    for k in range(8):
                        P.pe("matmul", po[:, :], lhsT=lh[:, k, :], rhs=wv[:, k, n * 512:(n + 1) * 512],
                                                                      start=(k == 0), stop=(k == 7),
                             reads=wb + b_lhs_all, writes=[bpo])
                    P.dve("tensor_tensor", out=gt[:, n * 512:(n + 1) * 512], in0=po[:, :],
                                                                              in1=BAD[gi][:, n * 512:(n + 1) * 512], op=ALU.add,
                          reads=[bpo, b_badag, b_badag1, b_g[bn]], writes=[b_g[bn]])
    convert_some(8)
    P.dve("scalar_tensor_tensor", out=A1p[:], in0=modT[:, 1, :, 0], scalar=1.0, in1=gattn_t[:], op0=ALU.add, op1=ALU.mult,
          reads=[b_modT, b_gattn], writes=[b_modp])
    P.dve("scalar_tensor_tensor", out=A2p[:], in0=modT[:, 3, :, 0], scalar=1.0, in1=gmlp_t[:], op0=ALU.add, op1=ALU.mult,
          reads=[b_modT, b_gmlp, b_modp], writes=[b_modp])
    P.dve("tensor_copy", B1p[:], modT[:, 0, :, 0], reads=[b_modT, b_modp], writes=[b_modp])
    P.dve("tensor_copy", B2p[:], modT[:, 2, :, 0], reads=[b_modT, b_modp], writes=[b_modp])
    P.dve("scalar_tensor_tensor", out=A1s[:], in0=modT[:, 1, :, 1:17], scalar=1.0,
                                           in1=gattn_t[:].unsqueeze(2).to_broadcast([128, 8, 16]), op0=ALU.add, op1=ALU.mult,
          reads=[b_modT, b_gattn], writes=[b_mods])
    P.dve("scalar_tensor_tensor", out=A2s[:], in0=modT[:, 3, :, 1:17], scalar=1.0,
                                           in1=gmlp_t[:].unsqueeze(2).to_broadcast([128, 8, 16]), op0=ALU.add, op1=ALU.mult,
          reads=[b_modT, b_gmlp, b_mods], writes=[b_mods])

    nj = D // NG
    JUNK = {"g": (REG[:, 0:nj, :].rearrange("p a b -> p (a b)"), b_REG[0:nj]),
            "k": (REG[:, 24:24 + nj, :].rearrange("p a b -> p (a b)"), b_REG[24:24 + nj])}
    TMP = [sb([128, 512]) for _ in range(2)]; b_TMP = [Buf(), Buf()]
    st_ss = sb([128, 8]); st_rs = sb([128, 8]); b_st = [Buf() for _ in range(8)]
    stc = {"n": 0}

    def rstd_of(src_ap, src_bufs, npart, ncols, inv_n, jsel="g"):
        i = stc["n"] % 8
        stc["n"] += 1
        junk, jb = JUNK[jsel]
        P.act("activation", out=junk[0:npart, 0:ncols], in_=src_ap, func=AF.Square, accum_out=st_ss[0:npart, i:i + 1],
              reads=list(src_bufs), writes=list(jb) + [b_st[i]])
        P.act("activation", out=st_rs[0:npart, i:i + 1], in_=st_ss[0:npart, i:i + 1], func=AF.Ln, scale=inv_n, bias=EPS,
              reads=[b_st[i]], writes=[b_st[i]])
        P.act("activation", out=st_rs[0:npart, i:i + 1], in_=st_rs[0:npart, i:i + 1], func=AF.Exp, scale=-0.5,
              reads=[b_st[i]], writes=[b_st[i]])
        return st_rs[0:npart, i:i + 1], b_st[i]

    xnc = {"n": 0}

    def make_uT(x_ap, x_bufs, npart, uT, uT_buf, col0, mod, jsel="g"):
        rs, rb = rstd_of(x_ap, x_bufs, npart, D, 1.0 / D, jsel)
        i = xnc["n"] % 2
        xnc["n"] += 1
        P.dve("tensor_scalar", xn[i][0:npart, :], x_ap, rs, None, op0=ALU.mult,
              reads=list(x_bufs) + [rb], writes=[b_xn[i]])
        for half in range(2):
            po, bpo = psum()
            for kk in range(4):
                k = half * 4 + kk
                P.pe("transpose", po[:, kk * 128:kk * 128 + npart], xn[i][0:npart, k * 128:(k + 1) * 128],
                                                              ident_f[0:npart, 0:npart],
                     reads=[b_xn[i], b_ident_f], writes=[bpo])
            for kk in range(4):
                k = half * 4 + kk
                if mod[0] == "p":
                    P.dve("tensor_scalar", uT[:, k, col0:col0 + npart], po[:, kk * 128:kk * 128 + npart],
                                                                       mod[1][:, k:k + 1], mod[2][:, k:k + 1], op0=ALU.mult, op1=ALU.add,
                          reads=[bpo, mod[3]], writes=[uT_buf])
                else:
                    tmp = sb_tmp_s
                    P.dve("tensor_tensor", out=tmp[:].rearrange("p (s t) -> p s t", t=8),
                                                                       in0=po[:, kk * 128:(kk + 1) * 128].rearrange("p (s t) -> p s t", t=8),
                                                                       in1=mod[1][:, k, :].unsqueeze(2).to_broadcast([128, 16, 8]), op=ALU.mult,
                          reads=[bpo, mod[3]], writes=[b_tmp_s])
                    P.dve("tensor_tensor", out=uT[:, k, col0:col0 + 128].rearrange("p (s t) -> p s t", t=8),
                                                         in0=tmp[:].rearrange("p (s t) -> p s t", t=8),
                                                         in1=mod[2][:, k, :].unsqueeze(2).to_broadcast([128, 16, 8]), op=ALU.add,
                          reads=[b_tmp_s, mod[3]], writes=[uT_buf])

    sb_tmp_s = TMP[0][:, 0:128]; b_tmp_s = b_TMP[0]

    kvf = [sb([128, 320]) for _ in range(2)]; b_kvf = [Buf(), Buf()]
    krb = [sb([128, 64], BF16) for _ in range(2)]; b_krb = [Buf(), Buf()]
    cst = [sb([128, 128]) for _ in range(2)]; b_cst = [Buf(), Buf()]
    krt = [sb([128, 128]) for _ in range(2)]; b_krt = [Buf(), Buf()]
    kvc = {"n": 0}

    def kv_build(uT, uT_buf, col0, cs_src, out_ckv, out_kr, vdst, vbuf, ktdst, ktbuf, split=False):
        i = kvc["n"] % 2
        kvc["n"] += 1
        load(cst[i][:], cs_src, b_cst[i])
        po, bpo = psum()
        for k in range(8):
            P.pe("matmul", po[:, 0:384], lhsT=uT[:, k, col0:col0 + 128], rhs=wkv[:, k, :], start=(k == 0), stop=(k == 7),
                 reads=[uT_buf, b_wkv], writes=[bpo])
        rs, rb = rstd_of(po[:, 0:256], [bpo], 128, 256, 1.0 / 256, "k")
        P.dve("scalar_tensor_tensor", out=kvf[i][:, 0:256], in0=po[:, 0:256], scalar=rs, in1=gkv_bc[:], op0=ALU.mult, op1=ALU.mult,
              reads=[bpo, rb, b_gkv], writes=[b_kvf[i]])
        P.dve("tensor_tensor", out=krt[i][:], in0=po[:, 256:384], in1=cst[i][:], op=ALU.mult,
              reads=[bpo, b_cst[i]], writes=[b_krt[i]])
        P.dve("tensor_tensor", out=kvf[i][:, 256:320], in0=krt[i][:, 0:64], in1=krt[i][:, 64:128], op=ALU.add,
              reads=[b_krt[i], b_kvf[i]], writes=[b_kvf[i]])
        P.dma("dma_start", out=out_ckv, in_=kvf[i][:, 0:256], reads=[b_kvf[i]])
        P.dma("dma_start", out=out_kr, in_=kvf[i][:, 256:320], reads=[b_kvf[i]])
        if split:
            return lambda: kv_build2(i, vdst, vbuf, ktdst, ktbuf)
        kv_build2(i, vdst, vbuf, ktdst, ktbuf)

    def kv_build2(i, vdst, vbuf, ktdst, ktbuf):
        P.act("activation", out=vdst, in_=kvf[i][:, 0:256], func=AF.Copy, reads=[b_kvf[i]], writes=[vbuf])
        P.act("activation", out=krb[i][:], in_=kvf[i][:, 256:320], func=AF.Copy, reads=[b_kvf[i]], writes=[b_krb[i]])
        pb, bpb = psum()
        pbb = pb[:, :].bitcast(BF16)
        P.pe("transpose", pbb[:, 0:128], vdst[:, 0:128], ident_b[:], reads=[vbuf, b_ident_b], writes=[bpb])
        P.pe("transpose", pbb[:, 128:256], vdst[:, 128:256], ident_b[:], reads=[vbuf, b_ident_b], writes=[bpb])
        P.pe("transpose", pbb[0:64, 256:384], krb[i][:], ident_b[:], reads=[b_krb[i], b_ident_b], writes=[bpb])
        P.act("activation", out=ktdst[:, 0:2, :], in_=pbb[:, 0:256].rearrange("p (c t) -> p c t", t=128), func=AF.Copy,
              reads=[bpb], writes=[ktbuf])
        P.act("activation", out=ktdst[0:64, 2, :], in_=pbb[0:64, 256:384], func=AF.Copy, reads=[bpb, ktbuf], writes=[ktbuf])

    uK = [UT[:, :, 0:128], UT[:, :, 128:256]]; b_uK = [b_UT[0], b_UT[1]]
    xk = [XG[:, 0, :], XG[:, 1, :]]; b_xk = [b_XG[0], b_XG[1]]
    modp = ("p", A1p, B1p, b_modp)
    NPRE = 16
    xk2 = sb([128, D], name="xk2"); b_xk2 = Buf()
    uK2 = sb([128, 8, 128], BF16, name="uK2"); b_uK2 = Buf()

    def stage_a(T):
        i = T % 2
        make_uT(xk[i], [b_xk[i]], 128, uK[i], b_uK[i], 0, modp, "k")

    def kv_tile_late(T):
        load(xk2[:], xs[T * 128:(T + 1) * 128, :], b_xk2)
        make_uT(xk2[:], [b_xk2], 128, uK2, b_uK2, 0, modp, "k")
        kv_build(uK2, b_uK2, 0, cs_tok_p[T * 128:(T + 1) * 128, :], ckv_p[T * 128:(T + 1) * 128, :], kr_p[T * 128:(T + 1) * 128, :],
                 VP[:, T, 0:256], b_VP[T], KT[:, :, T * 128:(T + 1) * 128], b_KT[T])

    load(xk[0], xs[0:128, :], b_xk[0])
    load(xk[1], xs[128:256, :], b_xk[1])
    stage_a(0)
    for T in range(NPRE):
        i = T % 2
        if T + 2 < NPRE:
            load(xk[i], xs[(T + 2) * 128:(T + 3) * 128, :], b_xk[i])
        part2 = kv_build(uK[i], b_uK[i], 0, cs_tok_p[T * 128:(T + 1) * 128, :], ckv_p[T * 128:(T + 1) * 128, :], kr_p[T * 128:(T + 1) * 128, :],
                         VP[:, T, 0:256], b_VP[T], KT[:, :, T * 128:(T + 1) * 128], b_KT[T], split=True)
        if T + 1 < NPRE:
            stage_a(T + 1)
        part2()
        convert_some(6)
    convert_some(1000)
    uH = sb([128, 8, 32], BF16); b_uH = Buf()
    load(xk[0][0:32, :], xh, b_xk[0])
    make_uT(xk[0][0:32, :], [b_xk[0]], 32, uH, b_uH, 0, modp, "k")

    GA = REG[:, 0:8, :]; b_GA = b_REG[0:8]
    GB = REG[:, 8:16, :]; b_GB = b_REG[8:16]
    MG = REG[:, 16:24, :]; b_MG = b_REG[16:24]
    HT = REG; b_HT = b_REG
    QA = sb([128, 3, NG], name="QA"); b_QA = Buf()
    SQ = sb([128, 3, NG], BF16, name="SQ"); b_SQ = Buf()
    RQ = sb([128, NG], name="RQ"); b_RQ = Buf()
    QN = sb([128, 3, NG], BF16, name="QN"); b_QN = Buf()
    QNOPE = [sb([128, NG], BF16) for _ in range(2)]; b_QNOPE = [Buf(), Buf()]
    QT = [sb([128, 3, NG], BF16) for _ in range(2)]; b_QT = [Buf(), Buf()]
    CSQ = sb([64, 2, NG], name="CSQ"); b_CSQ = Buf()
    RT = [sb([64, NG]) for _ in range(2)]; b_RT = [Buf(), Buf()]
    PT = [sb([128, NG], BF16) for _ in range(3)]; b_PT = [Buf() for _ in range(3)]
    OL = sb([128, NTG, 256], BF16, name="OL"); b_OL = [Buf() for _ in range(NTG)]
    OLT = sb([128, 2, NG], BF16, name="OLT"); b_OLT = Buf()
    RINV = sb([128, 4]); b_RINV = [Buf() for _ in range(4)]
    CG = [sb([128, NG + 8]) for _ in range(2)]; b_CG = [Buf(), Buf()]
    VPAD = [sb([128, max(NTG * 130, 160)]) for _ in range(2)]; b_VPAD = [Buf(), Buf()]
    ZC = [sb([128, NG]) for _ in range(2)]; b_ZC = [Buf(), Buf()]
    SG = [sb([128, NG]) for _ in range(2)]; b_SG = [Buf(), Buf()]
    VLAST = sb([128, 8, 32], name="VLAST"); b_VLAST = Buf()
    cnt = {"cg": 0, "tmp": 0, "pt": 0, "yo": 0, "h": 0}

    def group(N, ntile, x_src, mod1, mod2, g1t, bg1, g2t, bg2, csq_src, y_dst, attention, sample, preloaded=False, x_next=None, n_next=0):
        NW = N
        for t in range(ntile):
            if not preloaded:
                load(XG[:, t, :], x_src[t * 128:(t + 1) * 128, :], b_XG[t])
            make_uT(XG[:, t, :], [b_XG[t]], 128, UT, b_UT[t], t * 128, mod1)
        load(CSQ[:, :, 0:N], csq_src, b_CSQ)
        wv, wb = wload(w_main_b[:, 0:384], 8, 384, [b_wconv["main"]])
        for c in range(3):
            po, bpo = psum()
            for k in range(8):
                P.pe("matmul", po[:, 0:N], lhsT=wv[:, k, c * 128:(c + 1) * 128], rhs=UT[:, k, 0:N],
                                                        start=(k == 0), stop=(k == 7), reads=wb + b_UT[0:ntile], writes=[bpo])
            P.act("activation", out=QA[:, c, 0:N], in_=po[:, 0:N], func=AF.Copy, reads=[bpo, b_QA], writes=[b_QA])
            P.act("activation", out=SQ[:, c, 0:N], in_=po[:, 0:N], func=AF.Square, reads=[bpo, b_SQ], writes=[b_SQ])
        po, bpo = psum()
        for c in range(3):
            P.pe("matmul", po[:, 0:N], lhsT=ones_b[:], rhs=SQ[:, c, 0:N], start=(c == 0), stop=(c == 2),
                 reads=[b_ones, b_SQ], writes=[bpo])
        P.act("activation", out=RQ[:, 0:N], in_=po[:, 0:N], func=AF.Ln, scale=1.0 / 384, bias=EPS, reads=[bpo], writes=[b_RQ])
        P.act("activation", out=RQ[:, 0:N], in_=RQ[:, 0:N], func=AF.Exp, scale=-0.5, reads=[b_RQ], writes=[b_RQ])
        for c in range(3):
            P.dve("scalar_tensor_tensor", out=QN[:, c, 0:N], in0=QA[:, c, 0:N], scalar=gq_t[:, c:c + 1], in1=RQ[:, 0:N],
                                                        op0=ALU.mult, op1=ALU.mult, reads=[b_QA, b_gq, b_RQ, b_QN], writes=[b_QN])
        for fc in range(8):
            wv, wb = wload(w_main_b[:, 384 + fc * 640:384 + (fc + 1) * 640], 8, 640, [b_wconv["main"]])

            def proj(mc, po, bpo, rhs, ncols, c0=0):
                for k in range(8):
                    P.pe("matmul", po[:, c0:c0 + ncols], lhsT=wv[:, k, mc * 128:(mc + 1) * 128], rhs=rhs(k),
                                                 start=(k == 0), stop=(k == 7), reads=wb + [b_uH] + b_UT[0:ntile], writes=[bpo])

            ci = cnt["cg"] % 2
            cnt["cg"] += 1
            pcg, bpcg = psum()
            proj(1, pcg, bpcg, lambda k: UT[:, k, 0:N], N)
            P.act("activation", out=CG[ci][:, 0:N], in_=pcg[:, 0:N], func=AF.Copy, reads=[bpcg], writes=[b_CG[ci]])
            pxi, bpxi = psum()
            proj(2, pxi, bpxi, lambda k: UT[:, k, 0:N], N)
            vp = VPAD[ci][:, 0:ntile * 130].rearrange("p (t j) -> p t j", j=130)
            if not sample:
                ph, bph = psum()
                nh = 2 * ntile
                proj(1, ph, bph, lambda k: uH[:, k, cnt["g0"] * 2:cnt["g0"] * 2 + nh], nh, 0)
                proj(2, ph, bph, lambda k: uH[:, k, cnt["g0"] * 2:cnt["g0"] * 2 + nh], nh, 16)
                P.act("activation", out=CG[ci][:, NG:NG + nh], in_=ph[:, 0:nh], func=AF.Copy, reads=[bph, b_CG[ci]], writes=[b_CG[ci]])
                P.dve("tensor_tensor", out=vp[:, 0:ntile, 2:130], in0=pxi[:, 0:N].rearrange("p (t j) -> p t j", j=128),
                                                in1=CG[ci][:, 0:N].rearrange("p (t j) -> p t j", j=128), op=ALU.mult,
                      reads=[bpxi, b_CG[ci]], writes=[b_VPAD[ci]])
                P.dve("tensor_tensor", out=vp[:, 0:ntile, 0:2], in0=ph[:, 16:16 + nh].rearrange("p (t j) -> p t j", j=2),
                                                in1=CG[ci][:, NG:NG + nh].rearrange("p (t j) -> p t j", j=2), op=ALU.mult,
                      reads=[bph, b_CG[ci], b_VPAD[ci]], writes=[b_VPAD[ci]])
                g0 = cnt["g0"]
                P.dve("tensor_tensor", out=vp[:, 0:ntile, 0:2], in0=vp[:, 0:ntile, 0:2],
                                                in1=hmask[:, g0:g0 + ntile].unsqueeze(2).to_broadcast([128, ntile, 2]), op=ALU.mult,
                      reads=[b_VPAD[ci], b_hmask], writes=[b_VPAD[ci]])
                vin = [vp[:, 0:ntile, j:j + 128] for j in range(3)]
                zv = ZC[ci][:, 0:N].rearrange("p (t j) -> p t j", j=128)
                if cnt["g0"] + ntile == NOWN:
                    P.dve("tensor_copy", VLAST[:, fc, 0:2], vp[:, ntile - 1, 128:130], reads=[b_VPAD[ci], b_VLAST], writes=[b_VLAST])
            else:
                vps = VPAD[ci][:, 0:160].rearrange("p (s j) -> p s j", j=10)
                P.dve("tensor_tensor", out=vps[:, :, 2:10], in0=pxi[:, 0:128].rearrange("p (s j) -> p s j", j=8),
                                                in1=CG[ci][:, 0:128].rearrange("p (s j) -> p s j", j=8), op=ALU.mult,
                      reads=[bpxi, b_CG[ci]], writes=[b_VPAD[ci]])
                P.dve("tensor_copy", vps[:, :, 0:2], SCT[:, fc, :].rearrange("p (s j) -> p s j", j=2),
                      reads=[b_SCT, b_VPAD[ci]], writes=[b_VPAD[ci]])
                vin = [vps[:, :, j:j + 8] for j in range(3)]
                zv = ZC[ci][:, 0:128].rearrange("p (s j) -> p s j", j=8)
                P.dve("tensor_copy", VLAST[:, fc, :].rearrange("p (s j) -> p s j", j=2), vps[:, :, 8:10],
                      reads=[b_VPAD[ci], b_VLAST], writes=[b_VLAST])
            P.dve("tensor_scalar", zv, vin[0], convw[:, fc * 3:fc * 3 + 1], None, op0=ALU.mult,
                  reads=[b_VPAD[ci], b_convw], writes=[b_ZC[ci]])
            for j in (1, 2):
                P.dve("scalar_tensor_tensor", out=zv, in0=vin[j], scalar=convw[:, fc * 3 + j:fc * 3 + j + 1], in1=zv,
                                                            op0=ALU.mult, op1=ALU.add, reads=[b_VPAD[ci], b_convw, b_ZC[ci]], writes=[b_ZC[ci]])
            pbg, bpbg = psum()
            proj(0, pbg, bpbg, lambda k: UT[:, k, 0:N], N)
            P.dve("tensor_tensor", out=ZC[ci][:, 0:N], in0=pbg[:, 0:N], in1=ZC[ci][:, 0:N], op=ALU.mult,
                  reads=[bpbg, b_ZC[ci]], writes=[b_ZC[ci]])
            pga, bpga = psum()
            proj(3, pga, bpga, lambda k: UT[:, k, 0:N], N)
            P.act("activation", out=GA[:, fc, 0:N], in_=pga[:, 0:N], func=AF.Sigmoid, reads=[bpga], writes=[b_GA[fc]])
            pgb, bpgb = psum()
            proj(4, pgb, bpgb, lambda k: UT[:, k, 0:N], N)
            P.act("activation", out=SG[ci][:, 0:N], in_=pgb[:, 0:N], func=AF.Sigmoid, reads=[bpgb], writes=[b_SG[ci]])
            P.dve("tensor_tensor", out=GB[:, fc, 0:N], in0=SG[ci][:, 0:N], in1=ZC[ci][:, 0:N], op=ALU.mult,
                  reads=[b_SG[ci], b_ZC[ci]], writes=[b_GB[fc]])

        attention(N)

        wv, wb = wload(w_o_b, 8, D, [b_wconv["o"]])
        for t in range(ntile):
            for n in range(2):
                po, bpo = psum()
                for k in range(8):
                    P.pe("matmul", po[:, :], lhsT=MG[:, k, t * 128:(t + 1) * 128], rhs=wv[:, k, n * 512:(n + 1) * 512],
                                                                 start=(k == 0), stop=(k == 7), reads=wb + [b_MG[k]], writes=[bpo])
                i = cnt["tmp"] % 2
                cnt["tmp"] += 1
                P.dve("tensor_tensor", out=TMP[i][:], in0=po[:, :], in1=g1t[:, n * 512:(n + 1) * 512], op=ALU.mult,
                      reads=[bpo, bg1], writes=[b_TMP[i]])
                P.dve("tensor_tensor", out=XG[:, t, n * 512:(n + 1) * 512], in0=XG[:, t, n * 512:(n + 1) * 512],
                                                               in1=TMP[i][:], op=ALU.add, reads=[b_TMP[i], b_XG[t]], writes=[b_XG[t]])
        for t in range(ntile):
            make_uT(XG[:, t, :], [b_XG[t]], 128, UT, b_UT[t], t * 128, mod2)
        for j2 in range(8):
            wv, wb = wload(w_1_b[:, j2 * 512:(j2 + 1) * 512], 8, 512, [b_wconv["1"]])
            for m in range(4):
                po, bpo = psum()
                for k in range(8):
                    P.pe("matmul", po[:, 0:N], lhsT=wv[:, k, m * 128:(m + 1) * 128], rhs=UT[:, k, 0:N],
                                                            start=(k == 0), stop=(k == 7), reads=wb + b_UT[0:ntile], writes=[bpo])
                hc = j2 * 4 + m
                P.act("activation", out=TMP[hc % 2][:, 0:N], in_=po[:, 0:N], func=AF.Relu,
                      reads=[bpo], writes=[b_TMP[hc % 2]])
                P.dve("tensor_tensor", out=HT[:, hc, 0:N], in0=TMP[hc % 2][:, 0:N], in1=TMP[hc % 2][:, 0:N], op=ALU.mult,
                      reads=[b_TMP[hc % 2]], writes=[b_HT[hc]])
        for j in range(4):
            wva, wba = wload(w_2_b[0:2048, j * 256:(j + 1) * 256], 16, 256, [b_wconv["2"]])
            wvb, wbb = wload(w_2_b[2048:4096, j * 256:(j + 1) * 256], 16, 256, [b_wconv["2"]])
            for t in range(ntile):
                po, bpo = psum()
                for k in range(32):
                    wv, wb = (wva, wba) if k < 16 else (wvb, wbb)
                    P.pe("matmul", po[:, 0:256], lhsT=HT[:, k, t * 128:(t + 1) * 128], rhs=wv[:, k % 16, :],
                         start=(k == 0), stop=(k == 31), reads=wb + [b_HT[k]], writes=[bpo])
                i2 = cnt["tmp"] % 2
                cnt["tmp"] += 1
                P.dve("tensor_tensor", out=TMP[i2][:, 0:256], in0=po[:, 0:256], in1=g2t[:, j * 256:(j + 1) * 256], op=ALU.mult,
                      reads=[bpo, bg2], writes=[b_TMP[i2]])
                P.dve("tensor_tensor", out=XG[:, t, j * 256:(j + 1) * 256], in0=XG[:, t, j * 256:(j + 1) * 256],
                                                                 in1=TMP[i2][:, 0:256], op=ALU.add, reads=[b_TMP[i2], b_XG[t]], writes=[b_XG[t]])
        for t in range(ntile):
            rs, rb = rstd_of(XG[:, t, :], [b_XG[t]], 128, D, 1.0 / D)
            i = cnt["yo"] % 2
            cnt["yo"] += 1
            P.dve("scalar_tensor_tensor", out=YO[i][:], in0=XG[:, t, :], scalar=rs, in1=gfin_bc[:], op0=ALU.mult, op1=ALU.mult,
                  reads=[b_XG[t], rb, b_gfin], writes=[b_YO[i]])
            if x_next is not None and t < n_next:
                load(XG[:, t, :], x_next[t * 128:(t + 1) * 128, :], b_XG[t])
            P.dma("dma_start", out=y_dst[t * 128:(t + 1) * 128, :], in_=YO[i][:], reads=[b_YO[i]])

    def head_q(h, N, ktcol_lhsT):
        qi = h % 2
        po, bpo = psum()
        for c in range(3):
            P.pe("matmul", po[:, 0:N], lhsT=wq[:, c, h * 256:h * 256 + 128], rhs=QN[:, c, 0:N], start=(c == 0), stop=(c == 2),
                 reads=[b_wq, b_QN], writes=[bpo])
        P.act("activation", out=QNOPE[qi][:, 0:N], in_=po[:, 0:N], func=AF.Copy, reads=[bpo], writes=[b_QNOPE[qi]])
        pr, bpr = psum()
        for c in range(3):
            P.pe("matmul", pr[0:64, 0:N], lhsT=wq[:, c, h * 256 + 128:h * 256 + 192], rhs=QN[:, c, 0:N], start=(c == 0), stop=(c == 2),
                 reads=[b_wq, b_QN], writes=[bpr])
        pw, bpw = psum()
        for c in range(3):
            P.pe("matmul", pw[0:64, 0:N], lhsT=wq[:, c, h * 256 + 192:h * 256 + 256], rhs=QN[:, c, 0:N], start=(c == 0), stop=(c == 2),
                 reads=[b_wq, b_QN], writes=[bpw])
        P.dve("tensor_tensor", out=RT[0][:, 0:N], in0=pr[0:64, 0:N], in1=CSQ[:, 0, 0:N], op=ALU.mult, reads=[bpr, b_CSQ], writes=[b_RT[0]])
        P.dve("tensor_tensor", out=RT[1][:, 0:N], in0=pw[0:64, 0:N], in1=CSQ[:, 1, 0:N], op=ALU.mult, reads=[bpw, b_CSQ], writes=[b_RT[1]])
        P.dve("tensor_tensor", out=QT[qi][0:64, 2, 0:N], in0=RT[0][:, 0:N], in1=RT[1][:, 0:N], op=ALU.add,
              reads=[b_RT[0], b_RT[1], b_QT[qi]], writes=[b_QT[qi]])
        for m in range(2):
            pl, bpl = psum()
            P.pe("matmul", pl[:, 0:N], lhsT=wuk[:, h, m * 128:(m + 1) * 128], rhs=QNOPE[qi][:, 0:N], start=True, stop=True,
                 reads=[b_wuk, b_QNOPE[qi]], writes=[bpl])
            P.act("activation", out=QT[qi][:, m, 0:N], in_=pl[:, 0:N], func=AF.Copy, reads=[bpl, b_QT[qi]], writes=[b_QT[qi]])
        return qi

    def merge_head(h, N, pa, bpa):
        i = cnt["tmp"] % 2
        cnt["tmp"] += 1
        P.dve("tensor_tensor", out=TMP[i][:, 0:N], in0=pa[:, 0:N], in1=GA[:, h, 0:N], op=ALU.mult, reads=[bpa, b_GA[h]], writes=[b_TMP[i]])
        P.dve("tensor_tensor", out=MG[:, h, 0:N], in0=TMP[i][:, 0:N], in1=GB[:, h, 0:N], op=ALU.add, reads=[b_TMP[i], b_GB[h]], writes=[b_MG[h]])

    SBANK = (0, 1)
    OBANK = (2, 3, 4, 5)
    GBANK = [6, 7]

    def prompt_attention_factory(i0, ntile):
        def attention(N):
            nk = 2 * (i0 + ntile)
            g8 = 2 * i0
            DEFB[:] = [6, 7]
            GBANK[:] = [6, 7]

            def prep(h):
                qi = head_q(h, N, None)
                pm, bpm = psum(GBANK, "gb")
                for c in range(3):
                    kc = 128 if c < 2 else 64
                    P.pe("matmul", pm[0:1, 0:N], lhsT=KT[0:kc, c, 0:1], rhs=QT[qi][0:kc, c, 0:N], start=(c == 0), stop=(c == 2),
                         reads=[b_KT[0], b_QT[qi]], writes=[bpm])
                P.act("activation", out=QT[qi][64:65, 2, 0:N], in_=pm[0:1, 0:N], func=AF.Copy, scale=-1.0, reads=[bpm, b_QT[qi]], writes=[b_QT[qi]])
                return qi

            qis = {0: prep(0)}
            deferred = []
            first_needed = 2 * (i0 + ntile)
            late_tiles = [T for T in range(first_needed, min(NT, first_needed + 2 * ntile)) if T >= NPRE]
            for h in range(8):
                if h + 1 < 8:
                    qis[h + 1] = prep(h + 1)
                qi = qis[h]
                obk = OBANK[0:ntile] if (ntile > 2 or h % 2 == 0) else OBANK[2:2 + ntile]

                def stage_S(kt):
                    j = max(0, (kt - g8) // 2)
                    q0 = j * 128
                    e_ = (kt - g8) % 2 if kt >= g8 else None
                    pS, bpS = psum(SBANK, "sb")
                    for c in range(3):
                        kc = 128 if c < 2 else 65
                        P.pe("matmul", pS[:, q0:N], lhsT=KT[0:kc, c, kt * 128:(kt + 1) * 128], rhs=QT[qi][0:kc, c, q0:N],
                             start=(c == 0), stop=(c == 2), reads=[b_KT[kt], b_KTaug, b_QT[qi]], writes=[bpS])
                    pi = cnt["pt"] % 3
                    cnt["pt"] += 1
                    P.act("activation", out=PT[pi][:, q0:N], in_=pS[:, q0:N], func=AF.Exp, scale=SCALE, reads=[bpS], writes=[b_PT[pi]])
                    if e_ is not None:
                        P.dve("tensor_tensor", out=PT[pi][:, q0:q0 + 128], in0=PT[pi][:, q0:q0 + 128],
                              in1=masks[:, e_ * 128:(e_ + 1) * 128], op=ALU.mult, reads=[b_PT[pi], b_masks], writes=[b_PT[pi]])
                    return (kt, j, pi)

                def stage_V(kt, j, pi):
                    for jq in range(j, ntile):
                        last = g8 + 2 * jq + 1
                        P.pe("matmul", ps[obk[jq]][:, 0:257], lhsT=PT[pi][:, jq * 128:(jq + 1) * 128], rhs=VP[:, kt, 0:257],
                             start=(kt == 0), stop=(kt == last), reads=[b_PT[pi], b_VP[kt]], writes=[bps[obk[jq]]])

                pending = None
                for kt in range(nk):
                    info = stage_S(kt)
                    if pending is not None:
                        stage_V(*pending)
                    pending = info
                    if kt == min(2, nk - 1) and deferred:
                        deferred.pop()()
                stage_V(*pending)
                if late_tiles and h % 2 == 1:
                    kv_tile_late(late_tiles.pop(0))

                def epilogue(h=h, obk=obk):
                    for jq in range(ntile):
                        ob = ps[obk[jq]]; bob = bps[obk[jq]]
                        P.dve("reciprocal", RINV[:, jq:jq + 1], ob[:, 256:257], reads=[bob], writes=[b_RINV[jq]])
                        P.act("activation", out=OL[:, jq, :], in_=ob[:, 0:256], func=AF.Copy, scale=RINV[:, jq:jq + 1],
                              reads=[bob, b_RINV[jq]], writes=[b_OL[jq]])
                    pb, bpb = psum(GBANK, "gb")
                    pbb = pb[:, :].bitcast(BF16)
                    for jq in range(ntile):
                        for m in range(2):
                            P.pe("transpose", pbb[:, m * N + jq * 128:m * N + (jq + 1) * 128], OL[:, jq, m * 128:(m + 1) * 128], ident_b[:],
                                 reads=[b_OL[jq], b_ident_b], writes=[bpb])
                    P.act("activation", out=OLT[:].rearrange("p m q -> p (m q)"), in_=pbb[:, 0:2 * N], func=AF.Copy, reads=[bpb], writes=[b_OLT])
                    pa, bpa = psum(GBANK, "gb")
                    for m in range(2):
                        P.pe("matmul", pa[:, 0:N], lhsT=wuv[:, m, h, :], rhs=OLT[:, m, 0:N], start=(m == 0), stop=(m == 1),
                             reads=[b_wuv, b_OLT], writes=[bpa])
                    merge_head(h, N, pa, bpa)

                if ntile <= 2:
                    deferred.append(epilogue)
                else:
                    epilogue()
            while deferred:
                deferred.pop()()
            DEFB[:] = [0, 1, 2, 3, 4, 5, 6, 7]
        return attention

    if STAGE >= 2:
        for g in range(NOWN // NTG):
            cnt["g0"] = NTG * g
            last_g = (g == NOWN // NTG - 1)
            group(NG, NTG, xo[g * NG:(g + 1) * NG, :], modp, ("p", A2p, B2p, b_modp), g1p, b_g["g1p"], g2p, b_g["g2p"],
                  cs_feat_p.rearrange("p (a t) -> p a t", a=2)[:, :, g * NG:(g + 1) * NG], y_p[g * NG:(g + 1) * NG, :],
                  prompt_attention_factory(NTG * g, NTG), False, preloaded=(g > 0),
                  x_next=(xd if last_g else xo[(g + 1) * NG:(g + 2) * NG, :]) if STAGE >= 3 or not last_g else None,
                  n_next=(1 if last_g else NTG))
        for c8 in range(8):
            P.dma("dma_start", out=conv_p[:, c8 * 128:(c8 + 1) * 128].rearrange("t p -> p t"), in_=VLAST[:, c8, 0:2],
                  allow_slow_non_contiguous=True, reads=[b_VLAST])

    SCT = sb([128, 8, 32], name="SCT"); b_SCT = Buf()
    if STAGE >= 3:
        sct = xn[1][0:32, :]; b_sct = b_xn[1]; load(sct[:], sconv, b_sct)
        po, bpo = psum()
        for k in range(8):
            P.pe("transpose", po[:, k * 32:(k + 1) * 32], sct[0:32, k * 128:(k + 1) * 128], ident_f[0:32, 0:32],
                 reads=[b_sct, b_ident_f], writes=[bpo])
        P.dve("tensor_copy", SCT[:].rearrange("p k s -> p (k s)"), po[:, 0:256], reads=[bpo], writes=[b_SCT])

        KTN = carve([128, 3, 128]); b_KTN = Buf()
        VN = carve([128, 258]); b_VN = Buf()
        kind_f = sb([2, 128]); b_kind = Buf(); load(kind_f[:], kind_d, b_kind)
        augc = sb([2, 256]); b_augc = Buf(); load(augc[:], augc_d, b_augc)
        smask = carve([128, 8, 128]); b_smask = Buf()
        ptt = sb([128, 8], I32); b_ptt = Buf(); load(ptt[:], pt, b_ptt)
        QS = carve([128, 3, 8, 128]); b_QS = Buf()
        QP = [carve([128, 3, 128]) for _ in range(2)]; b_QP = [Buf(), Buf()]
        OTS = carve([128, 2, 8, 128]); b_OTS = Buf()
        R = 8
        KVT = [carve([128, R * 256]) for i in range(3)]; b_KVT = [Buf() for _ in range(3)]
        KRT = [carve([128, 16 * 64]) for i in range(2)]; b_KRT = [Buf() for _ in range(2)]
        KTS = [carve([128, 2, 3, 128]) for i in range(3)]; b_KTS = [Buf() for _ in range(3)]
        PS4 = [carve([128, 4, 128]) for _ in range(2)]; b_PS4 = [Buf(), Buf()]
        PN = carve([128, 128]); b_PN = Buf()
        OLS = carve([128, 256]); b_OLS = Buf()
        alias_barrier([b_KTN, b_VN, b_smask, b_QS, b_OTS, b_PN, b_OLS] + b_QP + b_KVT + b_KRT + b_KTS + b_PS4, b_KT + b_VP + [b_KTaug])
        P.dve("memset", VN[:, 256:258], 1.0, writes=[b_VN])
        P.dve("tensor_copy", KTN[64:66, 2, :], kind_f[:], reads=[b_kind], writes=[b_KTN])
        load(smask.rearrange("p a b -> p (a b)"), smask_d, b_smask, q=POOL)
        for i in range(3):
            P.dve("tensor_copy", KTS[i][64:66, :, 2, 0:64], kind_f[:, 0:1].unsqueeze(1).to_broadcast([2, 2, 64]),
                  reads=[b_kind], writes=[b_KTS[i]])
            P.dve("tensor_copy", KTS[i][64:66, :, 2, 64:128], kind_f[:, 8:9].unsqueeze(1).to_broadcast([2, 2, 64]),
                  reads=[b_kind, b_KTS[i]], writes=[b_KTS[i]])
        RIS = sb([128, 1]); b_RIS = Buf()
        VT = xn[0][0:32, :]; b_VT = b_xn[0]

        mods1 = ("s", A1s, modT[:, 0, :, 1:17], b_mods)
        mods2 = ("s", A2s, modT[:, 2, :, 1:17], b_mods)

        def sample_attention(N):
            DEFB[:] = [4, 5, 6, 7]
            GBANK[:] = [4, 5, 6, 7]
            kv_build(UT, b_UT[0], 0, cs_tok_s, ckv_s, kr_s, VN[:, 0:256], b_VN, KTN[:, :, :], b_KTN)
            for h in range(8):
                qi = head_q(h, N, None)
                P.dve("tensor_copy", QS[:, 0:2, h, :], QT[qi][:, 0:2, 0:128], reads=[b_QT[qi], b_QS], writes=[b_QS])
                P.dve("tensor_copy", QS[0:64, 2, h, :], QT[qi][0:64, 2, 0:128], reads=[b_QT[qi], b_QS], writes=[b_QS])
            def issue_kv(n):
                if n >= 128:
                    return
                jj, rbb = n // 16, n % 16
                if rbb % 2 == 0:
                    gk = n // 2
                    P.dma("indirect_dma_start", out=KRT[gk % 2][:, :], out_offset=None, in_=kr_pool,
                          in_offset=bass.IndirectOffsetOnAxis(ap=ptt[:, jj:jj + 1], axis=0),
                          element_offset=(rbb // 2) * 16 * 64, reads=[b_ptt], writes=[b_KRT[gk % 2]], q=POOL)
                P.dma("indirect_dma_start", out=KVT[n % 3][:, :], out_offset=None, in_=ckv_pool,
                      in_offset=bass.IndirectOffsetOnAxis(ap=ptt[:, jj:jj + 1], axis=0),
                      element_offset=rbb * R * 256, reads=[b_ptt], writes=[b_KVT[n % 3]], q=POOL)

            issue_kv(0)
            for j in range(8):
                qp = QP[j % 2]; bqp = b_QP[j % 2]
                for c in range(3):
                    kc = 128 if c < 2 else 64
                    P.dve("tensor_copy",
                          qp[0:kc, c, :].rearrange("p (s h t) -> p s h t", s=2, h=8),
                          QS[0:kc, c, :, 16 * j:16 * j + 16].rearrange("p h (s t) -> p s h t", t=8), reads=[b_QS, bqp], writes=[bqp])
                pm, bpm = psum(GBANK, "gb")
                for c in range(3):
                    kc = 128 if c < 2 else 64
                    P.pe("matmul", pm[0:2, 0:128], lhsT=KTN[0:kc, c, 16 * j:16 * j + 16:8], rhs=qp[0:kc, c, :],
                         start=(c == 0), stop=(c == 2), reads=[b_KTN, bqp], writes=[bpm])
                P.dve("tensor_tensor", out=augt[:], in0=pm[0:2, 0:128], in1=augc[:, 0:128], op=ALU.mult, reads=[bpm, b_augc], writes=[b_augt])
                P.dve("tensor_tensor", out=qp[64:66, 2, :], in0=augt[:], in1=augc[:, 128:256], op=ALU.add, reads=[b_augt, b_augc, bqp], writes=[bqp])
                ob = ps[OBANK[j % 2]]; bob = bps[OBANK[j % 2]]
                pS, bpS = psum(SBANK, "sb")
                for c in range(3):
                    kc = 128 if c < 2 else 66
                    P.pe("matmul", pS[:, 0:128], lhsT=KTN[0:kc, c, :], rhs=qp[0:kc, c, :], start=(c == 0), stop=(c == 2),
                         reads=[b_KTN, bqp], writes=[bpS])
                P.act("activation", out=PN[:], in_=pS[:, 0:128], func=AF.Exp, scale=SCALE, reads=[bpS], writes=[b_PN])
                P.dve("tensor_tensor", out=PN[:], in0=PN[:], in1=smask[:, j, :], op=ALU.mult, reads=[b_PN, b_smask], writes=[b_PN])
                P.pe("matmul", ob[:, 0:257], lhsT=PN[:], rhs=VN[:, 0:257], start=True, stop=False, reads=[b_PN, b_VN], writes=[bob])

                def stage_T(st):
                    n = 16 * j + st // 4
                    r0 = (st % 4) * 2
                    kvv = KVT[n % 3][:, :].rearrange("p (r f) -> p r f", f=256)
                    krv = KRT[(n // 2) % 2][:, :].rearrange("p (r f) -> p r f", f=64)
                    ti = cnt["h"] % 3
                    cnt["h"] += 1
                    pb, bpb = psum(GBANK, "gb")
                    pbb = pb[:, :].bitcast(BF16)
                    for rr_ in range(2):
                        r = r0 + rr_
                        rk = (n % 2) * 8 + r
                        P.pe("transpose", pbb[:, (rr_ * 3) * 128:(rr_ * 3 + 1) * 128], kvv[:, r, 0:128], ident_b[:],
                             reads=[b_KVT[n % 3], b_ident_b], writes=[bpb])
                        P.pe("transpose", pbb[:, (rr_ * 3 + 1) * 128:(rr_ * 3 + 2) * 128], kvv[:, r, 128:256], ident_b[:],
                             reads=[b_KVT[n % 3], b_ident_b], writes=[bpb])
                        P.pe("transpose", pbb[0:64, (rr_ * 3 + 2) * 128:(rr_ * 3 + 3) * 128], krv[:, rk, :], ident_b[:],
                             reads=[b_KRT[(n // 2) % 2], b_ident_b], writes=[bpb])
                    pv = pbb[:, 0:768].rearrange("p (a c t) -> p a c t", a=2, c=3)
                    P.act("activation", out=KTS[ti][:, :, 0:2, :], in_=pv[:, :, 0:2, :], func=AF.Copy, reads=[bpb, b_KTS[ti]], writes=[b_KTS[ti]])
                    P.dve("tensor_copy", KTS[ti][0:64, :, 2, :], pv[0:64, :, 2, :], reads=[bpb, b_KTS[ti]], writes=[b_KTS[ti]])
                    return ti

                def stage_S(st, ti, pS, bpS):
                    for rr_ in range(2):
                        col = ((st % 2) * 2 + rr_) * 128
                        for c in range(3):
                            kc = 128 if c < 2 else 66
                            P.pe("matmul", pS[:, col:col + 128], lhsT=KTS[ti][0:kc, rr_, c, :], rhs=qp[0:kc, c, :],
                                 start=(c == 0), stop=(c == 2), reads=[b_KTS[ti], bqp], writes=[bpS])

                def stage_V(q, pi):
                    n = 16 * j + q // 2
                    kvv = KVT[n % 3][:, :].rearrange("p (r f) -> p r f", f=256)
                    for r_ in range(4):
                        r = (q % 2) * 4 + r_
                        lastmm = (q == 31 and r_ == 3)
                        P.pe("matmul", ob[:, 0:256], lhsT=PS4[pi][:, r_, :], rhs=kvv[:, r, :], start=False, stop=lastmm, skip_group_check=True,
                             reads=[b_PS4[pi], b_KVT[n % 3]], writes=[bob])
                        P.pe("matmul", ob[:, 256:257], lhsT=PS4[pi][:, r_, :], rhs=ones_b[:, 0:1], start=False, stop=lastmm, skip_group_check=True,
                             reads=[b_PS4[pi], b_ones], writes=[bob])

                tis = {0: stage_T(0)}
                prev = None
                for st in range(64):
                    if st == 1 and j > 0:
                        issue_kv(16 * j + 2)
                    if st == 0 and j == 0:
                        issue_kv(1)
                        issue_kv(2)
                    if st + 1 < 64:
                        tis[st + 1] = stage_T(st + 1)
                    if st % 2 == 0:
                        pS, bpS = psum(SBANK, "sb")
                    stage_S(st, tis[st], pS, bpS)
                    if st % 2 == 1:
                        q = st // 2
                        pi = q % 2
                        P.act("activation", out=PS4[pi][:].rearrange("p a b -> p (a b)"), in_=pS[:, :], func=AF.Exp, scale=SCALE,
                              reads=[bpS], writes=[b_PS4[pi]])
                        if prev is not None:
                            stage_V(*prev)
                            if prev[0] % 2 == 1:
                                issue_kv(16 * j + prev[0] // 2 + 3)
                        prev = (q, pi)
                stage_V(*prev)
                P.dve("reciprocal", RIS[:], ob[:, 256:257], reads=[bob], writes=[b_RIS])
                P.act("activation", out=OLS[:], in_=ob[:, 0:256], func=AF.Copy, scale=RIS[:, 0:1], reads=[bob, b_RIS], writes=[b_OLS])
                pb, bpb = psum(GBANK, "gb")
                pbb = pb[:, :].bitcast(BF16)
                for m in range(2):
                    P.pe("transpose", pbb[:, m * 128:(m + 1) * 128], OLS[:, m * 128:(m + 1) * 128], ident_b[:],
                         reads=[b_OLS, b_ident_b], writes=[bpb])
                for m in range(2):
                    P.dve("tensor_copy", OTS[:, m, :, 16 * j:16 * j + 16].rearrange("p h (s t) -> p s h t", t=8),
                                                                pbb[:, m * 128:(m + 1) * 128].rearrange("p (s h t) -> p s h t", s=2, h=8),
                          reads=[bpb, b_OTS], writes=[b_OTS])
            for h in range(8):
                pa, bpa = psum(GBANK, "gb")
                for m in range(2):
                    P.pe("matmul", pa[:, 0:N], lhsT=wuv[:, m, h, :], rhs=OTS[:, m, h, :], start=(m == 0), stop=(m == 1),
                         reads=[b_wuv, b_OTS], writes=[bpa])
                merge_head(h, N, pa, bpa)
            DEFB[:] = [0, 1, 2, 3, 4, 5, 6, 7]

        augt = sb([2, 128]); b_augt = Buf()
        cnt["g0"] = 0
        group(128, 1, xd, mods1, mods2, g1s, b_g["g1s"], g2s, b_g["g2s"],
              cs_feat_s.rearrange("p (a t) -> p a t", a=2), y_s, sample_attention, True, preloaded=True)
        for hh in range(2):
            po, bpo = psum()
            for kk in range(4):
                k = hh * 4 + kk
                P.pe("transpose", po[0:32, kk * 128:(kk + 1) * 128], VLAST[:, k, :], ident_f[:], reads=[b_VLAST, b_ident_f], writes=[bpo])
            P.dve("tensor_copy", VT[:, hh * 512:(hh + 1) * 512], po[0:32, :], reads=[bpo, b_VT], writes=[b_VT])
        P.dma("dma_start", out=conv_s, in_=VT[:], reads=[b_VT])

    P.emit()
    return nc


_NC_CACHE = {}


def _rope_tables(pos):
    inv = (np.float32(10000.0) ** (-np.arange(0, 64, 2, dtype=np.float32) / np.float32(64))).astype(np.float32)
    ang = pos.astype(np.float32)[:, None] * inv[None, :]
    return np.cos(ang).astype(np.float32), np.sin(ang).astype(np.float32)


def kernel(x_prompt, x_sample, cache_ckv, cache_krope, state_conv, page_table, c_prompt, c_sample,
           w_ada, b_ada, g_attn, w_in, g_q, w_q_b, g_kv, w_kv_b, conv_w, w_o, g_mlp, w_1, w_2, g_final):
    f32 = np.float32
    A = lambda a: np.ascontiguousarray(np.asarray(a))
    x_prompt = np.asarray(x_prompt, f32); x_sample = np.asarray(x_sample, f32)
    if "nc" not in _NC_CACHE:
        _NC_CACHE["nc"] = build_program()
    nc = _NC_CACHE["nc"]

    ident = np.eye(128, dtype=f32)
    tri = (np.arange(128)[:, None] <= np.arange(128)[None, :]).astype(f32)
    cos_p, sin_p = _rope_tables(np.arange(SEQ))
    cs_tok_p = np.concatenate([cos_p, cos_p, -sin_p, sin_p], axis=1)
    pos_s = 8192 + (np.arange(128) % 8)
    cos_s, sin_s = _rope_tables(pos_s)
    cs_tok_s = np.concatenate([cos_s, cos_s, -sin_s, sin_s], axis=1)
    cs_feat_s = np.concatenate([np.concatenate([cos_s, cos_s], 1).T, np.concatenate([-sin_s, sin_s], 1).T], axis=1)
    smask = np.zeros((128, 8, 128), f32)
    kk_s = np.arange(128) // 8; kk_t = np.arange(128) % 8
    qc = np.arange(128); q_sl = qc // 64; q_t = qc % 8
    for j in range(8):
        smask[:, j, :] = ((kk_s[:, None] == (2 * j + q_sl)[None, :]) & (kk_t[:, None] <= q_t[None, :])).astype(f32)
    augc = np.zeros((2, 256), f32)
    augc[0, 0:64] = -1.0; augc[1, 64:128] = -1.0
    augc[0, 128 + 64:256] = -BIG; augc[1, 128:128 + 64] = -BIG
    kind = np.zeros((2, 128), f32)
    kind[0, :] = ((np.arange(128) // 8) % 2 == 0); kind[1, :] = ((np.arange(128) // 8) % 2 == 1)

    w_in0 = np.asarray(w_in[0], f32)
    q_a_w = w_in0[:, 0:384]; ckv_w = w_in0[:, 384:640]; kr_w = w_in0[:, 640:704]
    bg_w = w_in0[:, 704:1728]; cg_w = w_in0[:, 1728:2752]; xin_w = w_in0[:, 2752:3776]
    ga_w = w_in0[:, 3776:4800]; gb_w = w_in0[:, 4800:5824]
    kr_sw = np.concatenate([kr_w[:, 32:64], kr_w[:, 0:32]], axis=1)
    w_kvin = A(np.concatenate([ckv_w, kr_w, kr_sw], axis=1))
    parts = [q_a_w]
    for fc in range(8):
        sl = slice(fc * 128, (fc + 1) * 128)
        parts += [bg_w[:, sl], cg_w[:, sl], xin_w[:, sl], ga_w[:, sl], gb_w[:, sl]]
    w_main = A(np.concatenate(parts, axis=1))
    wqb = np.asarray(w_q_b[0], f32).reshape(384, 8, 192)
    w_q = A(np.concatenate([wqb[:, :, 0:128], wqb[:, :, 128:192], wqb[:, :, 160:192], wqb[:, :, 128:160]], axis=2).reshape(384, 2048))
    wkvb = np.asarray(w_kv_b[0], f32).reshape(256, 8, 256)
    w_ukT = A(wkvb[:, :, 0:128].transpose(2, 1, 0).reshape(128, 2048))
    w_uv = A(wkvb[:, :, 128:256].reshape(256, 1024))
    convT = A(np.asarray(conv_w[0], f32).T.reshape(8, 128, 3).transpose(1, 0, 2).reshape(128, 24))
    fm = lambda v, n: A(np.asarray(v, f32).reshape(n, 128).T)
    badaT = fm(b_ada[0], 48)
    bada_g = A(np.stack([np.asarray(b_ada[0], f32)[2048:3072], np.asarray(b_ada[0], f32)[5120:6144]]))
    shared = {
        "w_ada": A(np.asarray(w_ada[0], f32)), "badaT": badaT, "bada_g": bada_g,
        "gattnT": fm(g_attn[0], 8), "gmlpT": fm(g_mlp[0], 8), "gqT": fm(g_q[0], 3),
        "gkv": A(np.asarray(g_kv[0], f32)), "gfin": A(np.asarray(g_final, f32)),
        "w_kvin": w_kvin, "w_main": w_main, "w_q": w_q, "w_ukT": w_ukT, "w_uv": w_uv, "convT": convT,
        "w_o": A(np.asarray(w_o[0], f32)), "w_1": A(np.asarray(w_1[0], f32)), "w_2": A(np.asarray(w_2[0], f32)),
        "ident": ident, "cs_tok_p": A(cs_tok_p), "cs_tok_s": A(cs_tok_s), "cs_feat_s": A(cs_feat_s),
        "smask": A(smask.reshape(128, 1024)), "augc": augc, "kind": kind,
        "ckv_pool": np.asarray(cache_ckv[0], f32).reshape(NPOOL, 128 * 256),
        "kr_pool": np.asarray(cache_krope[0], f32).reshape(NPOOL, 128 * 64),
    }
    page_table = np.asarray(page_table, np.int32)
    in_maps = []
    own_rows = []
    for c in range(8):
        b, half = c // 2, c % 2
        tiles = [2 * i + half for i in range(NOWN)]
        rows = np.concatenate([np.arange(t * 128, (t + 1) * 128) for t in tiles])
        own_rows.append(rows)
        xb = x_prompt[b]
        xh = np.zeros((32, D), f32)
        hm = np.ones((128, 16), f32)
        for i, t in enumerate(tiles):
            if t == 0:
                hm[:, i] = 0.0
            else:
                xh[2 * i:2 * i + 2] = xb[t * 128 - 2:t * 128]
        masks = np.concatenate([tri, np.zeros((128, 128), f32)], 1) if half == 0 else np.concatenate([np.ones((128, 128), f32), tri], 1)
        cosq = np.concatenate([cos_p[rows], cos_p[rows]], 1).T
        sinq = np.concatenate([-sin_p[rows], sin_p[rows]], 1).T
        cs_feat_p = np.concatenate([cosq, sinq], axis=1)
        seqs = np.arange(16 * c, 16 * c + 16)
        ptc = np.zeros((128, 8), np.int32)
        for j in range(8):
            ptc[0:64, j] = page_table[seqs[2 * j]]
            ptc[64:128, j] = page_table[seqs[2 * j + 1]]
        m = dict(shared)
        m.update({
            "xs": A(xb), "xo": A(xb[rows]), "xh": xh, "xd": A(x_sample[seqs].reshape(128, D)),
            "cc": A(np.concatenate([np.asarray(c_prompt, f32)[b:b + 1], np.asarray(c_sample, f32)[seqs]], 0)),
            "pt": ptc, "sconv": A(np.asarray(state_conv[0], f32)[seqs].reshape(32, D)),
            "masks": A(masks), "hmask": hm, "cs_feat_p": A(cs_feat_p),
        })
        in_maps.append(m)

    res = run_bass_kernel_spmd(nc, in_maps, core_ids=list(range(8)))
    R = res.results
    y_prompt = np.zeros((4, SEQ, D), f32)
    y_sample = np.zeros((128, 8, D), f32)
    ckv_pr = np.zeros((1, 4, SEQ, 256), f32); kr_pr = np.zeros((1, 4, SEQ, 64), f32); conv_pr = np.zeros((1, 4, 2, D), f32)
    ckv_sm = np.zeros((1, 128, 8, 256), f32); kr_sm = np.zeros((1, 128, 8, 64), f32); conv_sm = np.zeros((1, 128, 2, D), f32)
    for c in range(8):
        b, half = c // 2, c % 2
        r = R[c]
        y_prompt[b, own_rows[c]] = r["y_p"]
        y_sample[16 * c:16 * c + 16] = r["y_s"].reshape(16, 8, D)
        if half == 0:
            ckv_pr[0, b] = r["ckv_p"]; kr_pr[0, b] = r["kr_p"]
        else:
            conv_pr[0, b] = r["conv_p"]
        ckv_sm[0, 16 * c:16 * c + 16] = r["ckv_s"].reshape(16, 8, 256)
        kr_sm[0, 16 * c:16 * c + 16] = r["kr_s"].reshape(16, 8, 64)
        conv_sm[0, 16 * c:16 * c + 16] = r["conv_s"].reshape(16, 2, D)
    return (y_prompt, y_sample, ckv_pr, kr_pr, conv_pr, ckv_sm, kr_sm, conv_sm)
```

```python
import contextlib
import numpy as np
import concourse.bass as bass
import concourse.mybir as mybir
from concourse.bass_utils import run_bass_kernel_spmd

F32 = mybir.dt.float32
BF16 = mybir.dt.bfloat16
I32 = mybir.dt.int32
AF = mybir.ActivationFunctionType
ALU = mybir.AluOpType

PE, ACT, DVE, POOL, SP = "pe", "act", "dve", "pool", "sp"
COMPUTE = (PE, ACT, DVE, POOL)
NQ = 8

D = 1024
SEQ = 4096
NT = 32
NOWN = 16
NPOOL = 10240
EPS = 1e-6
SCALE = float((128 + 64) ** -0.5)
BIG = 30000.0
STAGE = 3
NTG = 2
NG = 128 * NTG


class Buf:
    __slots__ = ("name", "writers", "readers")

    def __init__(self, name=""):
        self.name = name
        self.writers = {}
        self.readers = {}


class Op:
    __slots__ = ("eng", "fn", "deps", "idx", "is_dma", "signal", "sem", "val", "qslot")

    def __init__(self, eng, fn, is_dma):
        self.eng = eng
        self.fn = fn
        self.is_dma = is_dma
        self.deps = []
        self.signal = False
        self.sem = None
        self.val = None
        self.qslot = 0


class Prog:
    def __init__(self, nc):
        self.nc = nc
        self.ops = {e: [] for e in (PE, ACT, DVE, POOL, SP)}
        self.ndma = {e: 0 for e in (PE, ACT, DVE, POOL, SP)}

    def _key(self, op):
        return ("dma", id(op)) if op.is_dma else op.eng

    def op(self, eng, meth, args, kwargs, reads=(), writes=(), dma=False):
        fn = (lambda e: getattr(e, meth)(*args, **kwargs))
        o = Op(eng, fn, dma)
        o.idx = len(self.ops[eng])
        deps = []
        for b in reads:
            for w in b.writers.values():
                deps.append(w)
        for b in writes:
            for w in b.writers.values():
                if w.is_dma or dma or w.eng != eng:
                    deps.append(w)
            for r in b.readers.values():
                if r.is_dma or dma or r.eng != eng:
                    deps.append(r)
        o.deps = [d for d in deps if not (d.eng == PE and eng == PE and not d.is_dma and not dma)]
        for b in reads:
            b.readers[self._key(o)] = o
        for b in writes:
            b.writers = {self._key(o): o}
            b.readers = {}
        if dma:
            o.qslot = self.ndma[eng]
            self.ndma[eng] += 1
        self.ops[eng].append(o)
        return o

    def pe(self, meth, *args, reads=(), writes=(), **kw):
        return self.op(PE, meth, args, kw, reads, writes)

    def act(self, meth, *args, reads=(), writes=(), **kw):
        return self.op(ACT, meth, args, kw, reads, writes)

    def dve(self, meth, *args, reads=(), writes=(), **kw):
        return self.op(DVE, meth, args, kw, reads, writes)

    def pool(self, meth, *args, reads=(), writes=(), **kw):
        return self.op(POOL, meth, args, kw, reads, writes)

    def dma(self, meth, *args, reads=(), writes=(), q=SP, **kw):
        return self.op(q, meth, args, kw, reads, writes, dma=True)

    def emit(self):
        nc = self.nc
        for e in self.ops:
            for o in self.ops[e]:
                for d in o.deps:
                    d.signal = True
        with contextlib.ExitStack() as st:
            csem = {e: st.enter_context(nc.semaphore("s_" + e)) for e in COMPUTE}
            qsem = {}
            for e in self.ops:
                if self.ndma[e]:
                    qsem[e] = [st.enter_context(nc.semaphore("q_%s_%d" % (e, i))) for i in range(NQ)]
            for e in self.ops:
                c = 0
                for o in self.ops[e]:
                    if o.is_dma:
                        o.sem = qsem[e][o.qslot % NQ]
                        o.val = 16 * (o.qslot // NQ + 1)
                    elif o.signal:
                        c += 1
                        o.sem, o.val = csem[e], c
            block = st.enter_context(nc.Block())
            handles = {PE: block.tensor, ACT: block.scalar, DVE: block.vector, POOL: block.gpsimd, SP: block.sync}

            def make(e):
                def body(eng):
                    known = {}
                    for o in self.ops[e]:
                        waits = {}
                        for d in o.deps:
                            kk = id(d.sem)
                            if kk not in waits or waits[kk][1] < d.val:
                                waits[kk] = (d.sem, d.val)
                        if o.is_dma and o.qslot >= NQ:
                            s = qsem[e][o.qslot % NQ]
                            v = 16 * (o.qslot // NQ)
                            kk = id(s)
                            if kk not in waits or waits[kk][1] < v:
                                waits[kk] = (s, v)
                        for kk, (s, v) in waits.items():
                            if known.get(kk, 0) >= v:
                                continue
                            eng.wait_ge(s, v)
                            known[kk] = v
                        inst = o.fn(eng)
                        if o.is_dma:
                            inst.then_inc(o.sem, 16)
                        elif o.signal:
                            inst.then_inc(o.sem, 1)
                    if e == SP:
                        for qe in qsem:
                            n = self.ndma[qe]
                            for slot in range(min(NQ, n)):
                                last = ((n - 1 - slot) // NQ) + 1
                                eng.wait_ge(qsem[qe][slot], 16 * last)
                return body

            for e in (SP, POOL, ACT, DVE, PE):
                if self.ops[e] or e == SP:
                    handles[e](make(e))


def build_program():
    nc = bass.Bass("TRN2", target_bir_lowering=False)
    P = Prog(nc)

    def din(name, shape, dt=F32):
        return nc.dram_tensor(name, list(shape), dt, kind="ExternalInput").ap()

    def dout(name, shape):
        return nc.dram_tensor(name, list(shape), F32, kind="ExternalOutput").ap()

    _n = [0]

    def sb(shape, dt=F32, name=None):
        _n[0] += 1
        return nc.alloc_sbuf_tensor(name or ("t%d" % _n[0]), list(shape), dt)

    xs = din("xs", [SEQ, D])
    xo = din("xo", [NOWN * 128, D])
    xh = din("xh", [32, D])
    xd = din("xd", [128, D])
    cc = din("cc", [17, D])
    pt = din("pt", [128, 8], I32)
    ckv_pool = din("ckv_pool", [NPOOL, 128 * 256])
    kr_pool = din("kr_pool", [NPOOL, 128 * 64])
    sconv = din("sconv", [32, D])
    w_ada = din("w_ada", [D, 6 * D])
    badaT = din("badaT", [128, 48])
    bada_g = din("bada_g", [2, D])
    gattnT = din("gattnT", [128, 8])
    gmlpT = din("gmlpT", [128, 8])
    gqT = din("gqT", [128, 3])
    gkv = din("gkv", [256])
    gfin = din("gfin", [D])
    w_kvin = din("w_kvin", [D, 384])
    w_main = din("w_main", [D, 5504])
    w_q = din("w_q", [384, 2048])
    w_ukT = din("w_ukT", [128, 8 * 256])
    w_uv = din("w_uv", [256, 8 * 128])
    convT = din("convT", [128, 8 * 3])
    w_o = din("w_o", [D, D])
    w_1 = din("w_1", [D, 4 * D])
    w_2 = din("w_2", [4 * D, D])
    ident_d = din("ident", [128, 128])
    masks_d = din("masks", [128, 2 * 128])
    hmask_d = din("hmask", [128, 16])
    cs_tok_p = din("cs_tok_p", [SEQ, 128])
    cs_tok_s = din("cs_tok_s", [128, 128])
    cs_feat_p = din("cs_feat_p", [64, 2 * NOWN * 128])
    cs_feat_s = din("cs_feat_s", [64, 2 * 128])
    smask_d = din("smask", [128, 8 * 128])
    augc_d = din("augc", [2, 2 * 128])
    kind_d = din("kind", [2, 128])

    y_p = dout("y_p", [NOWN * 128, D])
    y_s = dout("y_s", [128, D])
    ckv_p = dout("ckv_p", [SEQ, 256])
    kr_p = dout("kr_p", [SEQ, 64])
    conv_p = dout("conv_p", [2, D])
    ckv_s = dout("ckv_s", [128, 256])
    kr_s = dout("kr_s", [128, 64])
    conv_s = dout("conv_s", [32, D])

    ps = [nc.alloc_psum_tensor("ps%d" % i, [128, 512], F32) for i in range(8)]
    bps = [Buf("ps%d" % i) for i in range(8)]
    rr = {"g": 0}

    DEFB = [0, 1, 2, 3, 4, 5, 6, 7]

    def psum(group=DEFB, key="g"):
        i = group[rr.get(key, 0) % len(group)]
        rr[key] = rr.get(key, 0) + 1
        return ps[i], bps[i]

    def load(dst_ap, src_ap, buf, q=SP, reads=()):
        P.dma("dma_start", out=dst_ap, in_=src_ap, reads=list(reads), writes=[buf], q=q)

    ident_f = sb([128, 128]); b_ident_f = Buf()
    ident_b = sb([128, 128], BF16); b_ident_b = Buf()
    ones_b = sb([128, 128], BF16); b_ones = Buf()
    load(ident_f[:], ident_d, b_ident_f)
    P.dve("tensor_copy", ident_b[:], ident_f[:], reads=[b_ident_f], writes=[b_ident_b])
    P.dve("memset", ones_b[:], 1.0, writes=[b_ones])
    masks_f = sb([128, 256]); b_masks_f = Buf()
    masks = sb([128, 256], BF16); b_masks = Buf()
    load(masks_f[:], masks_d, b_masks_f)
    P.dve("tensor_copy", masks[:], masks_f[:], reads=[b_masks_f], writes=[b_masks])
    hmask = sb([128, 16]); b_hmask = Buf()
    load(hmask[:], hmask_d, b_hmask)
    gattn_t = sb([128, 8]); b_gattn = Buf(); load(gattn_t[:], gattnT, b_gattn)
    gmlp_t = sb([128, 8]); b_gmlp = Buf(); load(gmlp_t[:], gmlpT, b_gmlp)
    gq_t = sb([128, 3]); b_gq = Buf(); load(gq_t[:], gqT, b_gq)
    bada_t = sb([128, 48]); b_bada = Buf(); load(bada_t[:], badaT, b_bada)
    convw = sb([128, 24]); b_convw = Buf(); load(convw[:], convT, b_convw)
    gkv_bc = sb([128, 256]); b_gkv = Buf(); load(gkv_bc[:], gkv.partition_broadcast(128), b_gkv)
    gfin_bc = sb([128, D]); b_gfin = Buf(); load(gfin_bc[:], gfin.partition_broadcast(128), b_gfin)
    xn = [sb([128, D]) for _ in range(2)]; b_xn = [Buf(), Buf()]
    YO = xn; b_YO = b_xn
    XG = sb([128, NTG, D], name="XG"); b_XG = [Buf("XG%d" % t) for t in range(NTG)]
    UT = sb([128, 8, NG], BF16, name="UT"); b_UT = [Buf("UT%d" % t) for t in range(NTG)]
    REG = sb([128, 32, NG], BF16, name="REG"); b_REG = [Buf() for _ in range(32)]
    BAD = [XG[:, 0, :], XG[:, 1, :]]
    b_badag, b_badag1 = b_XG[0], b_XG[1]
    load(BAD[0], bada_g[0].partition_broadcast(128), b_badag)
    load(BAD[1], bada_g[1].partition_broadcast(128), b_badag1)

    wkv = sb([128, 8, 384], BF16); b_wkv = Buf()
    load(wkv[:], w_kvin.rearrange("(k p) c -> p k c", p=128), b_wkv, q=POOL)
    wq = sb([128, 3, 2048], BF16); b_wq = Buf()
    load(wq[:], w_q.rearrange("(k p) c -> p k c", p=128), b_wq, q=POOL)
    wuk = sb([128, 8, 256], BF16); b_wuk = Buf()
    load(wuk[:].rearrange("p h c -> p (h c)"), w_ukT, b_wuk, q=POOL)
    wuv = sb([128, 2, 8, 128], BF16); b_wuv = Buf()
    load(wuv[:].rearrange("p m h v -> p m (h v)"), w_uv.rearrange("(m p) c -> p m c", p=128), b_wuv, q=POOL)

    WUNIT = 1024
    NUNIT = 18
    wring = sb([128, NUNIT * WUNIT], BF16, name="wring")
    bunits = [Buf("wu%d" % i) for i in range(NUNIT)]
    wstate = {"pos": 0}

    def wload(src2d, kch, cols, reads=()):
        n = kch * cols
        nu = (n + WUNIT - 1) // WUNIT
        if wstate["pos"] + nu > NUNIT:
            wstate["pos"] = 0
        u0 = wstate["pos"]
        wstate["pos"] += nu
        bufs = bunits[u0:u0 + nu]
        view = wring[:, u0 * WUNIT:u0 * WUNIT + n].rearrange("p (k c) -> p k c", c=cols)
        P.dma("dma_start", out=view, in_=src2d.rearrange("(k p) c -> p k c", p=128),
              reads=list(reads), writes=bufs, q=POOL)
        return view, bufs

    w_main_b = nc.dram_tensor("w_main_b", [D, 5504], BF16, kind="Internal").ap()
    w_o_b = nc.dram_tensor("w_o_b", [D, D], BF16, kind="Internal").ap()
    w_1_b = nc.dram_tensor("w_1_b", [D, 4 * D], BF16, kind="Internal").ap()
    w_2_b = nc.dram_tensor("w_2_b", [4 * D, D], BF16, kind="Internal").ap()
    b_wconv = {"main": Buf(), "o": Buf(), "1": Buf(), "2": Buf()}

    def convert_weights():
        for (src, dst, key) in ((w_main, w_main_b, "main"), (w_o, w_o_b, "o"), (w_1, w_1_b, "1"), (w_2, w_2_b, "2")):
            rows, cols = src.shape
            for r0 in range(0, rows, 128):
                for c0 in range(0, cols, 2048):
                    c1 = min(cols, c0 + 2048)
                    P.dma("dma_start", out=dst[r0:r0 + 128, c0:c1], in_=src[r0:r0 + 128, c0:c1],
                          reads=[b_wconv[key]], writes=[b_wconv[key]], q=POOL)
                    yield None

    conv_gen = convert_weights()

    def convert_some(n):
        for _ in range(n):
            try:
                next(conv_gen)
            except StopIteration:
                return

    RA = sb([128, 3 * SEQ + NT * 258 + 8], BF16, name="RA")
    KT = RA[:, 0:3 * SEQ].rearrange("p (c t) -> p c t", c=3); b_KT = [Buf("KT%d" % t) for t in range(NT)]
    VP = RA[:, 3 * SEQ:3 * SEQ + NT * 258].rearrange("p (t f) -> p t f", f=258); b_VP = [Buf("VP%d" % t) for t in range(NT)]
    ra_off = [0]

    def carve(shape):
        n = 1
        for d_ in shape[1:]:
            n *= d_
        v = RA[:, ra_off[0]:ra_off[0] + n]
        ra_off[0] += n + (n % 2)
        assert ra_off[0] <= 3 * SEQ + NT * 258
        if len(shape) == 2:
            return v
        if len(shape) == 3:
            return v.rearrange("p (a b) -> p a b", b=shape[2])
        return v.rearrange("p (a b c) -> p a b c", b=shape[2], c=shape[3])

    def alias_barrier(new_bufs, old_bufs):
        merged = {}
        for ob in old_bufs:
            for dct in (ob.readers, ob.writers):
                for k, o in dct.items():
                    if k not in merged or merged[k].idx < o.idx:
                        merged[k] = o
        for nb in new_bufs:
            nb.readers.update(merged)
    b_KTaug = Buf()
    P.dve("memset", KT[64:65, 2, :], 1.0, writes=[b_KTaug])
    P.dve("memset", VP[:, :, 256:258], 1.0, writes=b_VP)

    modT = sb([128, 4, 8, 17]); b_modT = Buf()
    A1p = sb([128, 8]); B1p = sb([128, 8]); A2p = sb([128, 8]); B2p = sb([128, 8]); b_modp = Buf()
    A1s = sb([128, 8, 16]); A2s = sb([128, 8, 16]); b_mods = Buf()
    g1p = sb([128, D]); g2p = sb([128, D]); g1s = sb([128, D]); g2s = sb([128, D])
    b_g = {"g1p": Buf(), "g2p": Buf(), "g1s": Buf(), "g2s": Buf()}

    cct = xn[0][0:17, :]; b_cc = b_xn[0]; load(cct[:], cc, b_cc)
    sil = xn[1][0:17, :]; b_sil = b_xn[1]
    P.act("activation", out=sil[:], in_=cct[:], func=AF.Silu, reads=[b_cc], writes=[b_sil])
    silT = sb([128, 8, 17], BF16); b_silT = Buf()
    pt_, bp_ = psum()
    for k in range(8):
        P.pe("transpose", pt_[:, k * 17:(k + 1) * 17], sil[0:17, k * 128:(k + 1) * 128], ident_f[0:17, 0:17],
             reads=[b_sil, b_ident_f], writes=[bp_])
    P.dve("tensor_copy", silT[:].rearrange("p k s -> p (k s)"), pt_[:, 0:8 * 17], reads=[bp_], writes=[b_silT])
    nreg = 1024 // NG
    lhs_p = REG[:, 16:16 + nreg, :].rearrange("p a b -> p (a b)").rearrange("p (k c) -> p k c", c=128)
    lhs_s = REG[:, 16 + nreg:16 + 2 * nreg, :].rearrange("p a b -> p (a b)").rearrange("p (k c) -> p k c", c=128)
    b_lhs = Buf()
    b_lhs_all = [b_lhs] + b_REG[16:16 + 2 * nreg]
    P.dve("tensor_copy", lhs_p[:], silT[:, :, 0:1].to_broadcast([128, 8, 128]), reads=[b_silT], writes=b_lhs_all)
    for k in range(8):
        P.dve("tensor_copy", lhs_s[:, k, :].rearrange("p (s t) -> p s t", t=8),
                                           silT[:, k, 1:17].unsqueeze(2).to_broadcast([128, 16, 8]),
              reads=[b_silT, b_lhs], writes=b_lhs_all)

    for j in range(6):
        wv, wb = wload(w_ada[:, j * D:(j + 1) * D], 8, D)
        if j in (0, 1, 3, 4):
            jj = (0, 1, None, 2, 3)[j]
            po, bpo = psum()
            for m in range(8):
                for k in range(8):
                    P.pe("matmul", po[:, m * 17:(m + 1) * 17], lhsT=wv[:, k, m * 128:(m + 1) * 128],
                                                      rhs=silT[:, k, :], start=(k == 0), stop=(k == 7),
                         reads=wb + [b_silT], writes=[bpo])
            for m in range(8):
                P.dve("tensor_scalar", modT[:, jj, m, :], po[:, m * 17:(m + 1) * 17],
                                                                 bada_t[:, j * 8 + m:j * 8 + m + 1], None, op0=ALU.add,
                      reads=[bpo, b_bada, b_modT], writes=[b_modT])
        else:
            gi = 0 if j == 2 else 1
            for (lh, gt, bn) in ((lhs_p, g1p if gi == 0 else g2p, "g%dp" % (gi + 1)),
                                 (lhs_s, g1s if gi == 0 else g2s, "g%ds" % (gi + 1))):
                for n in range(2):
                    po, bpo = psum()
                    for k in range(8):
                        P.pe("matmul", po[:, :], lhsT=lh[:, k, :], rhs=wv[:, k, n * 512:(n + 1) * 512],
                                                                      start=(k == 0), stop=(k == 7),
                             reads=wb + b_lhs_all, writes=[bpo])
                    P.dve("tensor_tensor", out=gt[:, n * 512:(n + 1) * 512], in0=po[:, :],
                                                                              in1=BAD[gi][:, n * 512:(n + 1) * 512], op=ALU.add,
                          reads=[bpo, b_badag, b_badag1, b_g[bn]], writes=[b_g[bn]])
    convert_some(8)
    P.dve("scalar_tensor_tensor", out=A1p[:], in0=modT[:, 1, :, 0], scalar=1.0, in1=gattn_t[:], op0=ALU.add, op1=ALU.mult,
          reads=[b_modT, b_gattn], writes=[b_modp])
    P.dve("scalar_tensor_tensor", out=A2p[:], in0=modT[:, 3, :, 0], scalar=1.0, in1=gmlp_t[:], op0=ALU.add, op1=ALU.mult,
          reads=[b_modT, b_gmlp, b_modp], writes=[b_modp])
    P.dve("tensor_copy", B1p[:], modT[:, 0, :, 0], reads=[b_modT, b_modp], writes=[b_modp])
    P.dve("tensor_copy", B2p[:], modT[:, 2, :, 0], reads=[b_modT, b_modp], writes=[b_modp])
    P.dve("scalar_tensor_tensor", out=A1s[:], in0=modT[:, 1, :, 1:17], scalar=1.0,
                                           in1=gattn_t[:].unsqueeze(2).to_broadcast([128, 8, 16]), op0=ALU.add, op1=ALU.mult,
          reads=[b_modT, b_gattn], writes=[b_mods])
    P.dve("scalar_tensor_tensor", out=A2s[:], in0=modT[:, 3, :, 1:17], scalar=1.0,
                                           in1=gmlp_t[:].unsqueeze(2).to_broadcast([128, 8, 16]), op0=ALU.add, op1=ALU.mult,
          reads=[b_modT, b_gmlp, b_mods], writes=[b_mods])

    nj = D // NG
    JUNK = {"g": (REG[:, 0:nj, :].rearrange("p a b -> p (a b)"), b_REG[0:nj]),
            "k": (REG[:, 24:24 + nj, :].rearrange("p a b -> p (a b)"), b_REG[24:24 + nj])}
    TMP = [sb([128, 512]) for _ in range(2)]; b_TMP = [Buf(), Buf()]
    st_ss = sb([128, 8]); st_rs = sb([128, 8]); b_st = [Buf() for _ in range(8)]
    stc = {"n": 0}

    def rstd_of(src_ap, src_bufs, npart, ncols, inv_n, jsel="g"):
        i = stc["n"] % 8
        stc["n"] += 1
        junk, jb = JUNK[jsel]
        P.act("activation", out=junk[0:npart, 0:ncols], in_=src_ap, func=AF.Square, accum_out=st_ss[0:npart, i:i + 1],
              reads=list(src_bufs), writes=list(jb) + [b_st[i]])
        P.act("activation", out=st_rs[0:npart, i:i + 1], in_=st_ss[0:npart, i:i + 1], func=AF.Ln, scale=inv_n, bias=EPS,
              reads=[b_st[i]], writes=[b_st[i]])
        P.act("activation", out=st_rs[0:npart, i:i + 1], in_=st_rs[0:npart, i:i + 1], func=AF.Exp, scale=-0.5,
              reads=[b_st[i]], writes=[b_st[i]])
        return st_rs[0:npart, i:i + 1], b_st[i]

    xnc = {"n": 0}

    def make_uT(x_ap, x_bufs, npart, uT, uT_buf, col0, mod, jsel="g"):
        rs, rb = rstd_of(x_ap, x_bufs, npart, D, 1.0 / D, jsel)
        i = xnc["n"] % 2
        xnc["n"] += 1
        xnb = xn[i][:, :].bitcast(BF16)
        P.dve("tensor_scalar", xnb[0:npart, 0:D], x_ap, rs, None, op0=ALU.mult,
              reads=list(x_bufs) + [rb], writes=[b_xn[i]])
        for half in range(2):
            po_, bpo = psum()
            po = po_[:, :].bitcast(BF16)
            for kk in range(4):
                k = half * 4 + kk
                P.pe("transpose", po[:, kk * 128:kk * 128 + npart], xnb[0:npart, k * 128:(k + 1) * 128],
                                                              ident_b[0:npart, 0:npart],
                     reads=[b_xn[i], b_ident_b], writes=[bpo])
            for kk in range(4):
                k = half * 4 + kk
                if mod[0] == "p":
                    P.dve("tensor_scalar", uT[:, k, col0:col0 + npart], po[:, kk * 128:kk * 128 + npart],
                                                                       mod[1][:, k:k + 1], mod[2][:, k:k + 1], op0=ALU.mult, op1=ALU.add,
                          reads=[bpo, mod[3]], writes=[uT_buf])
                else:
                    tmp = sb_tmp_s
                    P.dve("tensor_tensor", out=tmp[:].rearrange("p (s t) -> p s t", t=8),
                                                                       in0=po[:, kk * 128:(kk + 1) * 128].rearrange("p (s t) -> p s t", t=8),
                                                                       in1=mod[1][:, k, :].unsqueeze(2).to_broadcast([128, 16, 8]), op=ALU.mult,
                          reads=[bpo, mod[3]], writes=[b_tmp_s])
                    P.dve("tensor_tensor", out=uT[:, k, col0:col0 + 128].rearrange("p (s t) -> p s t", t=8),
                                                         in0=tmp[:].rearrange("p (s t) -> p s t", t=8),
                                                         in1=mod[2][:, k, :].unsqueeze(2).to_broadcast([128, 16, 8]), op=ALU.add,
                          reads=[b_tmp_s, mod[3]], writes=[uT_buf])

    sb_tmp_s = TMP[0][:, 0:128]; b_tmp_s = b_TMP[0]

    kvf = [sb([128, 320]) for _ in range(2)]; b_kvf = [Buf(), Buf()]
    krb = [sb([128, 64], BF16) for _ in range(2)]; b_krb = [Buf(), Buf()]
    cst = [sb([128, 128]) for _ in range(2)]; b_cst = [Buf(), Buf()]
    krt = [sb([128, 128]) for _ in range(2)]; b_krt = [Buf(), Buf()]
    kvc = {"n": 0}

    def kv_build(uT, uT_buf, col0, cs_src, out_ckv, out_kr, vdst, vbuf, ktdst, ktbuf, split=False):
        i = kvc["n"] % 2
        kvc["n"] += 1
        load(cst[i][:], cs_src, b_cst[i])
        po, bpo = psum()
        for k in range(8):
            P.pe("matmul", po[:, 0:384], lhsT=uT[:, k, col0:col0 + 128], rhs=wkv[:, k, :], start=(k == 0), stop=(k == 7),
                 reads=[uT_buf, b_wkv], writes=[bpo])
        rs, rb = rstd_of(po[:, 0:256], [bpo], 128, 256, 1.0 / 256, "k")
        P.dve("scalar_tensor_tensor", out=kvf[i][:, 0:256], in0=po[:, 0:256], scalar=rs, in1=gkv_bc[:], op0=ALU.mult, op1=ALU.mult,
              reads=[bpo, rb, b_gkv], writes=[b_kvf[i]])
        P.dve("tensor_tensor", out=krt[i][:], in0=po[:, 256:384], in1=cst[i][:], op=ALU.mult,
              reads=[bpo, b_cst[i]], writes=[b_krt[i]])
        P.dve("tensor_tensor", out=kvf[i][:, 256:320], in0=krt[i][:, 0:64], in1=krt[i][:, 64:128], op=ALU.add,
              reads=[b_krt[i], b_kvf[i]], writes=[b_kvf[i]])
        P.dma("dma_start", out=out_ckv, in_=kvf[i][:, 0:256], reads=[b_kvf[i]])
        P.dma("dma_start", out=out_kr, in_=kvf[i][:, 256:320], reads=[b_kvf[i]])
        if split:
            return lambda: kv_build2(i, vdst, vbuf, ktdst, ktbuf)
        kv_build2(i, vdst, vbuf, ktdst, ktbuf)

    def kv_build2(i, vdst, vbuf, ktdst, ktbuf):
        P.act("activation", out=vdst, in_=kvf[i][:, 0:256], func=AF.Copy, reads=[b_kvf[i]], writes=[vbuf])
        P.act("activation", out=krb[i][:], in_=kvf[i][:, 256:320], func=AF.Copy, reads=[b_kvf[i]], writes=[b_krb[i]])
        pb, bpb = psum()
        pbb = pb[:, :].bitcast(BF16)
        P.pe("transpose", pbb[:, 0:128], vdst[:, 0:128], ident_b[:], reads=[vbuf, b_ident_b], writes=[bpb])
        P.pe("transpose", pbb[:, 128:256], vdst[:, 128:256], ident_b[:], reads=[vbuf, b_ident_b], writes=[bpb])
        P.pe("transpose", pbb[0:64, 256:384], krb[i][:], ident_b[:], reads=[b_krb[i], b_ident_b], writes=[bpb])
        P.act("activation", out=ktdst[:, 0:2, :], in_=pbb[:, 0:256].rearrange("p (c t) -> p c t", t=128), func=AF.Copy,
              reads=[bpb], writes=[ktbuf])
        P.act("activation", out=ktdst[0:64, 2, :], in_=pbb[0:64, 256:384], func=AF.Copy, reads=[bpb, ktbuf], writes=[ktbuf])

    uK = [UT[:, :, 0:128], UT[:, :, 128:256]]; b_uK = [b_UT[0], b_UT[1]]
    xk = [XG[:, 0, :], XG[:, 1, :]]; b_xk = [b_XG[0], b_XG[1]]
    modp = ("p", A1p, B1p, b_modp)
    NPRE = NT

    def stage_a(T):
        i = T % 2
        make_uT(xk[i], [b_xk[i]], 128, uK[i], b_uK[i], 0, modp, "k")


    load(xk[0], xs[0:128, :], b_xk[0])
    load(xk[1], xs[128:256, :], b_xk[1])
    stage_a(0)
    for T in range(NPRE):
        i = T % 2
        if T + 2 < NPRE:
            load(xk[i], xs[(T + 2) * 128:(T + 3) * 128, :], b_xk[i])
        part2 = kv_build(uK[i], b_uK[i], 0, cs_tok_p[T * 128:(T + 1) * 128, :], ckv_p[T * 128:(T + 1) * 128, :], kr_p[T * 128:(T + 1) * 128, :],
                         VP[:, T, 0:256], b_VP[T], KT[:, :, T * 128:(T + 1) * 128], b_KT[T], split=True)
        if T + 1 < NPRE:
            stage_a(T + 1)
        part2()
        convert_some(3)
    convert_some(1000)
    uH = sb([128, 8, 32], BF16); b_uH = Buf()
    load(xk[0][0:32, :], xh, b_xk[0])
    make_uT(xk[0][0:32, :], [b_xk[0]], 32, uH, b_uH, 0, modp, "k")

    GA = REG[:, 0:8, :]; b_GA = b_REG[0:8]
    GB = REG[:, 8:16, :]; b_GB = b_REG[8:16]
    MG = REG[:, 16:24, :]; b_MG = b_REG[16:24]
    HT = REG; b_HT = b_REG
    QA = sb([128, 3, NG], name="QA"); b_QA = Buf()
    SQ = sb([128, 3, NG], BF16, name="SQ"); b_SQ = Buf()
    RQ = sb([128, NG], name="RQ"); b_RQ = Buf()
    QN = sb([128, 3, NG], BF16, name="QN"); b_QN = Buf()
    QNOPE = [sb([128, NG], BF16) for _ in range(2)]; b_QNOPE = [Buf(), Buf()]
    QT = [sb([128, 3, NG], BF16) for _ in range(2)]; b_QT = [Buf(), Buf()]
    CSQ = sb([64, 2, NG], name="CSQ"); b_CSQ = Buf()
    RT = [sb([64, NG]) for _ in range(2)]; b_RT = [Buf(), Buf()]
    PT = [sb([128, NG], BF16) for _ in range(3)]; b_PT = [Buf() for _ in range(3)]
    OL = sb([128, NTG, 256], BF16, name="OL"); b_OL = [Buf() for _ in range(NTG)]
    OLT = sb([128, 2, NG], BF16, name="OLT"); b_OLT = Buf()
    RINV = sb([128, 4]); b_RINV = [Buf() for _ in range(4)]
    CG = [sb([128, NG + 8]) for _ in range(2)]; b_CG = [Buf(), Buf()]
    VPAD = [sb([128, max(NTG * 130, 160)]) for _ in range(2)]; b_VPAD = [Buf(), Buf()]
    ZC = [sb([128, NG]) for _ in range(2)]; b_ZC = [Buf(), Buf()]
    SG = [sb([128, NG]) for _ in range(2)]; b_SG = [Buf(), Buf()]
    VLAST = sb([128, 8, 32], name="VLAST"); b_VLAST = Buf()
    cnt = {"cg": 0, "tmp": 0, "pt": 0, "yo": 0, "h": 0}

    def group(N, ntile, x_src, mod1, mod2, g1t, bg1, g2t, bg2, csq_src, y_dst, attention, sample, preloaded=False, x_next=None, n_next=0):
        NW = N
        for t in range(ntile):
            if not preloaded:
                load(XG[:, t, :], x_src[t * 128:(t + 1) * 128, :], b_XG[t])
            make_uT(XG[:, t, :], [b_XG[t]], 128, UT, b_UT[t], t * 128, mod1)
        load(CSQ[:, :, 0:N], csq_src, b_CSQ)
        wv, wb = wload(w_main_b[:, 0:384], 8, 384, [b_wconv["main"]])
        for c in range(3):
            po, bpo = psum()
            for k in range(8):
                P.pe("matmul", po[:, 0:N], lhsT=wv[:, k, c * 128:(c + 1) * 128], rhs=UT[:, k, 0:N],
                                                        start=(k == 0), stop=(k == 7), reads=wb + b_UT[0:ntile], writes=[bpo])
            P.act("activation", out=QA[:, c, 0:N], in_=po[:, 0:N], func=AF.Copy, reads=[bpo, b_QA], writes=[b_QA])
            P.act("activation", out=SQ[:, c, 0:N], in_=po[:, 0:N], func=AF.Square, reads=[bpo, b_SQ], writes=[b_SQ])
        po, bpo = psum()
        for c in range(3):
            P.pe("matmul", po[:, 0:N], lhsT=ones_b[:], rhs=SQ[:, c, 0:N], start=(c == 0), stop=(c == 2),
                 reads=[b_ones, b_SQ], writes=[bpo])
        P.act("activation", out=RQ[:, 0:N], in_=po[:, 0:N], func=AF.Ln, scale=1.0 / 384, bias=EPS, reads=[bpo], writes=[b_RQ])
        P.act("activation", out=RQ[:, 0:N], in_=RQ[:, 0:N], func=AF.Exp, scale=-0.5, reads=[b_RQ], writes=[b_RQ])
        for c in range(3):
            P.dve("scalar_tensor_tensor", out=QN[:, c, 0:N], in0=QA[:, c, 0:N], scalar=gq_t[:, c:c + 1], in1=RQ[:, 0:N],
                                                        op0=ALU.mult, op1=ALU.mult, reads=[b_QA, b_gq, b_RQ, b_QN], writes=[b_QN])
        for fc in range(8):
            wv, wb = wload(w_main_b[:, 384 + fc * 640:384 + (fc + 1) * 640], 8, 640, [b_wconv["main"]])

            def proj(mc, po, bpo, rhs, ncols, c0=0):
                for k in range(8):
                    P.pe("matmul", po[:, c0:c0 + ncols], lhsT=wv[:, k, mc * 128:(mc + 1) * 128], rhs=rhs(k),
                                                 start=(k == 0), stop=(k == 7), reads=wb + [b_uH] + b_UT[0:ntile], writes=[bpo])

            ci = cnt["cg"] % 2
            cnt["cg"] += 1
            pcg, bpcg = psum()
            proj(1, pcg, bpcg, lambda k: UT[:, k, 0:N], N)
            P.act("activation", out=CG[ci][:, 0:N], in_=pcg[:, 0:N], func=AF.Copy, reads=[bpcg], writes=[b_CG[ci]])
            pxi, bpxi = psum()
            proj(2, pxi, bpxi, lambda k: UT[:, k, 0:N], N)
            vp = VPAD[ci][:, 0:ntile * 130].rearrange("p (t j) -> p t j", j=130)
            if not sample:
                ph, bph = psum()
                nh = 2 * ntile
                proj(1, ph, bph, lambda k: uH[:, k, cnt["g0"] * 2:cnt["g0"] * 2 + nh], nh, 0)
                proj(2, ph, bph, lambda k: uH[:, k, cnt["g0"] * 2:cnt["g0"] * 2 + nh], nh, 16)
                P.act("activation", out=CG[ci][:, NG:NG + nh], in_=ph[:, 0:nh], func=AF.Copy, reads=[bph, b_CG[ci]], writes=[b_CG[ci]])
                P.dve("tensor_tensor", out=vp[:, 0:ntile, 2:130], in0=pxi[:, 0:N].rearrange("p (t j) -> p t j", j=128),
                                                in1=CG[ci][:, 0:N].rearrange("p (t j) -> p t j", j=128), op=ALU.mult,
                      reads=[bpxi, b_CG[ci]], writes=[b_VPAD[ci]])
                P.dve("tensor_tensor", out=vp[:, 0:ntile, 0:2], in0=ph[:, 16:16 + nh].rearrange("p (t j) -> p t j", j=2),
                                                in1=CG[ci][:, NG:NG + nh].rearrange("p (t j) -> p t j", j=2), op=ALU.mult,
                      reads=[bph, b_CG[ci], b_VPAD[ci]], writes=[b_VPAD[ci]])
                g0 = cnt["g0"]
                P.dve("tensor_tensor", out=vp[:, 0:ntile, 0:2], in0=vp[:, 0:ntile, 0:2],
                                                in1=hmask[:, g0:g0 + ntile].unsqueeze(2).to_broadcast([128, ntile, 2]), op=ALU.mult,
                      reads=[b_VPAD[ci], b_hmask], writes=[b_VPAD[ci]])
                vin = [vp[:, 0:ntile, j:j + 128] for j in range(3)]
                zv = ZC[ci][:, 0:N].rearrange("p (t j) -> p t j", j=128)
                if cnt["g0"] + ntile == NOWN:
                    P.dve("tensor_copy", VLAST[:, fc, 0:2], vp[:, ntile - 1, 128:130], reads=[b_VPAD[ci], b_VLAST], writes=[b_VLAST])
            else:
                vps = VPAD[ci][:, 0:160].rearrange("p (s j) -> p s j", j=10)
                P.dve("tensor_tensor", out=vps[:, :, 2:10], in0=pxi[:, 0:128].rearrange("p (s j) -> p s j", j=8),
                                                in1=CG[ci][:, 0:128].rearrange("p (s j) -> p s j", j=8), op=ALU.mult,
                      reads=[bpxi, b_CG[ci]], writes=[b_VPAD[ci]])
                P.dve("tensor_copy", vps[:, :, 0:2], SCT[:, fc, :].rearrange("p (s j) -> p s j", j=2),
                      reads=[b_SCT, b_VPAD[ci]], writes=[b_VPAD[ci]])
                vin = [vps[:, :, j:j + 8] for j in range(3)]
                zv = ZC[ci][:, 0:128].rearrange("p (s j) -> p s j", j=8)
                P.dve("tensor_copy", VLAST[:, fc, :].rearrange("p (s j) -> p s j", j=2), vps[:, :, 8:10],
                      reads=[b_VPAD[ci], b_VLAST], writes=[b_VLAST])
            P.dve("tensor_scalar", zv, vin[0], convw[:, fc * 3:fc * 3 + 1], None, op0=ALU.mult,
                  reads=[b_VPAD[ci], b_convw], writes=[b_ZC[ci]])
            for j in (1, 2):
                P.dve("scalar_tensor_tensor", out=zv, in0=vin[j], scalar=convw[:, fc * 3 + j:fc * 3 + j + 1], in1=zv,
                                                            op0=ALU.mult, op1=ALU.add, reads=[b_VPAD[ci], b_convw, b_ZC[ci]], writes=[b_ZC[ci]])
            pbg, bpbg = psum()
            proj(0, pbg, bpbg, lambda k: UT[:, k, 0:N], N)
            P.dve("tensor_tensor", out=ZC[ci][:, 0:N], in0=pbg[:, 0:N], in1=ZC[ci][:, 0:N], op=ALU.mult,
                  reads=[bpbg, b_ZC[ci]], writes=[b_ZC[ci]])
            pga, bpga = psum()
            proj(3, pga, bpga, lambda k: UT[:, k, 0:N], N)
            P.act("activation", out=GA[:, fc, 0:N], in_=pga[:, 0:N], func=AF.Sigmoid, reads=[bpga], writes=[b_GA[fc]])
            pgb, bpgb = psum()
            proj(4, pgb, bpgb, lambda k: UT[:, k, 0:N], N)
            P.act("activation", out=SG[ci][:, 0:N], in_=pgb[:, 0:N], func=AF.Sigmoid, reads=[bpgb], writes=[b_SG[ci]])
            P.dve("tensor_tensor", out=GB[:, fc, 0:N], in0=SG[ci][:, 0:N], in1=ZC[ci][:, 0:N], op=ALU.mult,
                  reads=[b_SG[ci], b_ZC[ci]], writes=[b_GB[fc]])

        attention(N)

        wv, wb = wload(w_o_b, 8, D, [b_wconv["o"]])
        for t in range(ntile):
            for n in range(2):
                po, bpo = psum()
                for k in range(8):
                    P.pe("matmul", po[:, :], lhsT=MG[:, k, t * 128:(t + 1) * 128], rhs=wv[:, k, n * 512:(n + 1) * 512],
                                                                 start=(k == 0), stop=(k == 7), reads=wb + [b_MG[k]], writes=[bpo])
                i = cnt["tmp"] % 2
                cnt["tmp"] += 1
                P.dve("tensor_tensor", out=TMP[i][:], in0=po[:, :], in1=g1t[:, n * 512:(n + 1) * 512], op=ALU.mult,
                      reads=[bpo, bg1], writes=[b_TMP[i]])
                P.dve("tensor_tensor", out=XG[:, t, n * 512:(n + 1) * 512], in0=XG[:, t, n * 512:(n + 1) * 512],
                                                               in1=TMP[i][:], op=ALU.add, reads=[b_TMP[i], b_XG[t]], writes=[b_XG[t]])
        for t in range(ntile):
            make_uT(XG[:, t, :], [b_XG[t]], 128, UT, b_UT[t], t * 128, mod2)
        for j2 in range(8):
            wv, wb = wload(w_1_b[:, j2 * 512:(j2 + 1) * 512], 8, 512, [b_wconv["1"]])
            for m in range(4):
                po, bpo = psum()
                for k in range(8):
                    P.pe("matmul", po[:, 0:N], lhsT=wv[:, k, m * 128:(m + 1) * 128], rhs=UT[:, k, 0:N],
                                                            start=(k == 0), stop=(k == 7), reads=wb + b_UT[0:ntile], writes=[bpo])
                hc = j2 * 4 + m
                P.act("activation", out=TMP[hc % 2][:, 0:N], in_=po[:, 0:N], func=AF.Relu,
                      reads=[bpo], writes=[b_TMP[hc % 2]])
                P.dve("tensor_tensor", out=HT[:, hc, 0:N], in0=TMP[hc % 2][:, 0:N], in1=TMP[hc % 2][:, 0:N], op=ALU.mult,
                      reads=[b_TMP[hc % 2]], writes=[b_HT[hc]])
        for j in range(4):
            wva, wba = wload(w_2_b[0:2048, j * 256:(j + 1) * 256], 16, 256, [b_wconv["2"]])
            wvb, wbb = wload(w_2_b[2048:4096, j * 256:(j + 1) * 256], 16, 256, [b_wconv["2"]])
            for t in range(ntile):
                po, bpo = psum()
                for k in range(32):
                    wv, wb = (wva, wba) if k < 16 else (wvb, wbb)
                    P.pe("matmul", po[:, 0:256], lhsT=HT[:, k, t * 128:(t + 1) * 128], rhs=wv[:, k % 16, :],
                         start=(k == 0), stop=(k == 31), reads=wb + [b_HT[k]], writes=[bpo])
                i2 = cnt["tmp"] % 2
                cnt["tmp"] += 1
                P.dve("tensor_tensor", out=TMP[i2][:, 0:256], in0=po[:, 0:256], in1=g2t[:, j * 256:(j + 1) * 256], op=ALU.mult,
                      reads=[bpo, bg2], writes=[b_TMP[i2]])
                P.dve("tensor_tensor", out=XG[:, t, j * 256:(j + 1) * 256], in0=XG[:, t, j * 256:(j + 1) * 256],
                                                                 in1=TMP[i2][:, 0:256], op=ALU.add, reads=[b_TMP[i2], b_XG[t]], writes=[b_XG[t]])
        for t in range(ntile):
            rs, rb = rstd_of(XG[:, t, :], [b_XG[t]], 128, D, 1.0 / D)
            i = cnt["yo"] % 2
            cnt["yo"] += 1
            P.dve("scalar_tensor_tensor", out=YO[i][:], in0=XG[:, t, :], scalar=rs, in1=gfin_bc[:], op0=ALU.mult, op1=ALU.mult,
                  reads=[b_XG[t], rb, b_gfin], writes=[b_YO[i]])
            if x_next is not None and t < n_next:
                load(XG[:, t, :], x_next[t * 128:(t + 1) * 128, :], b_XG[t])
            P.dma("dma_start", out=y_dst[t * 128:(t + 1) * 128, :], in_=YO[i][:], reads=[b_YO[i]])

    def head_q(h, N, ktcol_lhsT):
        qi = h % 2
        po, bpo = psum()
        for c in range(3):
            P.pe("matmul", po[:, 0:N], lhsT=wq[:, c, h * 256:h * 256 + 128], rhs=QN[:, c, 0:N], start=(c == 0), stop=(c == 2),
                 reads=[b_wq, b_QN], writes=[bpo])
        P.act("activation", out=QNOPE[qi][:, 0:N], in_=po[:, 0:N], func=AF.Copy, reads=[bpo], writes=[b_QNOPE[qi]])
        pr, bpr = psum()
        for c in range(3):
            P.pe("matmul", pr[0:64, 0:N], lhsT=wq[:, c, h * 256 + 128:h * 256 + 192], rhs=QN[:, c, 0:N], start=(c == 0), stop=(c == 2),
                 reads=[b_wq, b_QN], writes=[bpr])
        pw, bpw = psum()
        for c in range(3):
            P.pe("matmul", pw[0:64, 0:N], lhsT=wq[:, c, h * 256 + 192:h * 256 + 256], rhs=QN[:, c, 0:N], start=(c == 0), stop=(c == 2),
                 reads=[b_wq, b_QN], writes=[bpw])
        P.dve("tensor_tensor", out=RT[0][:, 0:N], in0=pr[0:64, 0:N], in1=CSQ[:, 0, 0:N], op=ALU.mult, reads=[bpr, b_CSQ], writes=[b_RT[0]])
        P.dve("tensor_tensor", out=RT[1][:, 0:N], in0=pw[0:64, 0:N], in1=CSQ[:, 1, 0:N], op=ALU.mult, reads=[bpw, b_CSQ], writes=[b_RT[1]])
        P.dve("tensor_tensor", out=QT[qi][0:64, 2, 0:N], in0=RT[0][:, 0:N], in1=RT[1][:, 0:N], op=ALU.add,
              reads=[b_RT[0], b_RT[1], b_QT[qi]], writes=[b_QT[qi]])
        for m in range(2):
            pl, bpl = psum()
            P.pe("matmul", pl[:, 0:N], lhsT=wuk[:, h, m * 128:(m + 1) * 128], rhs=QNOPE[qi][:, 0:N], start=True, stop=True,
                 reads=[b_wuk, b_QNOPE[qi]], writes=[bpl])
            P.act("activation", out=QT[qi][:, m, 0:N], in_=pl[:, 0:N], func=AF.Copy, reads=[bpl, b_QT[qi]], writes=[b_QT[qi]])
        return qi

    def merge_head(h, N, pa, bpa):
        i = cnt["tmp"] % 2
        cnt["tmp"] += 1
        P.dve("tensor_tensor", out=TMP[i][:, 0:N], in0=pa[:, 0:N], in1=GA[:, h, 0:N], op=ALU.mult, reads=[bpa, b_GA[h]], writes=[b_TMP[i]])
        P.dve("tensor_tensor", out=MG[:, h, 0:N], in0=TMP[i][:, 0:N], in1=GB[:, h, 0:N], op=ALU.add, reads=[b_TMP[i], b_GB[h]], writes=[b_MG[h]])

    SBANK = (0, 1)
    OBANK = (2, 3, 4, 5)
    GBANK = [6, 7]

    def prompt_attention_factory(i0, ntile):
        def attention(N):
            nk = 2 * (i0 + ntile)
            g8 = 2 * i0
            DEFB[:] = [6, 7]
            GBANK[:] = [6, 7]

            def prep(h):
                qi = head_q(h, N, None)
                pm, bpm = psum(GBANK, "gb")
                for c in range(3):
                    kc = 128 if c < 2 else 64
                    P.pe("matmul", pm[0:1, 0:N], lhsT=KT[0:kc, c, 0:1], rhs=QT[qi][0:kc, c, 0:N], start=(c == 0), stop=(c == 2),
                         reads=[b_KT[0], b_QT[qi]], writes=[bpm])
                P.act("activation", out=QT[qi][64:65, 2, 0:N], in_=pm[0:1, 0:N], func=AF.Copy, scale=-1.0, reads=[bpm, b_QT[qi]], writes=[b_QT[qi]])
                return qi

            qis = {0: prep(0)}
            deferred = []

            for h in range(8):
                if h + 1 < 8:
                    qis[h + 1] = prep(h + 1)
                qi = qis[h]
                obk = OBANK[0:ntile] if (ntile > 2 or h % 2 == 0) else OBANK[2:2 + ntile]

                def stage_S(kt):
                    j = max(0, (kt - g8) // 2)
                    q0 = j * 128
                    e_ = (kt - g8) % 2 if kt >= g8 else None
                    pS, bpS = psum(SBANK, "sb")
                    for c in range(3):
                        kc = 128 if c < 2 else 65
                        P.pe("matmul", pS[:, q0:N], lhsT=KT[0:kc, c, kt * 128:(kt + 1) * 128], rhs=QT[qi][0:kc, c, q0:N],
                             start=(c == 0), stop=(c == 2), reads=[b_KT[kt], b_KTaug, b_QT[qi]], writes=[bpS])
                    pi = cnt["pt"] % 3
                    cnt["pt"] += 1
                    P.act("activation", out=PT[pi][:, q0:N], in_=pS[:, q0:N], func=AF.Exp, scale=SCALE, reads=[bpS], writes=[b_PT[pi]])
                    if e_ is not None:
                        P.dve("tensor_tensor", out=PT[pi][:, q0:q0 + 128], in0=PT[pi][:, q0:q0 + 128],
                              in1=masks[:, e_ * 128:(e_ + 1) * 128], op=ALU.mult, reads=[b_PT[pi], b_masks], writes=[b_PT[pi]])
                    return (kt, j, pi)

                def stage_V(kt, j, pi):
                    for jq in range(j, ntile):
                        last = g8 + 2 * jq + 1
                        P.pe("matmul", ps[obk[jq]][:, 0:257], lhsT=PT[pi][:, jq * 128:(jq + 1) * 128], rhs=VP[:, kt, 0:257],
                             start=(kt == 0), stop=(kt == last), reads=[b_PT[pi], b_VP[kt]], writes=[bps[obk[jq]]])

                pending = None
                for kt in range(nk):
                    info = stage_S(kt)
                    if pending is not None:
                        stage_V(*pending)
                    pending = info
                    if kt == min(2, nk - 1) and deferred:
                        deferred.pop()()
                stage_V(*pending)

                def epilogue(h=h, obk=obk):
                    for jq in range(ntile):
                        ob = ps[obk[jq]]; bob = bps[obk[jq]]
                        P.dve("reciprocal", RINV[:, jq:jq + 1], ob[:, 256:257], reads=[bob], writes=[b_RINV[jq]])
                        P.act("activation", out=OL[:, jq, :], in_=ob[:, 0:256], func=AF.Copy, scale=RINV[:, jq:jq + 1],
                              reads=[bob, b_RINV[jq]], writes=[b_OL[jq]])
                    pb, bpb = psum(GBANK, "gb")
                    pbb = pb[:, :].bitcast(BF16)
                    for jq in range(ntile):
                        for m in range(2):
                            P.pe("transpose", pbb[:, m * N + jq * 128:m * N + (jq + 1) * 128], OL[:, jq, m * 128:(m + 1) * 128], ident_b[:],
                                 reads=[b_OL[jq], b_ident_b], writes=[bpb])
                    P.act("activation", out=OLT[:].rearrange("p m q -> p (m q)"), in_=pbb[:, 0:2 * N], func=AF.Copy, reads=[bpb], writes=[b_OLT])
                    pa, bpa = psum(GBANK, "gb")
                    for m in range(2):
                        P.pe("matmul", pa[:, 0:N], lhsT=wuv[:, m, h, :], rhs=OLT[:, m, 0:N], start=(m == 0), stop=(m == 1),
                             reads=[b_wuv, b_OLT], writes=[bpa])
                    merge_head(h, N, pa, bpa)

                if ntile <= 2:
                    deferred.append(epilogue)
                else:
                    epilogue()
            while deferred:
                deferred.pop()()
            DEFB[:] = [0, 1, 2, 3, 4, 5, 6, 7]
        return attention

    if STAGE >= 2:
        for g in range(NOWN // NTG):
            cnt["g0"] = NTG * g
            last_g = (g == NOWN // NTG - 1)
            group(NG, NTG, xo[g * NG:(g + 1) * NG, :], modp, ("p", A2p, B2p, b_modp), g1p, b_g["g1p"], g2p, b_g["g2p"],
                  cs_feat_p.rearrange("p (a t) -> p a t", a=2)[:, :, g * NG:(g + 1) * NG], y_p[g * NG:(g + 1) * NG, :],
                  prompt_attention_factory(NTG * g, NTG), False, preloaded=(g > 0),
                  x_next=(xd if last_g else xo[(g + 1) * NG:(g + 2) * NG, :]) if STAGE >= 3 or not last_g else None,
                  n_next=(1 if last_g else NTG))
        for c8 in range(8):
            P.dma("dma_start", out=conv_p[:, c8 * 128:(c8 + 1) * 128].rearrange("t p -> p t"), in_=VLAST[:, c8, 0:2],
                  allow_slow_non_contiguous=True, reads=[b_VLAST])

    SCT = sb([128, 8, 32], name="SCT"); b_SCT = Buf()
    if STAGE >= 3:
        sct = xn[1][0:32, :]; b_sct = b_xn[1]; load(sct[:], sconv, b_sct)
        po, bpo = psum()
        for k in range(8):
            P.pe("transpose", po[:, k * 32:(k + 1) * 32], sct[0:32, k * 128:(k + 1) * 128], ident_f[0:32, 0:32],
                 reads=[b_sct, b_ident_f], writes=[bpo])
        P.dve("tensor_copy", SCT[:].rearrange("p k s -> p (k s)"), po[:, 0:256], reads=[bpo], writes=[b_SCT])

        KTN = carve([128, 3, 128]); b_KTN = Buf()
        VN = carve([128, 258]); b_VN = Buf()
        kind_f = sb([2, 128]); b_kind = Buf(); load(kind_f[:], kind_d, b_kind)
        augc = sb([2, 256]); b_augc = Buf(); load(augc[:], augc_d, b_augc)
        smask = carve([128, 8, 128]); b_smask = Buf()
        ptt = sb([128, 8], I32); b_ptt = Buf(); load(ptt[:], pt, b_ptt)
        QS = carve([128, 3, 8, 128]); b_QS = Buf()
        QP = [carve([128, 3, 128]) for _ in range(2)]; b_QP = [Buf(), Buf()]
        OTS = carve([128, 2, 8, 128]); b_OTS = Buf()
        R = 8
        KVT = [carve([128, R * 256]) for i in range(3)]; b_KVT = [Buf() for _ in range(3)]
        KRT = [carve([128, 16 * 64]) for i in range(2)]; b_KRT = [Buf() for _ in range(2)]
        KTS = [carve([128, 2, 3, 128]) for i in range(3)]; b_KTS = [Buf() for _ in range(3)]
        PS4 = [carve([128, 4, 128]) for _ in range(2)]; b_PS4 = [Buf(), Buf()]
        PN = carve([128, 128]); b_PN = Buf()
        OLS = carve([128, 256]); b_OLS = Buf()
        alias_barrier([b_KTN, b_VN, b_smask, b_QS, b_OTS, b_PN, b_OLS] + b_QP + b_KVT + b_KRT + b_KTS + b_PS4, b_KT + b_VP + [b_KTaug])
        P.dve("memset", VN[:, 256:258], 1.0, writes=[b_VN])
        P.dve("tensor_copy", KTN[64:66, 2, :], kind_f[:], reads=[b_kind], writes=[b_KTN])
        load(smask.rearrange("p a b -> p (a b)"), smask_d, b_smask, q=POOL)
        for i in range(3):
            P.dve("tensor_copy", KTS[i][64:66, :, 2, 0:64], kind_f[:, 0:1].unsqueeze(1).to_broadcast([2, 2, 64]),
                  reads=[b_kind], writes=[b_KTS[i]])
            P.dve("tensor_copy", KTS[i][64:66, :, 2, 64:128], kind_f[:, 8:9].unsqueeze(1).to_broadcast([2, 2, 64]),
                  reads=[b_kind, b_KTS[i]], writes=[b_KTS[i]])
        RIS = sb([128, 1]); b_RIS = Buf()
        VT = xn[0][0:32, :]; b_VT = b_xn[0]

        mods1 = ("s", A1s, modT[:, 0, :, 1:17], b_mods)
        mods2 = ("s", A2s, modT[:, 2, :, 1:17], b_mods)

        def sample_attention(N):
            DEFB[:] = [4, 5, 6, 7]
            GBANK[:] = [4, 5, 6, 7]
            kv_build(UT, b_UT[0], 0, cs_tok_s, ckv_s, kr_s, VN[:, 0:256], b_VN, KTN[:, :, :], b_KTN)
            for h in range(8):
                qi = head_q(h, N, None)
                P.dve("tensor_copy", QS[:, 0:2, h, :], QT[qi][:, 0:2, 0:128], reads=[b_QT[qi], b_QS], writes=[b_QS])
                P.dve("tensor_copy", QS[0:64, 2, h, :], QT[qi][0:64, 2, 0:128], reads=[b_QT[qi], b_QS], writes=[b_QS])
            def issue_kv(n):
                if n >= 128:
                    return
                jj, rbb = n // 16, n % 16
                if rbb % 2 == 0:
                    gk = n // 2
                    P.dma("indirect_dma_start", out=KRT[gk % 2][:, :], out_offset=None, in_=kr_pool,
                          in_offset=bass.IndirectOffsetOnAxis(ap=ptt[:, jj:jj + 1], axis=0),
                          element_offset=(rbb // 2) * 16 * 64, reads=[b_ptt], writes=[b_KRT[gk % 2]], q=POOL)
                P.dma("indirect_dma_start", out=KVT[n % 3][:, :], out_offset=None, in_=ckv_pool,
                      in_offset=bass.IndirectOffsetOnAxis(ap=ptt[:, jj:jj + 1], axis=0),
                      element_offset=rbb * R * 256, reads=[b_ptt], writes=[b_KVT[n % 3]], q=POOL)

            issue_kv(0)
            for j in range(8):
                qp = QP[j % 2]; bqp = b_QP[j % 2]
                for c in range(3):
                    kc = 128 if c < 2 else 64
                    P.dve("tensor_copy",
                          qp[0:kc, c, :].rearrange("p (s h t) -> p s h t", s=2, h=8),
                          QS[0:kc, c, :, 16 * j:16 * j + 16].rearrange("p h (s t) -> p s h t", t=8), reads=[b_QS, bqp], writes=[bqp])
                pm, bpm = psum(GBANK, "gb")
                for c in range(3):
                    kc = 128 if c < 2 else 64
                    P.pe("matmul", pm[0:2, 0:128], lhsT=KTN[0:kc, c, 16 * j:16 * j + 16:8], rhs=qp[0:kc, c, :],
                         start=(c == 0), stop=(c == 2), reads=[b_KTN, bqp], writes=[bpm])
                P.dve("tensor_tensor", out=augt[:], in0=pm[0:2, 0:128], in1=augc[:, 0:128], op=ALU.mult, reads=[bpm, b_augc], writes=[b_augt])
                P.dve("tensor_tensor", out=qp[64:66, 2, :], in0=augt[:], in1=augc[:, 128:256], op=ALU.add, reads=[b_augt, b_augc, bqp], writes=[bqp])
                ob = ps[OBANK[j % 2]]; bob = bps[OBANK[j % 2]]
                pS, bpS = psum(SBANK, "sb")
                for c in range(3):
                    kc = 128 if c < 2 else 66
                    P.pe("matmul", pS[:, 0:128], lhsT=KTN[0:kc, c, :], rhs=qp[0:kc, c, :], start=(c == 0), stop=(c == 2),
                         reads=[b_KTN, bqp], writes=[bpS])
                P.act("activation", out=PN[:], in_=pS[:, 0:128], func=AF.Exp, scale=SCALE, reads=[bpS], writes=[b_PN])
                P.dve("tensor_tensor", out=PN[:], in0=PN[:], in1=smask[:, j, :], op=ALU.mult, reads=[b_PN, b_smask], writes=[b_PN])
                P.pe("matmul", ob[:, 0:257], lhsT=PN[:], rhs=VN[:, 0:257], start=True, stop=False, reads=[b_PN, b_VN], writes=[bob])

                def stage_T(st):
                    n = 16 * j + st // 4
                    r0 = (st % 4) * 2
                    kvv = KVT[n % 3][:, :].rearrange("p (r f) -> p r f", f=256)
                    krv = KRT[(n // 2) % 2][:, :].rearrange("p (r f) -> p r f", f=64)
                    ti = cnt["h"] % 3
                    cnt["h"] += 1
                    pb, bpb = psum(GBANK, "gb")
                    pbb = pb[:, :].bitcast(BF16)
                    for rr_ in range(2):
                        r = r0 + rr_
                        rk = (n % 2) * 8 + r
                        P.pe("transpose", pbb[:, (rr_ * 3) * 128:(rr_ * 3 + 1) * 128], kvv[:, r, 0:128], ident_b[:],
                             reads=[b_KVT[n % 3], b_ident_b], writes=[bpb])
                        P.pe("transpose", pbb[:, (rr_ * 3 + 1) * 128:(rr_ * 3 + 2) * 128], kvv[:, r, 128:256], ident_b[:],
                             reads=[b_KVT[n % 3], b_ident_b], writes=[bpb])
                        P.pe("transpose", pbb[0:64, (rr_ * 3 + 2) * 128:(rr_ * 3 + 3) * 128], krv[:, rk, :], ident_b[:],
                             reads=[b_KRT[(n // 2) % 2], b_ident_b], writes=[bpb])
                    pv = pbb[:, 0:768].rearrange("p (a c t) -> p a c t", a=2, c=3)
                    P.act("activation", out=KTS[ti][:, :, 0:2, :], in_=pv[:, :, 0:2, :], func=AF.Copy, reads=[bpb, b_KTS[ti]], writes=[b_KTS[ti]])
                    P.dve("tensor_copy", KTS[ti][0:64, :, 2, :], pv[0:64, :, 2, :], reads=[bpb, b_KTS[ti]], writes=[b_KTS[ti]])
                    return ti

                def stage_S(st, ti, pS, bpS):
                    for rr_ in range(2):
                        col = ((st % 2) * 2 + rr_) * 128
                        for c in range(3):
                            kc = 128 if c < 2 else 66
                            P.pe("matmul", pS[:, col:col + 128], lhsT=KTS[ti][0:kc, rr_, c, :], rhs=qp[0:kc, c, :],
                                 start=(c == 0), stop=(c == 2), reads=[b_KTS[ti], bqp], writes=[bpS])

                def stage_V(q, pi):
                    n = 16 * j + q // 2
                    kvv = KVT[n % 3][:, :].rearrange("p (r f) -> p r f", f=256)
                    for r_ in range(4):
                        r = (q % 2) * 4 + r_
                        lastmm = (q == 31 and r_ == 3)
                        P.pe("matmul", ob[:, 0:256], lhsT=PS4[pi][:, r_, :], rhs=kvv[:, r, :], start=False, stop=lastmm, skip_group_check=True,
                             reads=[b_PS4[pi], b_KVT[n % 3]], writes=[bob])
                        P.pe("matmul", ob[:, 256:257], lhsT=PS4[pi][:, r_, :], rhs=ones_b[:, 0:1], start=False, stop=lastmm, skip_group_check=True,
                             reads=[b_PS4[pi], b_ones], writes=[bob])

                tis = {0: stage_T(0)}
                prev = None
                for st in range(64):
                    if st == 1 and j > 0:
                        issue_kv(16 * j + 2)
                    if st == 0 and j == 0:
                        issue_kv(1)
                        issue_kv(2)
                    if st + 1 < 64:
                        tis[st + 1] = stage_T(st + 1)
                    if st % 2 == 0:
                        pS, bpS = psum(SBANK, "sb")
                    stage_S(st, tis[st], pS, bpS)
                    if st % 2 == 1:
                        q = st // 2
                        pi = q % 2
                        P.act("activation", out=PS4[pi][:].rearrange("p a b -> p (a b)"), in_=pS[:, :], func=AF.Exp, scale=SCALE,
                              reads=[bpS], writes=[b_PS4[pi]])
                        if prev is not None:
                            stage_V(*prev)
                            if prev[0] % 2 == 1:
                                issue_kv(16 * j + prev[0] // 2 + 3)
                        prev = (q, pi)
                stage_V(*prev)
                P.dve("reciprocal", RIS[:], ob[:, 256:257], reads=[bob], writes=[b_RIS])
                P.act("activation", out=OLS[:], in_=ob[:, 0:256], func=AF.Copy, scale=RIS[:, 0:1], reads=[bob, b_RIS], writes=[b_OLS])
                pb, bpb = psum(GBANK, "gb")
                pbb = pb[:, :].bitcast(BF16)
                for m in range(2):
                    P.pe("transpose", pbb[:, m * 128:(m + 1) * 128], OLS[:, m * 128:(m + 1) * 128], ident_b[:],
                         reads=[b_OLS, b_ident_b], writes=[bpb])
                for m in range(2):
                    P.dve("tensor_copy", OTS[:, m, :, 16 * j:16 * j + 16].rearrange("p h (s t) -> p s h t", t=8),
                                                                pbb[:, m * 128:(m + 1) * 128].rearrange("p (s h t) -> p s h t", s=2, h=8),
                          reads=[bpb, b_OTS], writes=[b_OTS])
            for h in range(8):
                pa, bpa = psum(GBANK, "gb")
                for m in range(2):
                    P.pe("matmul", pa[:, 0:N], lhsT=wuv[:, m, h, :], rhs=OTS[:, m, h, :], start=(m == 0), stop=(m == 1),
                         reads=[b_wuv, b_OTS], writes=[bpa])
                merge_head(h, N, pa, bpa)
            DEFB[:] = [0, 1, 2, 3, 4, 5, 6, 7]

        augt = sb([2, 128]); b_augt = Buf()
        cnt["g0"] = 0
        group(128, 1, xd, mods1, mods2, g1s, b_g["g1s"], g2s, b_g["g2s"],
              cs_feat_s.rearrange("p (a t) -> p a t", a=2), y_s, sample_attention, True, preloaded=True)
        for hh in range(2):
            po, bpo = psum()
            for kk in range(4):
                k = hh * 4 + kk
                P.pe("transpose", po[0:32, kk * 128:(kk + 1) * 128], VLAST[:, k, :], ident_f[:], reads=[b_VLAST, b_ident_f], writes=[bpo])
            P.dve("tensor_copy", VT[:, hh * 512:(hh + 1) * 512], po[0:32, :], reads=[bpo, b_VT], writes=[b_VT])
        P.dma("dma_start", out=conv_s, in_=VT[:], reads=[b_VT])

    P.emit()
    return nc


_NC_CACHE = {}


def _rope_tables(pos):
    inv = (np.float32(10000.0) ** (-np.arange(0, 64, 2, dtype=np.float32) / np.float32(64))).astype(np.float32)
    ang = pos.astype(np.float32)[:, None] * inv[None, :]
    return np.cos(ang).astype(np.float32), np.sin(ang).astype(np.float32)


def kernel(x_prompt, x_sample, cache_ckv, cache_krope, state_conv, page_table, c_prompt, c_sample,
           w_ada, b_ada, g_attn, w_in, g_q, w_q_b, g_kv, w_kv_b, conv_w, w_o, g_mlp, w_1, w_2, g_final):
    f32 = np.float32
    A = lambda a: np.ascontiguousarray(np.asarray(a))
    x_prompt = np.asarray(x_prompt, f32); x_sample = np.asarray(x_sample, f32)
    if "nc" not in _NC_CACHE:
        _NC_CACHE["nc"] = build_program()
    nc = _NC_CACHE["nc"]

    ident = np.eye(128, dtype=f32)
    tri = (np.arange(128)[:, None] <= np.arange(128)[None, :]).astype(f32)
    cos_p, sin_p = _rope_tables(np.arange(SEQ))
    cs_tok_p = np.concatenate([cos_p, cos_p, -sin_p, sin_p], axis=1)
    pos_s = 8192 + (np.arange(128) % 8)
    cos_s, sin_s = _rope_tables(pos_s)
    cs_tok_s = np.concatenate([cos_s, cos_s, -sin_s, sin_s], axis=1)
    cs_feat_s = np.concatenate([np.concatenate([cos_s, cos_s], 1).T, np.concatenate([-sin_s, sin_s], 1).T], axis=1)
    smask = np.zeros((128, 8, 128), f32)
    kk_s = np.arange(128) // 8; kk_t = np.arange(128) % 8
    qc = np.arange(128); q_sl = qc // 64; q_t = qc % 8
    for j in range(8):
        smask[:, j, :] = ((kk_s[:, None] == (2 * j + q_sl)[None, :]) & (kk_t[:, None] <= q_t[None, :])).astype(f32)
    augc = np.zeros((2, 256), f32)
    augc[0, 0:64] = -1.0; augc[1, 64:128] = -1.0
    augc[0, 128 + 64:256] = -BIG; augc[1, 128:128 + 64] = -BIG
    kind = np.zeros((2, 128), f32)
    kind[0, :] = ((np.arange(128) // 8) % 2 == 0); kind[1, :] = ((np.arange(128) // 8) % 2 == 1)

    w_in0 = np.asarray(w_in[0], f32)
    q_a_w = w_in0[:, 0:384]; ckv_w = w_in0[:, 384:640]; kr_w = w_in0[:, 640:704]
    bg_w = w_in0[:, 704:1728]; cg_w = w_in0[:, 1728:2752]; xin_w = w_in0[:, 2752:3776]
    ga_w = w_in0[:, 3776:4800]; gb_w = w_in0[:, 4800:5824]
    kr_sw = np.concatenate([kr_w[:, 32:64], kr_w[:, 0:32]], axis=1)
    w_kvin = A(np.concatenate([ckv_w, kr_w, kr_sw], axis=1))
    parts = [q_a_w]
    for fc in range(8):
        sl = slice(fc * 128, (fc + 1) * 128)
        parts += [bg_w[:, sl], cg_w[:, sl], xin_w[:, sl], ga_w[:, sl], gb_w[:, sl]]
    w_main = A(np.concatenate(parts, axis=1))
    wqb = np.asarray(w_q_b[0], f32).reshape(384, 8, 192)
    w_q = A(np.concatenate([wqb[:, :, 0:128], wqb[:, :, 128:192], wqb[:, :, 160:192], wqb[:, :, 128:160]], axis=2).reshape(384, 2048))
    wkvb = np.asarray(w_kv_b[0], f32).reshape(256, 8, 256)
    w_ukT = A(wkvb[:, :, 0:128].transpose(2, 1, 0).reshape(128, 2048))
    w_uv = A(wkvb[:, :, 128:256].reshape(256, 1024))
    convT = A(np.asarray(conv_w[0], f32).T.reshape(8, 128, 3).transpose(1, 0, 2).reshape(128, 24))
    fm = lambda v, n: A(np.asarray(v, f32).reshape(n, 128).T)
    badaT = fm(b_ada[0], 48)
    bada_g = A(np.stack([np.asarray(b_ada[0], f32)[2048:3072], np.asarray(b_ada[0], f32)[5120:6144]]))
    shared = {
        "w_ada": A(np.asarray(w_ada[0], f32)), "badaT": badaT, "bada_g": bada_g,
        "gattnT": fm(g_attn[0], 8), "gmlpT": fm(g_mlp[0], 8), "gqT": fm(g_q[0], 3),
        "gkv": A(np.asarray(g_kv[0], f32)), "gfin": A(np.asarray(g_final, f32)),
        "w_kvin": w_kvin, "w_main": w_main, "w_q": w_q, "w_ukT": w_ukT, "w_uv": w_uv, "convT": convT,
        "w_o": A(np.asarray(w_o[0], f32)), "w_1": A(np.asarray(w_1[0], f32)), "w_2": A(np.asarray(w_2[0], f32)),
        "ident": ident, "cs_tok_p": A(cs_tok_p), "cs_tok_s": A(cs_tok_s), "cs_feat_s": A(cs_feat_s),
        "smask": A(smask.reshape(128, 1024)), "augc": augc, "kind": kind,
        "ckv_pool": np.asarray(cache_ckv[0], f32).reshape(NPOOL, 128 * 256),
        "kr_pool": np.asarray(cache_krope[0], f32).reshape(NPOOL, 128 * 64),
    }
    page_table = np.asarray(page_table, np.int32)
    in_maps = []
    own_rows = []
    for c in range(8):
        b, half = c // 2, c % 2
        tiles = [2 * i + half for i in range(NOWN)]
        rows = np.concatenate([np.arange(t * 128, (t + 1) * 128) for t in tiles])
        own_rows.append(rows)
        xb = x_prompt[b]
        xh = np.zeros((32, D), f32)
        hm = np.ones((128, 16), f32)
        for i, t in enumerate(tiles):
            if t == 0:
                hm[:, i] = 0.0
            else:
                xh[2 * i:2 * i + 2] = xb[t * 128 - 2:t * 128]
        masks = np.concatenate([tri, np.zeros((128, 128), f32)], 1) if half == 0 else np.concatenate([np.ones((128, 128), f32), tri], 1)
        cosq = np.concatenate([cos_p[rows], cos_p[rows]], 1).T
        sinq = np.concatenate([-sin_p[rows], sin_p[rows]], 1).T
        cs_feat_p = np.concatenate([cosq, sinq], axis=1)
        seqs = np.arange(16 * c, 16 * c + 16)
        ptc = np.zeros((128, 8), np.int32)
        for j in range(8):
            ptc[0:64, j] = page_table[seqs[2 * j]]
            ptc[64:128, j] = page_table[seqs[2 * j + 1]]
        m = dict(shared)
        m.update({
            "xs": A(xb), "xo": A(xb[rows]), "xh": xh, "xd": A(x_sample[seqs].reshape(128, D)),
            "cc": A(np.concatenate([np.asarray(c_prompt, f32)[b:b + 1], np.asarray(c_sample, f32)[seqs]], 0)),
            "pt": ptc, "sconv": A(np.asarray(state_conv[0], f32)[seqs].reshape(32, D)),
            "masks": A(masks), "hmask": hm, "cs_feat_p": A(cs_feat_p),
        })
        in_maps.append(m)

    res = run_bass_kernel_spmd(nc, in_maps, core_ids=list(range(8)))
    R = res.results
    y_prompt = np.zeros((4, SEQ, D), f32)
    y_sample = np.zeros((128, 8, D), f32)
    ckv_pr = np.zeros((1, 4, SEQ, 256), f32); kr_pr = np.zeros((1, 4, SEQ, 64), f32); conv_pr = np.zeros((1, 4, 2, D), f32)
    ckv_sm = np.zeros((1, 128, 8, 256), f32); kr_sm = np.zeros((1, 128, 8, 64), f32); conv_sm = np.zeros((1, 128, 2, D), f32)
    for c in range(8):
        b, half = c // 2, c % 2
        r = R[c]
        y_prompt[b, own_rows[c]] = r["y_p"]
        y_sample[16 * c:16 * c + 16] = r["y_s"].reshape(16, 8, D)
        if half == 0:
            ckv_pr[0, b] = r["ckv_p"]; kr_pr[0, b] = r["kr_p"]
        else:
            conv_pr[0, b] = r["conv_p"]
        ckv_sm[0, 16 * c:16 * c + 16] = r["ckv_s"].reshape(16, 8, 256)
        kr_sm[0, 16 * c:16 * c + 16] = r["kr_s"].reshape(16, 8, 64)
        conv_sm[0, 16 * c:16 * c + 16] = r["conv_s"].reshape(16, 2, D)
    return (y_prompt, y_sample, ckv_pr, kr_pr, conv_pr, ckv_sm, kr_sm, conv_sm)
```

```python
import contextlib
import numpy as np
import concourse.bass as bass
import concourse.mybir as mybir
from concourse.bass_utils import run_bass_kernel_spmd

F32 = mybir.dt.float32
BF16 = mybir.dt.bfloat16
I32 = mybir.dt.int32
AF = mybir.ActivationFunctionType
ALU = mybir.AluOpType

PE, ACT, DVE, POOL, SP = "pe", "act", "dve", "pool", "sp"
COMPUTE = (PE, ACT, DVE, POOL)
NQ = 8

D = 1024
SEQ = 4096
NT = 32
NOWN = 16
NPOOL = 10240
EPS = 1e-6
SCALE = float((128 + 64) ** -0.5)
BIG = 30000.0
STAGE = 3
NTG = 2
NG = 128 * NTG


class Buf:
    __slots__ = ("name", "writers", "readers")

    def __init__(self, name=""):
        self.name = name
        self.writers = {}
        self.readers = {}


class Op:
    __slots__ = ("eng", "fn", "deps", "idx", "is_dma", "signal", "sem", "val", "qslot")

    def __init__(self, eng, fn, is_dma):
        self.eng = eng
        self.fn = fn
        self.is_dma = is_dma
        self.deps = []
        self.signal = False
        self.sem = None
        self.val = None
        self.qslot = 0


class Prog:
    def __init__(self, nc):
        self.nc = nc
        self.ops = {e: [] for e in (PE, ACT, DVE, POOL, SP)}
        self.ndma = {e: 0 for e in (PE, ACT, DVE, POOL, SP)}

    def _key(self, op):
        return ("dma", id(op)) if op.is_dma else op.eng

    def op(self, eng, meth, args, kwargs, reads=(), writes=(), dma=False):
        fn = (lambda e: getattr(e, meth)(*args, **kwargs))
        o = Op(eng, fn, dma)
        o.idx = len(self.ops[eng])
        deps = []
        for b in reads:
            for w in b.writers.values():
                deps.append(w)
        for b in writes:
            for w in b.writers.values():
                if w.is_dma or dma or w.eng != eng:
                    deps.append(w)
            for r in b.readers.values():
                if r.is_dma or dma or r.eng != eng:
                    deps.append(r)
        o.deps = [d for d in deps if not (d.eng == PE and eng == PE and not d.is_dma and not dma)]
        for b in reads:
            b.readers[self._key(o)] = o
        for b in writes:
            b.writers = {self._key(o): o}
            b.readers = {}
        if dma:
            o.qslot = self.ndma[eng]
            self.ndma[eng] += 1
        self.ops[eng].append(o)
        return o

    def pe(self, meth, *args, reads=(), writes=(), **kw):
        return self.op(PE, meth, args, kw, reads, writes)

    def act(self, meth, *args, reads=(), writes=(), **kw):
        return self.op(ACT, meth, args, kw, reads, writes)

    def dve(self, meth, *args, reads=(), writes=(), **kw):
        return self.op(DVE, meth, args, kw, reads, writes)

    def pool(self, meth, *args, reads=(), writes=(), **kw):
        return self.op(POOL, meth, args, kw, reads, writes)

    def dma(self, meth, *args, reads=(), writes=(), q=SP, **kw):
        return self.op(q, meth, args, kw, reads, writes, dma=True)

    def emit(self):
        nc = self.nc
        for e in self.ops:
            for o in self.ops[e]:
                for d in o.deps:
                    d.signal = True
        with contextlib.ExitStack() as st:
            csem = {e: st.enter_context(nc.semaphore("s_" + e)) for e in COMPUTE}
            qsem = {}
            for e in self.ops:
                if self.ndma[e]:
                    qsem[e] = [st.enter_context(nc.semaphore("q_%s_%d" % (e, i))) for i in range(NQ)]
            for e in self.ops:
                c = 0
                for o in self.ops[e]:
                    if o.is_dma:
                        o.sem = qsem[e][o.qslot % NQ]
                        o.val = 16 * (o.qslot // NQ + 1)
                    elif o.signal:
                        c += 1
                        o.sem, o.val = csem[e], c
            block = st.enter_context(nc.Block())
            handles = {PE: block.tensor, ACT: block.scalar, DVE: block.vector, POOL: block.gpsimd, SP: block.sync}

            def make(e):
                def body(eng):
                    known = {}
                    for o in self.ops[e]:
                        waits = {}
                        for d in o.deps:
                            kk = id(d.sem)
                            if kk not in waits or waits[kk][1] < d.val:
                                waits[kk] = (d.sem, d.val)
                        if o.is_dma and o.qslot >= NQ:
                            s = qsem[e][o.qslot % NQ]
                            v = 16 * (o.qslot // NQ)
                            kk = id(s)
                            if kk not in waits or waits[kk][1] < v:
                                waits[kk] = (s, v)
                        for kk, (s, v) in waits.items():
                            if known.get(kk, 0) >= v:
                                continue
                            eng.wait_ge(s, v)
                            known[kk] = v
                        inst = o.fn(eng)
                        if o.is_dma:
                            inst.then_inc(o.sem, 16)
                        elif o.signal:
                            inst.then_inc(o.sem, 1)
                    if e == SP:
                        for qe in qsem:
                            n = self.ndma[qe]
                            for slot in range(min(NQ, n)):
                                last = ((n - 1 - slot) // NQ) + 1
                                eng.wait_ge(qsem[qe][slot], 16 * last)
                return body

            for e in (SP, POOL, ACT, DVE, PE):
                if self.ops[e] or e == SP:
                    handles[e](make(e))


def build_program():
    nc = bass.Bass("TRN2", target_bir_lowering=False)
    P = Prog(nc)

    def din(name, shape, dt=F32):
        return nc.dram_tensor(name, list(shape), dt, kind="ExternalInput").ap()

    def dout(name, shape):
        return nc.dram_tensor(name, list(shape), F32, kind="ExternalOutput").ap()

    _n = [0]

    def sb(shape, dt=F32, name=None):
        _n[0] += 1
        return nc.alloc_sbuf_tensor(name or ("t%d" % _n[0]), list(shape), dt)

    xs = din("xs", [SEQ, D])
    xo = din("xo", [NOWN * 128, D])
    xh = din("xh", [32, D])
    xd = din("xd", [128, D])
    cc = din("cc", [17, D])
    pt = din("pt", [128, 8], I32)
    ckv_pool = din("ckv_pool", [NPOOL, 128 * 256])
    kr_pool = din("kr_pool", [NPOOL, 128 * 64])
    sconv = din("sconv", [32, D])
    w_ada = din("w_ada", [D, 6 * D])
    badaT = din("badaT", [128, 48])
    bada_g = din("bada_g", [2, D])
    gattnT = din("gattnT", [128, 8])
    gmlpT = din("gmlpT", [128, 8])
    gqT = din("gqT", [128, 3])
    gkv = din("gkv", [256])
    gfin = din("gfin", [D])
    w_kvin = din("w_kvin", [D, 384])
    w_main = din("w_main", [D, 5504])
    w_q = din("w_q", [384, 2048])
    w_ukT = din("w_ukT", [128, 8 * 256])
    w_uv = din("w_uv", [256, 8 * 128])
    convT = din("convT", [128, 8 * 3])
    w_o = din("w_o", [D, D])
    w_1 = din("w_1", [D, 4 * D])
    w_2 = din("w_2", [4 * D, D])
    ident_d = din("ident", [128, 128])
    masks_d = din("masks", [128, 2 * 128])
    hmask_d = din("hmask", [128, 16])
    cs_tok_p = din("cs_tok_p", [SEQ, 128])
    cs_tok_s = din("cs_tok_s", [128, 128])
    cs_feat_p = din("cs_feat_p", [64, 2 * NOWN * 128])
    cs_feat_s = din("cs_feat_s", [64, 2 * 128])
    smask_d = din("smask", [128, 8 * 128])
    augc_d = din("augc", [2, 2 * 128])
    kind_d = din("kind", [2, 128])

    y_p = dout("y_p", [NOWN * 128, D])
    y_s = dout("y_s", [128, D])
    ckv_p = dout("ckv_p", [SEQ, 256])
    kr_p = dout("kr_p", [SEQ, 64])
    conv_p = dout("conv_p", [2, D])
    ckv_s = dout("ckv_s", [128, 256])
    kr_s = dout("kr_s", [128, 64])
    conv_s = dout("conv_s", [32, D])

    ps = [nc.alloc_psum_tensor("ps%d" % i, [128, 512], F32) for i in range(8)]
    bps = [Buf("ps%d" % i) for i in range(8)]
    rr = {"g": 0}

    DEFB = [0, 1, 2, 3, 4, 5, 6, 7]

    def psum(group=DEFB, key="g"):
        i = group[rr.get(key, 0) % len(group)]
        rr[key] = rr.get(key, 0) + 1
        return ps[i], bps[i]

    def load(dst_ap, src_ap, buf, q=SP, reads=()):
        P.dma("dma_start", out=dst_ap, in_=src_ap, reads=list(reads), writes=[buf], q=q)

    ident_f = sb([128, 128]); b_ident_f = Buf()
    ident_b = sb([128, 128], BF16); b_ident_b = Buf()
    ones_b = sb([128, 128], BF16); b_ones = Buf()
    load(ident_f[:], ident_d, b_ident_f)
    P.dve("tensor_copy", ident_b[:], ident_f[:], reads=[b_ident_f], writes=[b_ident_b])
    P.dve("memset", ones_b[:], 1.0, writes=[b_ones])
    masks_f = sb([128, 256]); b_masks_f = Buf()
    masks = sb([128, 256], BF16); b_masks = Buf()
    load(masks_f[:], masks_d, b_masks_f)
    P.dve("tensor_copy", masks[:], masks_f[:], reads=[b_masks_f], writes=[b_masks])
    hmask = sb([128, 16]); b_hmask = Buf()
    load(hmask[:], hmask_d, b_hmask)
    gattn_t = sb([128, 8]); b_gattn = Buf(); load(gattn_t[:], gattnT, b_gattn)
    gmlp_t = sb([128, 8]); b_gmlp = Buf(); load(gmlp_t[:], gmlpT, b_gmlp)
    gq_t = sb([128, 3]); b_gq = Buf(); load(gq_t[:], gqT, b_gq)
    bada_t = sb([128, 48]); b_bada = Buf(); load(bada_t[:], badaT, b_bada)
    convw = sb([128, 24]); b_convw = Buf(); load(convw[:], convT, b_convw)
    gkv_bc = sb([128, 256]); b_gkv = Buf(); load(gkv_bc[:], gkv.partition_broadcast(128), b_gkv)
    gfin_bc = sb([128, D]); b_gfin = Buf(); load(gfin_bc[:], gfin.partition_broadcast(128), b_gfin)
    xn = [sb([128, D]) for _ in range(2)]; b_xn = [Buf(), Buf()]
    YO = xn; b_YO = b_xn
    XG = sb([128, NTG, D], name="XG"); b_XG = [Buf("XG%d" % t) for t in range(NTG)]
    UT = sb([128, 8, NG], BF16, name="UT"); b_UT = [Buf("UT%d" % t) for t in range(NTG)]
    REG = sb([128, 32, NG], BF16, name="REG"); b_REG = [Buf() for _ in range(32)]
    BAD = [XG[:, 0, :], XG[:, 1, :]]
    b_badag, b_badag1 = b_XG[0], b_XG[1]
    load(BAD[0], bada_g[0].partition_broadcast(128), b_badag)
    load(BAD[1], bada_g[1].partition_broadcast(128), b_badag1)

    wkv = sb([128, 8, 384], BF16); b_wkv = Buf()
    load(wkv[:], w_kvin.rearrange("(k p) c -> p k c", p=128), b_wkv, q=POOL)
    wq = sb([128, 3, 2048], BF16); b_wq = Buf()
    load(wq[:], w_q.rearrange("(k p) c -> p k c", p=128), b_wq, q=POOL)
    wuk = sb([128, 8, 256], BF16); b_wuk = Buf()
    load(wuk[:].rearrange("p h c -> p (h c)"), w_ukT, b_wuk, q=POOL)
    wuv = sb([128, 2, 8, 128], BF16); b_wuv = Buf()
    load(wuv[:].rearrange("p m h v -> p m (h v)"), w_uv.rearrange("(m p) c -> p m c", p=128), b_wuv, q=POOL)

    WUNIT = 1024
    NUNIT = 18
    wring = sb([128, NUNIT * WUNIT], BF16, name="wring")
    bunits = [Buf("wu%d" % i) for i in range(NUNIT)]
    wstate = {"pos": 0}

    def wload(src2d, kch, cols, reads=()):
        n = kch * cols
        nu = (n + WUNIT - 1) // WUNIT
        if wstate["pos"] + nu > NUNIT:
            wstate["pos"] = 0
        u0 = wstate["pos"]
        wstate["pos"] += nu
        bufs = bunits[u0:u0 + nu]
        view = wring[:, u0 * WUNIT:u0 * WUNIT + n].rearrange("p (k c) -> p k c", c=cols)
        P.dma("dma_start", out=view, in_=src2d.rearrange("(k p) c -> p k c", p=128),
              reads=list(reads), writes=bufs, q=POOL)
        return view, bufs

    w_main_b = nc.dram_tensor("w_main_b", [D, 5504], BF16, kind="Internal").ap()
    w_o_b = nc.dram_tensor("w_o_b", [D, D], BF16, kind="Internal").ap()
    w_1_b = nc.dram_tensor("w_1_b", [D, 4 * D], BF16, kind="Internal").ap()
    w_2_b = nc.dram_tensor("w_2_b", [4 * D, D], BF16, kind="Internal").ap()
    b_wconv = {"main": Buf(), "o": Buf(), "1": Buf(), "2": Buf()}

    def convert_weights():
        for (src, dst, key) in ((w_main, w_main_b, "main"), (w_o, w_o_b, "o"), (w_1, w_1_b, "1"), (w_2, w_2_b, "2")):
            rows, cols = src.shape
            for r0 in range(0, rows, 128):
                for c0 in range(0, cols, 2048):
                    c1 = min(cols, c0 + 2048)
                    P.dma("dma_start", out=dst[r0:r0 + 128, c0:c1], in_=src[r0:r0 + 128, c0:c1],
                          reads=[b_wconv[key]], writes=[b_wconv[key]], q=POOL)
                    yield None

    conv_gen = convert_weights()

    def convert_some(n):
        for _ in range(n):
            try:
                next(conv_gen)
            except StopIteration:
                return

    RA = sb([128, 3 * SEQ + NT * 258 + 8], BF16, name="RA")
    KT = RA[:, 0:3 * SEQ].rearrange("p (c t) -> p c t", c=3); b_KT = [Buf("KT%d" % t) for t in range(NT)]
    VP = RA[:, 3 * SEQ:3 * SEQ + NT * 258].rearrange("p (t f) -> p t f", f=258); b_VP = [Buf("VP%d" % t) for t in range(NT)]
    ra_off = [0]

    def carve(shape):
        n = 1
        for d_ in shape[1:]:
            n *= d_
        v = RA[:, ra_off[0]:ra_off[0] + n]
        ra_off[0] += n + (n % 2)
        assert ra_off[0] <= 3 * SEQ + NT * 258
        if len(shape) == 2:
            return v
        if len(shape) == 3:
            return v.rearrange("p (a b) -> p a b", b=shape[2])
        return v.rearrange("p (a b c) -> p a b c", b=shape[2], c=shape[3])

    def alias_barrier(new_bufs, old_bufs):
        merged = {}
        for ob in old_bufs:
            for dct in (ob.readers, ob.writers):
                for k, o in dct.items():
                    if k not in merged or merged[k].idx < o.idx:
                        merged[k] = o
        for nb in new_bufs:
            nb.readers.update(merged)
    b_KTaug = Buf()
    P.dve("memset", KT[64:65, 2, :], 1.0, writes=[b_KTaug])
    P.dve("memset", VP[:, :, 256:258], 1.0, writes=b_VP)

    modT = sb([128, 4, 8, 17]); b_modT = Buf()
    A1p = sb([128, 8]); B1p = sb([128, 8]); A2p = sb([128, 8]); B2p = sb([128, 8]); b_modp = Buf()
    A1s = sb([128, 8, 16]); A2s = sb([128, 8, 16]); b_mods = Buf()
    g1p = sb([128, D]); g2p = sb([128, D]); g1s = sb([128, D]); g2s = sb([128, D])
    b_g = {"g1p": Buf(), "g2p": Buf(), "g1s": Buf(), "g2s": Buf()}

    cct = xn[0][0:17, :]; b_cc = b_xn[0]; load(cct[:], cc, b_cc)
    sil = xn[1][0:17, :]; b_sil = b_xn[1]
    P.act("activation", out=sil[:], in_=cct[:], func=AF.Silu, reads=[b_cc], writes=[b_sil])
    silT = sb([128, 8, 17], BF16); b_silT = Buf()
    pt_, bp_ = psum()
    for k in range(8):
        P.pe("transpose", pt_[:, k * 17:(k + 1) * 17], sil[0:17, k * 128:(k + 1) * 128], ident_f[0:17, 0:17],
             reads=[b_sil, b_ident_f], writes=[bp_])
    P.dve("tensor_copy", silT[:].rearrange("p k s -> p (k s)"), pt_[:, 0:8 * 17], reads=[bp_], writes=[b_silT])
    nreg = 1024 // NG
    lhs_p = REG[:, 16:16 + nreg, :].rearrange("p a b -> p (a b)").rearrange("p (k c) -> p k c", c=128)
    lhs_s = REG[:, 16 + nreg:16 + 2 * nreg, :].rearrange("p a b -> p (a b)").rearrange("p (k c) -> p k c", c=128)
    b_lhs = Buf()
    b_lhs_all = [b_lhs] + b_REG[16:16 + 2 * nreg]
    P.dve("tensor_copy", lhs_p[:], silT[:, :, 0:1].to_broadcast([128, 8, 128]), reads=[b_silT], writes=b_lhs_all)
    for k in range(8):
        P.dve("tensor_copy", lhs_s[:, k, :].rearrange("p (s t) -> p s t", t=8),
                                           silT[:, k, 1:17].unsqueeze(2).to_broadcast([128, 16, 8]),
              reads=[b_silT, b_lhs], writes=b_lhs_all)

    for j in range(6):
        wv, wb = wload(w_ada[:, j * D:(j + 1) * D], 8, D)
        if j in (0, 1, 3, 4):
            jj = (0, 1, None, 2, 3)[j]
            po, bpo = psum()
            for m in range(8):
                for k in range(8):
                    P.pe("matmul", po[:, m * 17:(m + 1) * 17], lhsT=wv[:, k, m * 128:(m + 1) * 128],
                                                      rhs=silT[:, k, :], start=(k == 0), stop=(k == 7),
                         reads=wb + [b_silT], writes=[bpo])
            for m in range(8):
                P.dve("tensor_scalar", modT[:, jj, m, :], po[:, m * 17:(m + 1) * 17],
                                                                 bada_t[:, j * 8 + m:j * 8 + m + 1], None, op0=ALU.add,
                      reads=[bpo, b_bada, b_modT], writes=[b_modT])
        else:
            gi = 0 if j == 2 else 1
            for (lh, gt, bn) in ((lhs_p, g1p if gi == 0 else g2p, "g%dp" % (gi + 1)),
                                 (lhs_s, g1s if gi == 0 else g2s, "g%ds" % (gi + 1))):
                for n in range(2):
                    po, bpo = psum()
                    for k in range(8):
                        P.pe("matmul", po[:, :], lhsT=lh[:, k, :], rhs=wv[:, k, n * 512:(n + 1) * 512],
                                                                      start=(k == 0), stop=(k == 7),
                             reads=wb + b_lhs_all, writes=[bpo])
                    P.dve("tensor_tensor", out=gt[:, n * 512:(n + 1) * 512], in0=po[:, :],
                                                                              in1=BAD[gi][:, n * 512:(n + 1) * 512], op=ALU.add,
                          reads=[bpo, b_badag, b_badag1, b_g[bn]], writes=[b_g[bn]])
    convert_some(8)
    P.dve("scalar_tensor_tensor", out=A1p[:], in0=modT[:, 1, :, 0], scalar=1.0, in1=gattn_t[:], op0=ALU.add, op1=ALU.mult,
          reads=[b_modT, b_gattn], writes=[b_modp])
    P.dve("scalar_tensor_tensor", out=A2p[:], in0=modT[:, 3, :, 0], scalar=1.0, in1=gmlp_t[:], op0=ALU.add, op1=ALU.mult,
          reads=[b_modT, b_gmlp, b_modp], writes=[b_modp])
    P.dve("tensor_copy", B1p[:], modT[:, 0, :, 0], reads=[b_modT, b_modp], writes=[b_modp])
    P.dve("tensor_copy", B2p[:], modT[:, 2, :, 0], reads=[b_modT, b_modp], writes=[b_modp])
    P.dve("scalar_tensor_tensor", out=A1s[:], in0=modT[:, 1, :, 1:17], scalar=1.0,
                                           in1=gattn_t[:].unsqueeze(2).to_broadcast([128, 8, 16]), op0=ALU.add, op1=ALU.mult,
          reads=[b_modT, b_gattn], writes=[b_mods])
    P.dve("scalar_tensor_tensor", out=A2s[:], in0=modT[:, 3, :, 1:17], scalar=1.0,
                                           in1=gmlp_t[:].unsqueeze(2).to_broadcast([128, 8, 16]), op0=ALU.add, op1=ALU.mult,
          reads=[b_modT, b_gmlp, b_mods], writes=[b_mods])

    nj = D // NG
    JUNK = {"g": (REG[:, 0:nj, :].rearrange("p a b -> p (a b)"), b_REG[0:nj]),
            "k": (REG[:, 24:24 + nj, :].rearrange("p a b -> p (a b)"), b_REG[24:24 + nj])}
    TMP = [sb([128, 512]) for _ in range(2)]; b_TMP = [Buf(), Buf()]
    st_ss = sb([128, 8]); st_rs = sb([128, 8]); b_st = [Buf() for _ in range(8)]
    stc = {"n": 0}

    def rstd_of(src_ap, src_bufs, npart, ncols, inv_n, jsel="g"):
        i = stc["n"] % 8
        stc["n"] += 1
        junk, jb = JUNK[jsel]
        P.act("activation", out=junk[0:npart, 0:ncols], in_=src_ap, func=AF.Square, accum_out=st_ss[0:npart, i:i + 1],
              reads=list(src_bufs), writes=list(jb) + [b_st[i]])
        P.act("activation", out=st_rs[0:npart, i:i + 1], in_=st_ss[0:npart, i:i + 1], func=AF.Ln, scale=inv_n, bias=EPS,
              reads=[b_st[i]], writes=[b_st[i]])
        P.act("activation", out=st_rs[0:npart, i:i + 1], in_=st_rs[0:npart, i:i + 1], func=AF.Exp, scale=-0.5,
              reads=[b_st[i]], writes=[b_st[i]])
        return st_rs[0:npart, i:i + 1], b_st[i]

    xnc = {"n": 0}

    def make_uT(x_ap, x_bufs, npart, uT, uT_buf, col0, mod, jsel="g"):
        rs, rb = rstd_of(x_ap, x_bufs, npart, D, 1.0 / D, jsel)
        i = xnc["n"] % 2
        xnc["n"] += 1
        xnb = xn[i][:, :].bitcast(BF16)
        P.dve("tensor_scalar", xnb[0:npart, 0:D], x_ap, rs, None, op0=ALU.mult,
              reads=list(x_bufs) + [rb], writes=[b_xn[i]])
        for half in range(2):
            po_, bpo = psum()
            po = po_[:, :].bitcast(BF16)
            for kk in range(4):
                k = half * 4 + kk
                P.pe("transpose", po[:, kk * 128:kk * 128 + npart], xnb[0:npart, k * 128:(k + 1) * 128],
                                                              ident_b[0:npart, 0:npart],
                     reads=[b_xn[i], b_ident_b], writes=[bpo])
            for kk in range(4):
                k = half * 4 + kk
                if mod[0] == "p":
                    P.dve("tensor_scalar", uT[:, k, col0:col0 + npart], po[:, kk * 128:kk * 128 + npart],
                                                                       mod[1][:, k:k + 1], mod[2][:, k:k + 1], op0=ALU.mult, op1=ALU.add,
                          reads=[bpo, mod[3]], writes=[uT_buf])
                else:
                    tmp = sb_tmp_s
                    P.dve("tensor_tensor", out=tmp[:].rearrange("p (s t) -> p s t", t=8),
                                                                       in0=po[:, kk * 128:(kk + 1) * 128].rearrange("p (s t) -> p s t", t=8),
                                                                       in1=mod[1][:, k, :].unsqueeze(2).to_broadcast([128, 16, 8]), op=ALU.mult,
                          reads=[bpo, mod[3]], writes=[b_tmp_s])
                    P.dve("tensor_tensor", out=uT[:, k, col0:col0 + 128].rearrange("p (s t) -> p s t", t=8),
                                                         in0=tmp[:].rearrange("p (s t) -> p s t", t=8),
                                                         in1=mod[2][:, k, :].unsqueeze(2).to_broadcast([128, 16, 8]), op=ALU.add,
                          reads=[b_tmp_s, mod[3]], writes=[uT_buf])

    sb_tmp_s = TMP[0][:, 0:128]; b_tmp_s = b_TMP[0]

    kvf = [sb([128, 320]) for _ in range(2)]; b_kvf = [Buf(), Buf()]
    krb = [sb([128, 64], BF16) for _ in range(2)]; b_krb = [Buf(), Buf()]
    cst = [sb([128, 128]) for _ in range(2)]; b_cst = [Buf(), Buf()]
    krt = [sb([128, 128]) for _ in range(2)]; b_krt = [Buf(), Buf()]
    kvc = {"n": 0}

    def kv_build(uT, uT_buf, col0, cs_src, out_ckv, out_kr, vdst, vbuf, ktdst, ktbuf, split=False):
        i = kvc["n"] % 2
        kvc["n"] += 1
        load(cst[i][:], cs_src, b_cst[i])
        po, bpo = psum()
        for k in range(8):
            P.pe("matmul", po[:, 0:384], lhsT=uT[:, k, col0:col0 + 128], rhs=wkv[:, k, :], start=(k == 0), stop=(k == 7),
                 reads=[uT_buf, b_wkv], writes=[bpo])
        rs, rb = rstd_of(po[:, 0:256], [bpo], 128, 256, 1.0 / 256, "k")
        P.dve("scalar_tensor_tensor", out=kvf[i][:, 0:256], in0=po[:, 0:256], scalar=rs, in1=gkv_bc[:], op0=ALU.mult, op1=ALU.mult,
              reads=[bpo, rb, b_gkv], writes=[b_kvf[i]])
        P.dve("tensor_tensor", out=krt[i][:], in0=po[:, 256:384], in1=cst[i][:], op=ALU.mult,
              reads=[bpo, b_cst[i]], writes=[b_krt[i]])
        P.dve("tensor_tensor", out=kvf[i][:, 256:320], in0=krt[i][:, 0:64], in1=krt[i][:, 64:128], op=ALU.add,
              reads=[b_krt[i], b_kvf[i]], writes=[b_kvf[i]])
        P.dma("dma_start", out=out_ckv, in_=kvf[i][:, 0:256], reads=[b_kvf[i]])
        P.dma("dma_start", out=out_kr, in_=kvf[i][:, 256:320], reads=[b_kvf[i]])
        if split:
            return lambda: kv_build2(i, vdst, vbuf, ktdst, ktbuf)
        kv_build2(i, vdst, vbuf, ktdst, ktbuf)

    def kv_build2(i, vdst, vbuf, ktdst, ktbuf):
        P.act("activation", out=vdst, in_=kvf[i][:, 0:256], func=AF.Copy, reads=[b_kvf[i]], writes=[vbuf])
        P.act("activation", out=krb[i][:], in_=kvf[i][:, 256:320], func=AF.Copy, reads=[b_kvf[i]], writes=[b_krb[i]])
        pb, bpb = psum()
        pbb = pb[:, :].bitcast(BF16)
        P.pe("transpose", pbb[:, 0:128], vdst[:, 0:128], ident_b[:], reads=[vbuf, b_ident_b], writes=[bpb])
        P.pe("transpose", pbb[:, 128:256], vdst[:, 128:256], ident_b[:], reads=[vbuf, b_ident_b], writes=[bpb])
        P.pe("transpose", pbb[0:64, 256:384], krb[i][:], ident_b[:], reads=[b_krb[i], b_ident_b], writes=[bpb])
        P.act("activation", out=ktdst[:, 0:2, :], in_=pbb[:, 0:256].rearrange("p (c t) -> p c t", t=128), func=AF.Copy,
              reads=[bpb], writes=[ktbuf])
        P.act("activation", out=ktdst[0:64, 2, :], in_=pbb[0:64, 256:384], func=AF.Copy, reads=[bpb, ktbuf], writes=[ktbuf])

    uK = [UT[:, :, 0:128], UT[:, :, 128:256]]; b_uK = [b_UT[0], b_UT[1]]
    xk = [XG[:, 0, :], XG[:, 1, :]]; b_xk = [b_XG[0], b_XG[1]]
    modp = ("p", A1p, B1p, b_modp)
    NPRE = NT

    def stage_a(T):
        i = T % 2
        make_uT(xk[i], [b_xk[i]], 128, uK[i], b_uK[i], 0, modp, "k")


    load(xk[0], xs[0:128, :], b_xk[0])
    load(xk[1], xs[128:256, :], b_xk[1])
    stage_a(0)
    for T in range(NPRE):
        i = T % 2
        if T + 2 < NPRE:
            load(xk[i], xs[(T + 2) * 128:(T + 3) * 128, :], b_xk[i])
        part2 = kv_build(uK[i], b_uK[i], 0, cs_tok_p[T * 128:(T + 1) * 128, :], ckv_p[T * 128:(T + 1) * 128, :], kr_p[T * 128:(T + 1) * 128, :],
                         VP[:, T, 0:256], b_VP[T], KT[:, :, T * 128:(T + 1) * 128], b_KT[T], split=True)
        if T + 1 < NPRE:
            stage_a(T + 1)
        part2()
        convert_some(3)
    convert_some(1000)
    uH = sb([128, 8, 32], BF16); b_uH = Buf()
    load(xk[0][0:32, :], xh, b_xk[0])
    make_uT(xk[0][0:32, :], [b_xk[0]], 32, uH, b_uH, 0, modp, "k")

    GA = REG[:, 0:8, :]; b_GA = b_REG[0:8]
    GB = REG[:, 8:16, :]; b_GB = b_REG[8:16]
    MG = REG[:, 16:24, :]; b_MG = b_REG[16:24]
    HT = REG; b_HT = b_REG
    QA = sb([128, 3, NG], name="QA"); b_QA = Buf()
    SQ = sb([128, 3, NG], BF16, name="SQ"); b_SQ = Buf()
    RQ = sb([128, NG], name="RQ"); b_RQ = Buf()
    QN = sb([128, 3, NG], BF16, name="QN"); b_QN = Buf()
    QNOPE = [sb([128, NG], BF16) for _ in range(2)]; b_QNOPE = [Buf(), Buf()]
    QT = [sb([128, 3, NG], BF16) for _ in range(2)]; b_QT = [Buf(), Buf()]
    CSQ = sb([64, 2, NG], name="CSQ"); b_CSQ = Buf()
    RT = [sb([64, NG]) for _ in range(2)]; b_RT = [Buf(), Buf()]
    PT = [sb([128, NG], BF16) for _ in range(3)]; b_PT = [Buf() for _ in range(3)]
    OL = sb([128, NTG, 256], BF16, name="OL"); b_OL = [Buf() for _ in range(NTG)]
    OLT = sb([128, 2, NG], BF16, name="OLT"); b_OLT = Buf()
    RINV = sb([128, 4]); b_RINV = [Buf() for _ in range(4)]
    CG = [sb([128, NG + 8]) for _ in range(2)]; b_CG = [Buf(), Buf()]
    VPAD = [sb([128, max(NTG * 130, 160)]) for _ in range(2)]; b_VPAD = [Buf(), Buf()]
    ZC = [sb([128, NG]) for _ in range(2)]; b_ZC = [Buf(), Buf()]
    SG = [sb([128, NG]) for _ in range(2)]; b_SG = [Buf(), Buf()]
    VLAST = sb([128, 8, 32], name="VLAST"); b_VLAST = Buf()
    cnt = {"cg": 0, "tmp": 0, "pt": 0, "yo": 0, "h": 0}

    def group(N, ntile, x_src, mod1, mod2, g1t, bg1, g2t, bg2, csq_src, y_dst, attention, sample, preloaded=False, x_next=None, n_next=0):
        NW = N
        for t in range(ntile):
            if not preloaded:
                load(XG[:, t, :], x_src[t * 128:(t + 1) * 128, :], b_XG[t])
            make_uT(XG[:, t, :], [b_XG[t]], 128, UT, b_UT[t], t * 128, mod1)
        load(CSQ[:, :, 0:N], csq_src, b_CSQ)
        wv, wb = wload(w_main_b[:, 0:384], 8, 384, [b_wconv["main"]])
        for c in range(3):
            po, bpo = psum()
            for k in range(8):
                P.pe("matmul", po[:, 0:N], lhsT=wv[:, k, c * 128:(c + 1) * 128], rhs=UT[:, k, 0:N],
                                                        start=(k == 0), stop=(k == 7), reads=wb + b_UT[0:ntile], writes=[bpo])
            P.act("activation", out=QA[:, c, 0:N], in_=po[:, 0:N], func=AF.Copy, reads=[bpo, b_QA], writes=[b_QA])
            P.act("activation", out=SQ[:, c, 0:N], in_=po[:, 0:N], func=AF.Square, reads=[bpo, b_SQ], writes=[b_SQ])
        po, bpo = psum()
        for c in range(3):
            P.pe("matmul", po[:, 0:N], lhsT=ones_b[:], rhs=SQ[:, c, 0:N], start=(c == 0), stop=(c == 2),
                 reads=[b_ones, b_SQ], writes=[bpo])
        P.act("activation", out=RQ[:, 0:N], in_=po[:, 0:N], func=AF.Ln, scale=1.0 / 384, bias=EPS, reads=[bpo], writes=[b_RQ])
        P.act("activation", out=RQ[:, 0:N], in_=RQ[:, 0:N], func=AF.Exp, scale=-0.5, reads=[b_RQ], writes=[b_RQ])
        for c in range(3):
            P.dve("scalar_tensor_tensor", out=QN[:, c, 0:N], in0=QA[:, c, 0:N], scalar=gq_t[:, c:c + 1], in1=RQ[:, 0:N],
                                                        op0=ALU.mult, op1=ALU.mult, reads=[b_QA, b_gq, b_RQ, b_QN], writes=[b_QN])
        for fc in range(8):
            wv, wb = wload(w_main_b[:, 384 + fc * 640:384 + (fc + 1) * 640], 8, 640, [b_wconv["main"]])

            def proj(mc, po, bpo, rhs, ncols, c0=0):
                for k in range(8):
                    P.pe("matmul", po[:, c0:c0 + ncols], lhsT=wv[:, k, mc * 128:(mc + 1) * 128], rhs=rhs(k),
                                                 start=(k == 0), stop=(k == 7), reads=wb + [b_uH] + b_UT[0:ntile], writes=[bpo])

            ci = cnt["cg"] % 2
            cnt["cg"] += 1
            pcg, bpcg = psum()
            proj(1, pcg, bpcg, lambda k: UT[:, k, 0:N], N)
            P.act("activation", out=CG[ci][:, 0:N], in_=pcg[:, 0:N], func=AF.Copy, reads=[bpcg], writes=[b_CG[ci]])
            pxi, bpxi = psum()
            proj(2, pxi, bpxi, lambda k: UT[:, k, 0:N], N)
            vp = VPAD[ci][:, 0:ntile * 130].rearrange("p (t j) -> p t j", j=130)
            if not sample:
                ph, bph = psum()
                nh = 2 * ntile
                proj(1, ph, bph, lambda k: uH[:, k, cnt["g0"] * 2:cnt["g0"] * 2 + nh], nh, 0)
                proj(2, ph, bph, lambda k: uH[:, k, cnt["g0"] * 2:cnt["g0"] * 2 + nh], nh, 16)
                P.act("activation", out=CG[ci][:, NG:NG + nh], in_=ph[:, 0:nh], func=AF.Copy, reads=[bph, b_CG[ci]], writes=[b_CG[ci]])
                P.dve("tensor_tensor", out=vp[:, 0:ntile, 2:130], in0=pxi[:, 0:N].rearrange("p (t j) -> p t j", j=128),
                                                in1=CG[ci][:, 0:N].rearrange("p (t j) -> p t j", j=128), op=ALU.mult,
                      reads=[bpxi, b_CG[ci]], writes=[b_VPAD[ci]])
                P.dve("tensor_tensor", out=vp[:, 0:ntile, 0:2], in0=ph[:, 16:16 + nh].rearrange("p (t j) -> p t j", j=2),
                                                in1=CG[ci][:, NG:NG + nh].rearrange("p (t j) -> p t j", j=2), op=ALU.mult,
                      reads=[bph, b_CG[ci], b_VPAD[ci]], writes=[b_VPAD[ci]])
                g0 = cnt["g0"]
                P.dve("tensor_tensor", out=vp[:, 0:ntile, 0:2], in0=vp[:, 0:ntile, 0:2],
                                                in1=hmask[:, g0:g0 + ntile].unsqueeze(2).to_broadcast([128, ntile, 2]), op=ALU.mult,
                      reads=[b_VPAD[ci], b_hmask], writes=[b_VPAD[ci]])
                vin = [vp[:, 0:ntile, j:j + 128] for j in range(3)]
                zv = ZC[ci][:, 0:N].rearrange("p (t j) -> p t j", j=128)
                if cnt["g0"] + ntile == NOWN:
                    P.dve("tensor_copy", VLAST[:, fc, 0:2], vp[:, ntile - 1, 128:130], reads=[b_VPAD[ci], b_VLAST], writes=[b_VLAST])
            else:
                vps = VPAD[ci][:, 0:160].rearrange("p (s j) -> p s j", j=10)
                P.dve("tensor_tensor", out=vps[:, :, 2:10], in0=pxi[:, 0:128].rearrange("p (s j) -> p s j", j=8),
                                                in1=CG[ci][:, 0:128].rearrange("p (s j) -> p s j", j=8), op=ALU.mult,
                      reads=[bpxi, b_CG[ci]], writes=[b_VPAD[ci]])
                P.dve("tensor_copy", vps[:, :, 0:2], SCT[:, fc, :].rearrange("p (s j) -> p s j", j=2),
                      reads=[b_SCT, b_VPAD[ci]], writes=[b_VPAD[ci]])
                vin = [vps[:, :, j:j + 8] for j in range(3)]
                zv = ZC[ci][:, 0:128].rearrange("p (s j) -> p s j", j=8)
                P.dve("tensor_copy", VLAST[:, fc, :].rearrange("p (s j) -> p s j", j=2), vps[:, :, 8:10],
                      reads=[b_VPAD[ci], b_VLAST], writes=[b_VLAST])
            P.dve("tensor_scalar", zv, vin[0], convw[:, fc * 3:fc * 3 + 1], None, op0=ALU.mult,
                  reads=[b_VPAD[ci], b_convw], writes=[b_ZC[ci]])
            for j in (1, 2):
                P.dve("scalar_tensor_tensor", out=zv, in0=vin[j], scalar=convw[:, fc * 3 + j:fc * 3 + j + 1], in1=zv,
                                                            op0=ALU.mult, op1=ALU.add, reads=[b_VPAD[ci], b_convw, b_ZC[ci]], writes=[b_ZC[ci]])
            pbg, bpbg = psum()
            proj(0, pbg, bpbg, lambda k: UT[:, k, 0:N], N)
            P.dve("tensor_tensor", out=ZC[ci][:, 0:N], in0=pbg[:, 0:N], in1=ZC[ci][:, 0:N], op=ALU.mult,
                  reads=[bpbg, b_ZC[ci]], writes=[b_ZC[ci]])
            pga, bpga = psum()
            proj(3, pga, bpga, lambda k: UT[:, k, 0:N], N)
            P.act("activation", out=GA[:, fc, 0:N], in_=pga[:, 0:N], func=AF.Sigmoid, reads=[bpga], writes=[b_GA[fc]])
            pgb, bpgb = psum()
            proj(4, pgb, bpgb, lambda k: UT[:, k, 0:N], N)
            P.act("activation", out=SG[ci][:, 0:N], in_=pgb[:, 0:N], func=AF.Sigmoid, reads=[bpgb], writes=[b_SG[ci]])
            P.dve("tensor_tensor", out=GB[:, fc, 0:N], in0=SG[ci][:, 0:N], in1=ZC[ci][:, 0:N], op=ALU.mult,
                  reads=[b_SG[ci], b_ZC[ci]], writes=[b_GB[fc]])

        attention(N)

        wv, wb = wload(w_o_b, 8, D, [b_wconv["o"]])
        for t in range(ntile):
            for n in range(2):
                po, bpo = psum()
                for k in range(8):
                    P.pe("matmul", po[:, :], lhsT=MG[:, k, t * 128:(t + 1) * 128], rhs=wv[:, k, n * 512:(n + 1) * 512],
                                                                 start=(k == 0), stop=(k == 7), reads=wb + [b_MG[k]], writes=[bpo])
                i = cnt["tmp"] % 2
                cnt["tmp"] += 1
                P.dve("tensor_tensor", out=TMP[i][:], in0=po[:, :], in1=g1t[:, n * 512:(n + 1) * 512], op=ALU.mult,
                      reads=[bpo, bg1], writes=[b_TMP[i]])
                P.dve("tensor_tensor", out=XG[:, t, n * 512:(n + 1) * 512], in0=XG[:, t, n * 512:(n + 1) * 512],
                                                               in1=TMP[i][:], op=ALU.add, reads=[b_TMP[i], b_XG[t]], writes=[b_XG[t]])
        for t in range(ntile):
            make_uT(XG[:, t, :], [b_XG[t]], 128, UT, b_UT[t], t * 128, mod2)
        for j2 in range(8):
            wv, wb = wload(w_1_b[:, j2 * 512:(j2 + 1) * 512], 8, 512, [b_wconv["1"]])
            for m in range(4):
                po, bpo = psum()
                for k in range(8):
                    P.pe("matmul", po[:, 0:N], lhsT=wv[:, k, m * 128:(m + 1) * 128], rhs=UT[:, k, 0:N],
                                                            start=(k == 0), stop=(k == 7), reads=wb + b_UT[0:ntile], writes=[bpo])
                hc = j2 * 4 + m
                P.act("activation", out=TMP[hc % 2][:, 0:N], in_=po[:, 0:N], func=AF.Relu,
                      reads=[bpo], writes=[b_TMP[hc % 2]])
                P.dve("tensor_tensor", out=HT[:, hc, 0:N], in0=TMP[hc % 2][:, 0:N], in1=TMP[hc % 2][:, 0:N], op=ALU.mult,
                      reads=[b_TMP[hc % 2]], writes=[b_HT[hc]])
        for j in range(4):
            wva, wba = wload(w_2_b[0:2048, j * 256:(j + 1) * 256], 16, 256, [b_wconv["2"]])
            wvb, wbb = wload(w_2_b[2048:4096, j * 256:(j + 1) * 256], 16, 256, [b_wconv["2"]])
            for t in range(ntile):
                po, bpo = psum()
                for k in range(32):
                    wv, wb = (wva, wba) if k < 16 else (wvb, wbb)
                    P.pe("matmul", po[:, 0:256], lhsT=HT[:, k, t * 128:(t + 1) * 128], rhs=wv[:, k % 16, :],
                         start=(k == 0), stop=(k == 31), reads=wb + [b_HT[k]], writes=[bpo])
                i2 = cnt["tmp"] % 2
                cnt["tmp"] += 1
                P.dve("tensor_tensor", out=TMP[i2][:, 0:256], in0=po[:, 0:256], in1=g2t[:, j * 256:(j + 1) * 256], op=ALU.mult,
                      reads=[bpo, bg2], writes=[b_TMP[i2]])
                P.dve("tensor_tensor", out=XG[:, t, j * 256:(j + 1) * 256], in0=XG[:, t, j * 256:(j + 1) * 256],
                                                                 in1=TMP[i2][:, 0:256], op=ALU.add, reads=[b_TMP[i2], b_XG[t]], writes=[b_XG[t]])
        for t in range(ntile):
            rs, rb = rstd_of(XG[:, t, :], [b_XG[t]], 128, D, 1.0 / D)
            i = cnt["yo"] % 2
            cnt["yo"] += 1
            P.dve("scalar_tensor_tensor", out=YO[i][:], in0=XG[:, t, :], scalar=rs, in1=gfin_bc[:], op0=ALU.mult, op1=ALU.mult,
                  reads=[b_XG[t], rb, b_gfin], writes=[b_YO[i]])
            if x_next is not None and t < n_next:
                load(XG[:, t, :], x_next[t * 128:(t + 1) * 128, :], b_XG[t])
            P.dma("dma_start", out=y_dst[t * 128:(t + 1) * 128, :], in_=YO[i][:], reads=[b_YO[i]])

    def head_q(h, N, ktcol_lhsT):
        qi = h % 2
        po, bpo = psum()
        for c in range(3):
            P.pe("matmul", po[:, 0:N], lhsT=wq[:, c, h * 256:h * 256 + 128], rhs=QN[:, c, 0:N], start=(c == 0), stop=(c == 2),
                 reads=[b_wq, b_QN], writes=[bpo])
        P.act("activation", out=QNOPE[qi][:, 0:N], in_=po[:, 0:N], func=AF.Copy, reads=[bpo], writes=[b_QNOPE[qi]])
        pr, bpr = psum()
        for c in range(3):
            P.pe("matmul", pr[0:64, 0:N], lhsT=wq[:, c, h * 256 + 128:h * 256 + 192], rhs=QN[:, c, 0:N], start=(c == 0), stop=(c == 2),
                 reads=[b_wq, b_QN], writes=[bpr])
        pw, bpw = psum()
        for c in range(3):
            P.pe("matmul", pw[0:64, 0:N], lhsT=wq[:, c, h * 256 + 192:h * 256 + 256], rhs=QN[:, c, 0:N], start=(c == 0), stop=(c == 2),
                 reads=[b_wq, b_QN], writes=[bpw])
        P.dve("tensor_tensor", out=RT[0][:, 0:N], in0=pr[0:64, 0:N], in1=CSQ[:, 0, 0:N], op=ALU.mult, reads=[bpr, b_CSQ], writes=[b_RT[0]])
        P.dve("tensor_tensor", out=RT[1][:, 0:N], in0=pw[0:64, 0:N], in1=CSQ[:, 1, 0:N], op=ALU.mult, reads=[bpw, b_CSQ], writes=[b_RT[1]])
        P.dve("tensor_tensor", out=QT[qi][0:64, 2, 0:N], in0=RT[0][:, 0:N], in1=RT[1][:, 0:N], op=ALU.add,
              reads=[b_RT[0], b_RT[1], b_QT[qi]], writes=[b_QT[qi]])
        for m in range(2):
            pl, bpl = psum()
            P.pe("matmul", pl[:, 0:N], lhsT=wuk[:, h, m * 128:(m + 1) * 128], rhs=QNOPE[qi][:, 0:N], start=True, stop=True,
                 reads=[b_wuk, b_QNOPE[qi]], writes=[bpl])
            P.act("activation", out=QT[qi][:, m, 0:N], in_=pl[:, 0:N], func=AF.Copy, reads=[bpl, b_QT[qi]], writes=[b_QT[qi]])
        return qi

    def merge_head(h, N, pa, bpa):
        i = cnt["tmp"] % 2
        cnt["tmp"] += 1
        P.dve("tensor_tensor", out=TMP[i][:, 0:N], in0=pa[:, 0:N], in1=GA[:, h, 0:N], op=ALU.mult, reads=[bpa, b_GA[h]], writes=[b_TMP[i]])
        P.dve("tensor_tensor", out=MG[:, h, 0:N], in0=TMP[i][:, 0:N], in1=GB[:, h, 0:N], op=ALU.add, reads=[b_TMP[i], b_GB[h]], writes=[b_MG[h]])

    SBANK = (0, 1)
    OBANK = (2, 3, 4, 5)
    GBANK = [6, 7]

    def prompt_attention_factory(i0, ntile):
        def attention(N):
            nk = 2 * (i0 + ntile)
            g8 = 2 * i0
            DEFB[:] = [6, 7]
            GBANK[:] = [6, 7]

            def prep(h):
                qi = head_q(h, N, None)
                pm, bpm = psum(GBANK, "gb")
                for c in range(3):
                    kc = 128 if c < 2 else 64
                    P.pe("matmul", pm[0:1, 0:N], lhsT=KT[0:kc, c, 0:1], rhs=QT[qi][0:kc, c, 0:N], start=(c == 0), stop=(c == 2),
                         reads=[b_KT[0], b_QT[qi]], writes=[bpm])
                P.act("activation", out=QT[qi][64:65, 2, 0:N], in_=pm[0:1, 0:N], func=AF.Copy, scale=-1.0, reads=[bpm, b_QT[qi]], writes=[b_QT[qi]])
                return qi

            qis = {0: prep(0)}
            deferred = []

            for h in range(8):
                if h + 1 < 8:
                    qis[h + 1] = prep(h + 1)
                qi = qis[h]
                obk = OBANK[0:ntile] if (ntile > 2 or h % 2 == 0) else OBANK[2:2 + ntile]

                def stage_S(kt):
                    j = max(0, (kt - g8) // 2)
                    q0 = j * 128
                    e_ = (kt - g8) % 2 if kt >= g8 else None
                    pS, bpS = psum(SBANK, "sb")
                    for c in range(3):
                        kc = 128 if c < 2 else 65
                        P.pe("matmul", pS[:, q0:N], lhsT=KT[0:kc, c, kt * 128:(kt + 1) * 128], rhs=QT[qi][0:kc, c, q0:N],
                             start=(c == 0), stop=(c == 2), reads=[b_KT[kt], b_KTaug, b_QT[qi]], writes=[bpS])
                    pi = cnt["pt"] % 3
                    cnt["pt"] += 1
                    P.act("activation", out=PT[pi][:, q0:N], in_=pS[:, q0:N], func=AF.Exp, scale=SCALE, reads=[bpS], writes=[b_PT[pi]])
                    if e_ is not None:
                        P.dve("tensor_tensor", out=PT[pi][:, q0:q0 + 128], in0=PT[pi][:, q0:q0 + 128],
                              in1=masks[:, e_ * 128:(e_ + 1) * 128], op=ALU.mult, reads=[b_PT[pi], b_masks], writes=[b_PT[pi]])
                    return (kt, j, pi)

                def stage_V(kt, j, pi):
                    for jq in range(j, ntile):
                        last = g8 + 2 * jq + 1
                        P.pe("matmul", ps[obk[jq]][:, 0:257], lhsT=PT[pi][:, jq * 128:(jq + 1) * 128], rhs=VP[:, kt, 0:257],
                             start=(kt == 0), stop=(kt == last), reads=[b_PT[pi], b_VP[kt]], writes=[bps[obk[jq]]])

                pending = None
                for kt in range(nk):
                    info = stage_S(kt)
                    if pending is not None:
                        stage_V(*pending)
                    pending = info
                    if kt == min(2, nk - 1) and deferred:
                        deferred.pop()()
                stage_V(*pending)

                def epilogue(h=h, obk=obk):
                    for jq in range(ntile):
                        ob = ps[obk[jq]]; bob = bps[obk[jq]]
                        P.dve("reciprocal", RINV[:, jq:jq + 1], ob[:, 256:257], reads=[bob], writes=[b_RINV[jq]])
                        P.act("activation", out=OL[:, jq, :], in_=ob[:, 0:256], func=AF.Copy, scale=RINV[:, jq:jq + 1],
                              reads=[bob, b_RINV[jq]], writes=[b_OL[jq]])
                    pb, bpb = psum(GBANK, "gb")
                    pbb = pb[:, :].bitcast(BF16)
                    for jq in range(ntile):
                        for m in range(2):
                            P.pe("transpose", pbb[:, m * N + jq * 128:m * N + (jq + 1) * 128], OL[:, jq, m * 128:(m + 1) * 128], ident_b[:],
                                 reads=[b_OL[jq], b_ident_b], writes=[bpb])
                    P.act("activation", out=OLT[:].rearrange("p m q -> p (m q)"), in_=pbb[:, 0:2 * N], func=AF.Copy, reads=[bpb], writes=[b_OLT])
                    pa, bpa = psum(GBANK, "gb")
                    for m in range(2):
                        P.pe("matmul", pa[:, 0:N], lhsT=wuv[:, m, h, :], rhs=OLT[:, m, 0:N], start=(m == 0), stop=(m == 1),
                             reads=[b_wuv, b_OLT], writes=[bpa])
                    merge_head(h, N, pa, bpa)

                if ntile <= 2:
                    deferred.append(epilogue)
                else:
                    epilogue()
            while deferred:
                deferred.pop()()
            DEFB[:] = [0, 1, 2, 3, 4, 5, 6, 7]
        return attention

    if STAGE >= 2:
        for g in range(NOWN // NTG):
            cnt["g0"] = NTG * g
            last_g = (g == NOWN // NTG - 1)
            group(NG, NTG, xo[g * NG:(g + 1) * NG, :], modp, ("p", A2p, B2p, b_modp), g1p, b_g["g1p"], g2p, b_g["g2p"],
                  cs_feat_p.rearrange("p (a t) -> p a t", a=2)[:, :, g * NG:(g + 1) * NG], y_p[g * NG:(g + 1) * NG, :],
                  prompt_attention_factory(NTG * g, NTG), False, preloaded=(g > 0),
                  x_next=(xd if last_g else xo[(g + 1) * NG:(g + 2) * NG, :]) if STAGE >= 3 or not last_g else None,
                  n_next=(1 if last_g else NTG))
        for c8 in range(8):
            P.dma("dma_start", out=conv_p[:, c8 * 128:(c8 + 1) * 128].rearrange("t p -> p t"), in_=VLAST[:, c8, 0:2],
                  allow_slow_non_contiguous=True, reads=[b_VLAST])

    SCT = sb([128, 8, 32], name="SCT"); b_SCT = Buf()
    if STAGE >= 3:
        sct = xn[1][0:32, :]; b_sct = b_xn[1]; load(sct[:], sconv, b_sct)
        po, bpo = psum()
        for k in range(8):
            P.pe("transpose", po[:, k * 32:(k + 1) * 32], sct[0:32, k * 128:(k + 1) * 128], ident_f[0:32, 0:32],
                 reads=[b_sct, b_ident_f], writes=[bpo])
        P.dve("tensor_copy", SCT[:].rearrange("p k s -> p (k s)"), po[:, 0:256], reads=[bpo], writes=[b_SCT])

        KTN = carve([128, 3, 128]); b_KTN = Buf()
        VN = carve([128, 258]); b_VN = Buf()
        kind_f = sb([2, 128]); b_kind = Buf(); load(kind_f[:], kind_d, b_kind)
        augc = sb([2, 256]); b_augc = Buf(); load(augc[:], augc_d, b_augc)
        smask = carve([128, 8, 128]); b_smask = Buf()
        ptt = sb([128, 8], I32); b_ptt = Buf(); load(ptt[:], pt, b_ptt)
        QS = carve([128, 3, 8, 128]); b_QS = Buf()
        QP = [carve([128, 3, 128]) for _ in range(2)]; b_QP = [Buf(), Buf()]
        OTS = carve([128, 2, 8, 128]); b_OTS = Buf()
        R = 8
        KVT = [carve([128, R * 256]) for i in range(3)]; b_KVT = [Buf() for _ in range(3)]
        KRT = [carve([128, 16 * 64]) for i in range(2)]; b_KRT = [Buf() for _ in range(2)]
        KTS = [carve([128, 2, 3, 128]) for i in range(3)]; b_KTS = [Buf() for _ in range(3)]
        PS4 = [carve([128, 4, 128]) for _ in range(2)]; b_PS4 = [Buf(), Buf()]
        PN = carve([128, 128]); b_PN = Buf()
        OLS = carve([128, 256]); b_OLS = Buf()
        alias_barrier([b_KTN, b_VN, b_smask, b_QS, b_OTS, b_PN, b_OLS] + b_QP + b_KVT + b_KRT + b_KTS + b_PS4, b_KT + b_VP + [b_KTaug])
        P.dve("memset", VN[:, 256:258], 1.0, writes=[b_VN])
        P.dve("tensor_copy", KTN[64:66, 2, :], kind_f[:], reads=[b_kind], writes=[b_KTN])
        load(smask.rearrange("p a b -> p (a b)"), smask_d, b_smask, q=POOL)
        for i in range(3):
            P.dve("tensor_copy", KTS[i][64:66, :, 2, 0:64], kind_f[:, 0:1].unsqueeze(1).to_broadcast([2, 2, 64]),
                  reads=[b_kind], writes=[b_KTS[i]])
            P.dve("tensor_copy", KTS[i][64:66, :, 2, 64:128], kind_f[:, 8:9].unsqueeze(1).to_broadcast([2, 2, 64]),
                  reads=[b_kind, b_KTS[i]], writes=[b_KTS[i]])
        RIS = sb([128, 1]); b_RIS = Buf()
        VT = xn[0][0:32, :]; b_VT = b_xn[0]

        mods1 = ("s", A1s, modT[:, 0, :, 1:17], b_mods)
        mods2 = ("s", A2s, modT[:, 2, :, 1:17], b_mods)

        def sample_attention(N):
            DEFB[:] = [4, 5, 6, 7]
            GBANK[:] = [4, 5, 6, 7]
            kv_build(UT, b_UT[0], 0, cs_tok_s, ckv_s, kr_s, VN[:, 0:256], b_VN, KTN[:, :, :], b_KTN)
            for h in range(8):
                qi = head_q(h, N, None)
                P.dve("tensor_copy", QS[:, 0:2, h, :], QT[qi][:, 0:2, 0:128], reads=[b_QT[qi], b_QS], writes=[b_QS])
                P.dve("tensor_copy", QS[0:64, 2, h, :], QT[qi][0:64, 2, 0:128], reads=[b_QT[qi], b_QS], writes=[b_QS])
            def issue_kv(n):
                if n >= 128:
                    return
                jj, rbb = n // 16, n % 16
                if rbb % 2 == 0:
                    gk = n // 2
                    P.dma("indirect_dma_start", out=KRT[gk % 2][:, :], out_offset=None, in_=kr_pool,
                          in_offset=bass.IndirectOffsetOnAxis(ap=ptt[:, jj:jj + 1], axis=0),
                          element_offset=(rbb // 2) * 16 * 64, reads=[b_ptt], writes=[b_KRT[gk % 2]], q=POOL)
                P.dma("indirect_dma_start", out=KVT[n % 3][:, :], out_offset=None, in_=ckv_pool,
                      in_offset=bass.IndirectOffsetOnAxis(ap=ptt[:, jj:jj + 1], axis=0),
                      element_offset=rbb * R * 256, reads=[b_ptt], writes=[b_KVT[n % 3]], q=POOL)

            issue_kv(0)
            for j in range(8):
                qp = QP[j % 2]; bqp = b_QP[j % 2]
                for c in range(3):
                    kc = 128 if c < 2 else 64
                    P.dve("tensor_copy",
                          qp[0:kc, c, :].rearrange("p (s h t) -> p s h t", s=2, h=8),
                          QS[0:kc, c, :, 16 * j:16 * j + 16].rearrange("p h (s t) -> p s h t", t=8), reads=[b_QS, bqp], writes=[bqp])
                pm, bpm = psum(GBANK, "gb")
                for c in range(3):
                    kc = 128 if c < 2 else 64
                    P.pe("matmul", pm[0:2, 0:128], lhsT=KTN[0:kc, c, 16 * j:16 * j + 16:8], rhs=qp[0:kc, c, :],
                         start=(c == 0), stop=(c == 2), reads=[b_KTN, bqp], writes=[bpm])
                P.dve("tensor_tensor", out=augt[:], in0=pm[0:2, 0:128], in1=augc[:, 0:128], op=ALU.mult, reads=[bpm, b_augc], writes=[b_augt])
                P.dve("tensor_tensor", out=qp[64:66, 2, :], in0=augt[:], in1=augc[:, 128:256], op=ALU.add, reads=[b_augt, b_augc, bqp], writes=[bqp])
                ob = ps[OBANK[j % 2]]; bob = bps[OBANK[j % 2]]
                pS, bpS = psum(SBANK, "sb")
                for c in range(3):
                    kc = 128 if c < 2 else 66
                    P.pe("matmul", pS[:, 0:128], lhsT=KTN[0:kc, c, :], rhs=qp[0:kc, c, :], start=(c == 0), stop=(c == 2),
                         reads=[b_KTN, bqp], writes=[bpS])
                P.act("activation", out=PN[:], in_=pS[:, 0:128], func=AF.Exp, scale=SCALE, reads=[bpS], writes=[b_PN])
                P.dve("tensor_tensor", out=PN[:], in0=PN[:], in1=smask[:, j, :], op=ALU.mult, reads=[b_PN, b_smask], writes=[b_PN])
                P.pe("matmul", ob[:, 0:257], lhsT=PN[:], rhs=VN[:, 0:257], start=True, stop=False, reads=[b_PN, b_VN], writes=[bob])

                def stage_T(st):
                    n = 16 * j + st // 4
                    r0 = (st % 4) * 2
                    kvv = KVT[n % 3][:, :].rearrange("p (r f) -> p r f", f=256)
                    krv = KRT[(n // 2) % 2][:, :].rearrange("p (r f) -> p r f", f=64)
                    ti = cnt["h"] % 3
                    cnt["h"] += 1
                    pb, bpb = psum(GBANK, "gb")
                    pbb = pb[:, :].bitcast(BF16)
                    for rr_ in range(2):
                        r = r0 + rr_
                        P.pe("transpose", pbb[:, (rr_ * 2) * 128:(rr_ * 2 + 1) * 128], kvv[:, r, 0:128], ident_b[:],
                             reads=[b_KVT[n % 3], b_ident_b], writes=[bpb])
                        P.pe("transpose", pbb[:, (rr_ * 2 + 1) * 128:(rr_ * 2 + 2) * 128], kvv[:, r, 128:256], ident_b[:],
                             reads=[b_KVT[n % 3], b_ident_b], writes=[bpb])
                    rk = (n % 2) * 8 + r0
                    P.pe("transpose", pbb[:, 512:640], KRT[(n // 2) % 2][:, rk * 64:(rk + 2) * 64], ident_b[:],
                         reads=[b_KRT[(n // 2) % 2], b_ident_b], writes=[bpb])
                    pv = pbb[:, 0:512].rearrange("p (a c t) -> p a c t", a=2, c=2)
                    P.act("activation", out=KTS[ti][:, :, 0:2, :], in_=pv, func=AF.Copy, reads=[bpb, b_KTS[ti]], writes=[b_KTS[ti]])
                    P.dve("tensor_copy", KTS[ti][0:64, 0, 2, :], pbb[0:64, 512:640], reads=[bpb, b_KTS[ti]], writes=[b_KTS[ti]])
                    P.dve("tensor_copy", KTS[ti][0:64, 1, 2, :], pbb[64:128, 512:640], reads=[bpb, b_KTS[ti]], writes=[b_KTS[ti]])
                    return ti

                def stage_S(st, ti, pS, bpS):
                    for rr_ in range(2):
                        col = ((st % 2) * 2 + rr_) * 128
                        for c in range(3):
                            kc = 128 if c < 2 else 66
                            P.pe("matmul", pS[:, col:col + 128], lhsT=KTS[ti][0:kc, rr_, c, :], rhs=qp[0:kc, c, :],
                                 start=(c == 0), stop=(c == 2), reads=[b_KTS[ti], bqp], writes=[bpS])

                def stage_V(q, pi):
                    n = 16 * j + q // 2
                    kvv = KVT[n % 3][:, :].rearrange("p (r f) -> p r f", f=256)
                    for r_ in range(4):
                        r = (q % 2) * 4 + r_
                        lastmm = (q == 31 and r_ == 3)
                        P.pe("matmul", ob[:, 0:256], lhsT=PS4[pi][:, r_, :], rhs=kvv[:, r, :], start=False, stop=lastmm, skip_group_check=True,
                             reads=[b_PS4[pi], b_KVT[n % 3]], writes=[bob])
                        P.pe("matmul", ob[:, 256:257], lhsT=PS4[pi][:, r_, :], rhs=ones_b[:, 0:1], start=False, stop=lastmm, skip_group_check=True,
                             reads=[b_PS4[pi], b_ones], writes=[bob])

                tis = {0: stage_T(0)}
                prev = None
                for st in range(64):
                    if st == 1 and j > 0:
                        issue_kv(16 * j + 2)
                    if st == 0 and j == 0:
                        issue_kv(1)
                        issue_kv(2)
                    if st + 1 < 64:
                        tis[st + 1] = stage_T(st + 1)
                    if st % 2 == 0:
                        pS, bpS = psum(SBANK, "sb")
                    stage_S(st, tis[st], pS, bpS)
                    if st % 2 == 1:
                        q = st // 2
                        pi = q % 2
                        P.act("activation", out=PS4[pi][:].rearrange("p a b -> p (a b)"), in_=pS[:, :], func=AF.Exp, scale=SCALE,
                              reads=[bpS], writes=[b_PS4[pi]])
                        if prev is not None:
                            stage_V(*prev)
                            if prev[0] % 2 == 1:
                                issue_kv(16 * j + prev[0] // 2 + 3)
                        prev = (q, pi)
                stage_V(*prev)
                P.dve("reciprocal", RIS[:], ob[:, 256:257], reads=[bob], writes=[b_RIS])
                P.act("activation", out=OLS[:], in_=ob[:, 0:256], func=AF.Copy, scale=RIS[:, 0:1], reads=[bob, b_RIS], writes=[b_OLS])
                pb, bpb = psum(GBANK, "gb")
                pbb = pb[:, :].bitcast(BF16)
                for m in range(2):
                    P.pe("transpose", pbb[:, m * 128:(m + 1) * 128], OLS[:, m * 128:(m + 1) * 128], ident_b[:],
                         reads=[b_OLS, b_ident_b], writes=[bpb])
                for m in range(2):
                    P.dve("tensor_copy", OTS[:, m, :, 16 * j:16 * j + 16].rearrange("p h (s t) -> p s h t", t=8),
                                                                pbb[:, m * 128:(m + 1) * 128].rearrange("p (s h t) -> p s h t", s=2, h=8),
                          reads=[bpb, b_OTS], writes=[b_OTS])
            for h in range(8):
                pa, bpa = psum(GBANK, "gb")
                for m in range(2):
                    P.pe("matmul", pa[:, 0:N], lhsT=wuv[:, m, h, :], rhs=OTS[:, m, h, :], start=(m == 0), stop=(m == 1),
                         reads=[b_wuv, b_OTS], writes=[bpa])
                merge_head(h, N, pa, bpa)
            DEFB[:] = [0, 1, 2, 3, 4, 5, 6, 7]

        augt = sb([2, 128]); b_augt = Buf()
        cnt["g0"] = 0
        group(128, 1, xd, mods1, mods2, g1s, b_g["g1s"], g2s, b_g["g2s"],
              cs_feat_s.rearrange("p (a t) -> p a t", a=2), y_s, sample_attention, True, preloaded=True)
        for hh in range(2):
            po, bpo = psum()
            for kk in range(4):
                k = hh * 4 + kk
                P.pe("transpose", po[0:32, kk * 128:(kk + 1) * 128], VLAST[:, k, :], ident_f[:], reads=[b_VLAST, b_ident_f], writes=[bpo])
            P.dve("tensor_copy", VT[:, hh * 512:(hh + 1) * 512], po[0:32, :], reads=[bpo, b_VT], writes=[b_VT])
        P.dma("dma_start", out=conv_s, in_=VT[:], reads=[b_VT])

    P.emit()
    return nc


_NC_CACHE = {}


def _rope_tables(pos):
    inv = (np.float32(10000.0) ** (-np.arange(0, 64, 2, dtype=np.float32) / np.float32(64))).astype(np.float32)
    ang = pos.astype(np.float32)[:, None] * inv[None, :]
    return np.cos(ang).astype(np.float32), np.sin(ang).astype(np.float32)


def kernel(x_prompt, x_sample, cache_ckv, cache_krope, state_conv, page_table, c_prompt, c_sample,
           w_ada, b_ada, g_attn, w_in, g_q, w_q_b, g_kv, w_kv_b, conv_w, w_o, g_mlp, w_1, w_2, g_final):
    f32 = np.float32
    A = lambda a: np.ascontiguousarray(np.asarray(a))
    x_prompt = np.asarray(x_prompt, f32); x_sample = np.asarray(x_sample, f32)
    if "nc" not in _NC_CACHE:
        _NC_CACHE["nc"] = build_program()
    nc = _NC_CACHE["nc"]

    ident = np.eye(128, dtype=f32)
    tri = (np.arange(128)[:, None] <= np.arange(128)[None, :]).astype(f32)
    cos_p, sin_p = _rope_tables(np.arange(SEQ))
    cs_tok_p = np.concatenate([cos_p, cos_p, -sin_p, sin_p], axis=1)
    pos_s = 8192 + (np.arange(128) % 8)
    cos_s, sin_s = _rope_tables(pos_s)
    cs_tok_s = np.concatenate([cos_s, cos_s, -sin_s, sin_s], axis=1)
    cs_feat_s = np.concatenate([np.concatenate([cos_s, cos_s], 1).T, np.concatenate([-sin_s, sin_s], 1).T], axis=1)
    smask = np.zeros((128, 8, 128), f32)
    kk_s = np.arange(128) // 8; kk_t = np.arange(128) % 8
    qc = np.arange(128); q_sl = qc // 64; q_t = qc % 8
    for j in range(8):
        smask[:, j, :] = ((kk_s[:, None] == (2 * j + q_sl)[None, :]) & (kk_t[:, None] <= q_t[None, :])).astype(f32)
    augc = np.zeros((2, 256), f32)
    augc[0, 0:64] = -1.0; augc[1, 64:128] = -1.0
    augc[0, 128 + 64:256] = -BIG; augc[1, 128:128 + 64] = -BIG
    kind = np.zeros((2, 128), f32)
    kind[0, :] = ((np.arange(128) // 8) % 2 == 0); kind[1, :] = ((np.arange(128) // 8) % 2 == 1)

    w_in0 = np.asarray(w_in[0], f32)
    q_a_w = w_in0[:, 0:384]; ckv_w = w_in0[:, 384:640]; kr_w = w_in0[:, 640:704]
    bg_w = w_in0[:, 704:1728]; cg_w = w_in0[:, 1728:2752]; xin_w = w_in0[:, 2752:3776]
    ga_w = w_in0[:, 3776:4800]; gb_w = w_in0[:, 4800:5824]
    kr_sw = np.concatenate([kr_w[:, 32:64], kr_w[:, 0:32]], axis=1)
    w_kvin = A(np.concatenate([ckv_w, kr_w, kr_sw], axis=1))
    parts = [q_a_w]
    for fc in range(8):
        sl = slice(fc * 128, (fc + 1) * 128)
        parts += [bg_w[:, sl], cg_w[:, sl], xin_w[:, sl], ga_w[:, sl], gb_w[:, sl]]
    w_main = A(np.concatenate(parts, axis=1))
    wqb = np.asarray(w_q_b[0], f32).reshape(384, 8, 192)
    w_q = A(np.concatenate([wqb[:, :, 0:128], wqb[:, :, 128:192], wqb[:, :, 160:192], wqb[:, :, 128:160]], axis=2).reshape(384, 2048))
    wkvb = np.asarray(w_kv_b[0], f32).reshape(256, 8, 256)
    w_ukT = A(wkvb[:, :, 0:128].transpose(2, 1, 0).reshape(128, 2048))
    w_uv = A(wkvb[:, :, 128:256].reshape(256, 1024))
    convT = A(np.asarray(conv_w[0], f32).T.reshape(8, 128, 3).transpose(1, 0, 2).reshape(128, 24))
    fm = lambda v, n: A(np.asarray(v, f32).reshape(n, 128).T)
    badaT = fm(b_ada[0], 48)
    bada_g = A(np.stack([np.asarray(b_ada[0], f32)[2048:3072], np.asarray(b_ada[0], f32)[5120:6144]]))
    shared = {
        "w_ada": A(np.asarray(w_ada[0], f32)), "badaT": badaT, "bada_g": bada_g,
        "gattnT": fm(g_attn[0], 8), "gmlpT": fm(g_mlp[0], 8), "gqT": fm(g_q[0], 3),
        "gkv": A(np.asarray(g_kv[0], f32)), "gfin": A(np.asarray(g_final, f32)),
        "w_kvin": w_kvin, "w_main": w_main, "w_q": w_q, "w_ukT": w_ukT, "w_uv": w_uv, "convT": convT,
        "w_o": A(np.asarray(w_o[0], f32)), "w_1": A(np.asarray(w_1[0], f32)), "w_2": A(np.asarray(w_2[0], f32)),
        "ident": ident, "cs_tok_p": A(cs_tok_p), "cs_tok_s": A(cs_tok_s), "cs_feat_s": A(cs_feat_s),
        "smask": A(smask.reshape(128, 1024)), "augc": augc, "kind": kind,
        "ckv_pool": np.asarray(cache_ckv[0], f32).reshape(NPOOL, 128 * 256),
        "kr_pool": np.asarray(cache_krope[0], f32).reshape(NPOOL, 128 * 64),
    }
    page_table = np.asarray(page_table, np.int32)
    in_maps = []
    own_rows = []
    for c in range(8):
        b, half = c // 2, c % 2
        tiles = [2 * i + half for i in range(NOWN)]
        rows = np.concatenate([np.arange(t * 128, (t + 1) * 128) for t in tiles])
        own_rows.append(rows)
        xb = x_prompt[b]
        xh = np.zeros((32, D), f32)
        hm = np.ones((128, 16), f32)
        for i, t in enumerate(tiles):
            if t == 0:
                hm[:, i] = 0.0
            else:
                xh[2 * i:2 * i + 2] = xb[t * 128 - 2:t * 128]
        masks = np.concatenate([tri, np.zeros((128, 128), f32)], 1) if half == 0 else np.concatenate([np.ones((128, 128), f32), tri], 1)
        cosq = np.concatenate([cos_p[rows], cos_p[rows]], 1).T
        sinq = np.concatenate([-sin_p[rows], sin_p[rows]], 1).T
        cs_feat_p = np.concatenate([cosq, sinq], axis=1)
        seqs = np.arange(16 * c, 16 * c + 16)
        ptc = np.zeros((128, 8), np.int32)
        for j in range(8):
            ptc[0:64, j] = page_table[seqs[2 * j]]
            ptc[64:128, j] = page_table[seqs[2 * j + 1]]
        m = dict(shared)
        m.update({
            "xs": A(xb), "xo": A(xb[rows]), "xh": xh, "xd": A(x_sample[seqs].reshape(128, D)),
            "cc": A(np.concatenate([np.asarray(c_prompt, f32)[b:b + 1], np.asarray(c_sample, f32)[seqs]], 0)),
            "pt": ptc, "sconv": A(np.asarray(state_conv[0], f32)[seqs].reshape(32, D)),
            "masks": A(masks), "hmask": hm, "cs_feat_p": A(cs_feat_p),
        })
        in_maps.append(m)

    res = run_bass_kernel_spmd(nc, in_maps, core_ids=list(range(8)))
    R = res.results
    y_prompt = np.zeros((4, SEQ, D), f32)
    y_sample = np.zeros((128, 8, D), f32)
    ckv_pr = np.zeros((1, 4, SEQ, 256), f32); kr_pr = np.zeros((1, 4, SEQ, 64), f32); conv_pr = np.zeros((1, 4, 2, D), f32)
    ckv_sm = np.zeros((1, 128, 8, 256), f32); kr_sm = np.zeros((1, 128, 8, 64), f32); conv_sm = np.zeros((1, 128, 2, D), f32)
    for c in range(8):
        b, half = c // 2, c % 2
        r = R[c]
        y_prompt[b, own_rows[c]] = r["y_p"]
        y_sample[16 * c:16 * c + 16] = r["y_s"].reshape(16, 8, D)
        if half == 0:
            ckv_pr[0, b] = r["ckv_p"]; kr_pr[0, b] = r["kr_p"]
        else:
            conv_pr[0, b] = r["conv_p"]
        ckv_sm[0, 16 * c:16 * c + 16] = r["ckv_s"].reshape(16, 8, 256)
        kr_sm[0, 16 * c:16 * c + 16] = r["kr_s"].reshape(16, 8, 64)
        conv_sm[0, 16 * c:16 * c + 16] = r["conv_s"].reshape(16, 2, D)
    return (y_prompt, y_sample, ckv_pr, kr_pr, conv_pr, ckv_sm, kr_sm, conv_sm)
```

```python
import contextlib
import numpy as np
import concourse.bass as bass
import concourse.mybir as mybir
from concourse.bass_utils import run_bass_kernel_spmd

F32 = mybir.dt.float32
BF16 = mybir.dt.bfloat16
I32 = mybir.dt.int32
AF = mybir.ActivationFunctionType
ALU = mybir.AluOpType

PE, ACT, DVE, POOL, SP = "pe", "act", "dve", "pool", "sp"
COMPUTE = (PE, ACT, DVE, POOL)
NQ = 8

D = 1024
SEQ = 4096
NT = 32
NOWN = 16
NPOOL = 10240
EPS = 1e-6
SCALE = float((128 + 64) ** -0.5)
BIG = 30000.0
STAGE = 3
NTG = 2
NG = 128 * NTG


class Buf:
    __slots__ = ("name", "writers", "readers")

    def __init__(self, name=""):
        self.name = name
        self.writers = {}
        self.readers = {}


class Op:
    __slots__ = ("eng", "fn", "deps", "idx", "is_dma", "signal", "sem", "val", "qslot")

    def __init__(self, eng, fn, is_dma):
        self.eng = eng
        self.fn = fn
        self.is_dma = is_dma
        self.deps = []
        self.signal = False
        self.sem = None
        self.val = None
        self.qslot = 0


class Prog:
    def __init__(self, nc):
        self.nc = nc
        self.ops = {e: [] for e in (PE, ACT, DVE, POOL, SP)}
        self.ndma = {e: 0 for e in (PE, ACT, DVE, POOL, SP)}

    def _key(self, op):
        return ("dma", id(op)) if op.is_dma else op.eng

    def op(self, eng, meth, args, kwargs, reads=(), writes=(), dma=False):
        fn = (lambda e: getattr(e, meth)(*args, **kwargs))
        o = Op(eng, fn, dma)
        o.idx = len(self.ops[eng])
        deps = []
        for b in reads:
            for w in b.writers.values():
                deps.append(w)
        for b in writes:
            for w in b.writers.values():
                if w.is_dma or dma or w.eng != eng:
                    deps.append(w)
            for r in b.readers.values():
                if r.is_dma or dma or r.eng != eng:
                    deps.append(r)
        o.deps = [d for d in deps if not (d.eng == PE and eng == PE and not d.is_dma and not dma)]
        for b in reads:
            b.readers[self._key(o)] = o
        for b in writes:
            b.writers = {self._key(o): o}
            b.readers = {}
        if dma:
            o.qslot = self.ndma[eng]
            self.ndma[eng] += 1
        self.ops[eng].append(o)
        return o

    def pe(self, meth, *args, reads=(), writes=(), **kw):
        return self.op(PE, meth, args, kw, reads, writes)

    def act(self, meth, *args, reads=(), writes=(), **kw):
        return self.op(ACT, meth, args, kw, reads, writes)

    def dve(self, meth, *args, reads=(), writes=(), **kw):
        return self.op(DVE, meth, args, kw, reads, writes)

    def pool(self, meth, *args, reads=(), writes=(), **kw):
        return self.op(POOL, meth, args, kw, reads, writes)

    def dma(self, meth, *args, reads=(), writes=(), q=SP, **kw):
        return self.op(q, meth, args, kw, reads, writes, dma=True)

    def emit(self):
        nc = self.nc
        for e in self.ops:
            for o in self.ops[e]:
                for d in o.deps:
                    d.signal = True
        with contextlib.ExitStack() as st:
            csem = {e: st.enter_context(nc.semaphore("s_" + e)) for e in COMPUTE}
            qsem = {}
            for e in self.ops:
                if self.ndma[e]:
                    qsem[e] = [st.enter_context(nc.semaphore("q_%s_%d" % (e, i))) for i in range(NQ)]
            for e in self.ops:
                c = 0
                for o in self.ops[e]:
                    if o.is_dma:
                        o.sem = qsem[e][o.qslot % NQ]
                        o.val = 16 * (o.qslot // NQ + 1)
                    elif o.signal:
                        c += 1
                        o.sem, o.val = csem[e], c
            block = st.enter_context(nc.Block())
            handles = {PE: block.tensor, ACT: block.scalar, DVE: block.vector, POOL: block.gpsimd, SP: block.sync}

            def make(e):
                def body(eng):
                    known = {}
                    for o in self.ops[e]:
                        waits = {}
                        for d in o.deps:
                            kk = id(d.sem)
                            if kk not in waits or waits[kk][1] < d.val:
                                waits[kk] = (d.sem, d.val)
                        if o.is_dma and o.qslot >= NQ:
                            s = qsem[e][o.qslot % NQ]
                            v = 16 * (o.qslot // NQ)
                            kk = id(s)
                            if kk not in waits or waits[kk][1] < v:
                                waits[kk] = (s, v)
                        for kk, (s, v) in waits.items():
                            if known.get(kk, 0) >= v:
                                continue
                            eng.wait_ge(s, v)
                            known[kk] = v
                        inst = o.fn(eng)
                        if o.is_dma:
                            inst.then_inc(o.sem, 16)
                        elif o.signal:
                            inst.then_inc(o.sem, 1)
                    if e == SP:
                        for qe in qsem:
                            n = self.ndma[qe]
                            for slot in range(min(NQ, n)):
                                last = ((n - 1 - slot) // NQ) + 1
                                eng.wait_ge(qsem[qe][slot], 16 * last)
                return body

            for e in (SP, POOL, ACT, DVE, PE):
                if self.ops[e] or e == SP:
                    handles[e](make(e))


def build_program():
    nc = bass.Bass("TRN2", target_bir_lowering=False)
    P = Prog(nc)

    def din(name, shape, dt=F32):
        return nc.dram_tensor(name, list(shape), dt, kind="ExternalInput").ap()

    def dout(name, shape):
        return nc.dram_tensor(name, list(shape), F32, kind="ExternalOutput").ap()

    _n = [0]

    def sb(shape, dt=F32, name=None):
        _n[0] += 1
        return nc.alloc_sbuf_tensor(name or ("t%d" % _n[0]), list(shape), dt)

    xs = din("xs", [SEQ, D])
    xo = din("xo", [NOWN * 128, D])
    xh = din("xh", [32, D])
    xd = din("xd", [128, D])
    cc = din("cc", [17, D])
    pt = din("pt", [128, 8], I32)
    ckv_pool = din("ckv_pool", [NPOOL, 128 * 256])
    kr_pool = din("kr_pool", [NPOOL, 128 * 64])
    sconv = din("sconv", [32, D])
    w_ada = din("w_ada", [D, 6 * D])
    badaT = din("badaT", [128, 48])
    bada_g = din("bada_g", [2, D])
    gattnT = din("gattnT", [128, 8])
    gmlpT = din("gmlpT", [128, 8])
    gqT = din("gqT", [128, 3])
    gkv = din("gkv", [256])
    gfin = din("gfin", [D])
    w_kvin = din("w_kvin", [D, 384])
    w_main = din("w_main", [D, 5504])
    w_q = din("w_q", [384, 2048])
    w_ukT = din("w_ukT", [128, 8 * 256])
    w_uv = din("w_uv", [256, 8 * 128])
    convT = din("convT", [128, 8 * 3])
    w_o = din("w_o", [D, D])
    w_1 = din("w_1", [D, 4 * D])
    w_2 = din("w_2", [4 * D, D])
    ident_d = din("ident", [128, 128])
    masks_d = din("masks", [128, 2 * 128])
    hmask_d = din("hmask", [128, 16])
    cs_tok_p = din("cs_tok_p", [SEQ, 128])
    cs_tok_s = din("cs_tok_s", [128, 128])
    cs_feat_p = din("cs_feat_p", [64, 2 * NOWN * 128])
    cs_feat_s = din("cs_feat_s", [64, 2 * 128])
    smask_d = din("smask", [128, 8 * 128])
    augc_d = din("augc", [2, 2 * 128])
    kind_d = din("kind", [2, 128])

    y_p = dout("y_p", [NOWN * 128, D])
    y_s = dout("y_s", [128, D])
    ckv_p = dout("ckv_p", [SEQ, 256])
    kr_p = dout("kr_p", [SEQ, 64])
    conv_p = dout("conv_p", [2, D])
    ckv_s = dout("ckv_s", [128, 256])
    kr_s = dout("kr_s", [128, 64])
    conv_s = dout("conv_s", [32, D])

    ps = [nc.alloc_psum_tensor("ps%d" % i, [128, 512], F32) for i in range(8)]
    bps = [Buf("ps%d" % i) for i in range(8)]
    rr = {"g": 0}

    DEFB = [0, 1, 2, 3, 4, 5, 6, 7]

    def psum(group=DEFB, key="g"):
        i = group[rr.get(key, 0) % len(group)]
        rr[key] = rr.get(key, 0) + 1
        return ps[i], bps[i]

    def load(dst_ap, src_ap, buf, q=SP, reads=()):
        P.dma("dma_start", out=dst_ap, in_=src_ap, reads=list(reads), writes=[buf], q=q)

    ident_f = sb([128, 128]); b_ident_f = Buf()
    ident_b = sb([128, 128], BF16); b_ident_b = Buf()
    ones_b = sb([128, 128], BF16); b_ones = Buf()
    load(ident_f[:], ident_d, b_ident_f)
    P.dve("tensor_copy", ident_b[:], ident_f[:], reads=[b_ident_f], writes=[b_ident_b])
    P.dve("memset", ones_b[:], 1.0, writes=[b_ones])
    masks_f = sb([128, 256]); b_masks_f = Buf()
    masks = sb([128, 256], BF16); b_masks = Buf()
    load(masks_f[:], masks_d, b_masks_f)
    P.dve("tensor_copy", masks[:], masks_f[:], reads=[b_masks_f], writes=[b_masks])
    hmask = sb([128, 16]); b_hmask = Buf()
    load(hmask[:], hmask_d, b_hmask)
    gattn_t = sb([128, 8]); b_gattn = Buf(); load(gattn_t[:], gattnT, b_gattn)
    gmlp_t = sb([128, 8]); b_gmlp = Buf(); load(gmlp_t[:], gmlpT, b_gmlp)
    gq_t = sb([128, 3]); b_gq = Buf(); load(gq_t[:], gqT, b_gq)
    bada_t = sb([128, 48]); b_bada = Buf(); load(bada_t[:], badaT, b_bada)
    convw = sb([128, 24]); b_convw = Buf(); load(convw[:], convT, b_convw)
    gkv_bc = sb([128, 256]); b_gkv = Buf(); load(gkv_bc[:], gkv.partition_broadcast(128), b_gkv)
    gfin_bc = sb([128, D]); b_gfin = Buf(); load(gfin_bc[:], gfin.partition_broadcast(128), b_gfin)
    xn = [sb([128, D]) for _ in range(2)]; b_xn = [Buf(), Buf()]
    YO = xn; b_YO = b_xn
    XG = sb([128, NTG, D], name="XG"); b_XG = [Buf("XG%d" % t) for t in range(NTG)]
    UT = sb([128, 8, NG], BF16, name="UT"); b_UT = [Buf("UT%d" % t) for t in range(NTG)]
    REG = sb([128, 32, NG], BF16, name="REG"); b_REG = [Buf() for _ in range(32)]
    BAD = [XG[:, 0, :], XG[:, 1, :]]
    b_badag, b_badag1 = b_XG[0], b_XG[1]
    load(BAD[0], bada_g[0].partition_broadcast(128), b_badag)
    load(BAD[1], bada_g[1].partition_broadcast(128), b_badag1)

    wkv = sb([128, 8, 384], BF16); b_wkv = Buf()
    load(wkv[:], w_kvin.rearrange("(k p) c -> p k c", p=128), b_wkv, q=POOL)
    wq = sb([128, 3, 2048], BF16); b_wq = Buf()
    load(wq[:], w_q.rearrange("(k p) c -> p k c", p=128), b_wq, q=POOL)
    wuk = sb([128, 8, 256], BF16); b_wuk = Buf()
    load(wuk[:].rearrange("p h c -> p (h c)"), w_ukT, b_wuk, q=POOL)
    wuv = sb([128, 2, 8, 128], BF16); b_wuv = Buf()
    load(wuv[:].rearrange("p m h v -> p m (h v)"), w_uv.rearrange("(m p) c -> p m c", p=128), b_wuv, q=POOL)

    WUNIT = 1024
    NUNIT = 18
    wring = sb([128, NUNIT * WUNIT], BF16, name="wring")
    bunits = [Buf("wu%d" % i) for i in range(NUNIT)]
    wstate = {"pos": 0}

    def wload(src2d, kch, cols, reads=()):
        n = kch * cols
        nu = (n + WUNIT - 1) // WUNIT
        if wstate["pos"] + nu > NUNIT:
            wstate["pos"] = 0
        u0 = wstate["pos"]
        wstate["pos"] += nu
        bufs = bunits[u0:u0 + nu]
        view = wring[:, u0 * WUNIT:u0 * WUNIT + n].rearrange("p (k c) -> p k c", c=cols)
        P.dma("dma_start", out=view, in_=src2d.rearrange("(k p) c -> p k c", p=128),
              reads=list(reads), writes=bufs, q=POOL)
        return view, bufs

    w_main_b = nc.dram_tensor("w_main_b", [D, 5504], BF16, kind="Internal").ap()
    w_o_b = nc.dram_tensor("w_o_b", [D, D], BF16, kind="Internal").ap()
    w_1_b = nc.dram_tensor("w_1_b", [D, 4 * D], BF16, kind="Internal").ap()
    w_2_b = nc.dram_tensor("w_2_b", [4 * D, D], BF16, kind="Internal").ap()
    b_wconv = {"main": Buf(), "o": Buf(), "1": Buf(), "2": Buf()}

    def convert_weights():
        for (src, dst, key) in ((w_main, w_main_b, "main"), (w_o, w_o_b, "o"), (w_1, w_1_b, "1"), (w_2, w_2_b, "2")):
            rows, cols = src.shape
            for r0 in range(0, rows, 128):
                for c0 in range(0, cols, 2048):
                    c1 = min(cols, c0 + 2048)
                    P.dma("dma_start", out=dst[r0:r0 + 128, c0:c1], in_=src[r0:r0 + 128, c0:c1],
                          reads=[b_wconv[key]], writes=[b_wconv[key]], q=POOL)
                    yield None

    conv_gen = convert_weights()

    def convert_some(n):
        for _ in range(n):
            try:
                next(conv_gen)
            except StopIteration:
                return

    RA = sb([128, 3 * SEQ + NT * 258 + 8], BF16, name="RA")
    KT = RA[:, 0:3 * SEQ].rearrange("p (c t) -> p c t", c=3); b_KT = [Buf("KT%d" % t) for t in range(NT)]
    VP = RA[:, 3 * SEQ:3 * SEQ + NT * 258].rearrange("p (t f) -> p t f", f=258); b_VP = [Buf("VP%d" % t) for t in range(NT)]
    ra_off = [0]

    def carve(shape):
        n = 1
        for d_ in shape[1:]:
            n *= d_
        v = RA[:, ra_off[0]:ra_off[0] + n]
        ra_off[0] += n + (n % 2)
        assert ra_off[0] <= 3 * SEQ + NT * 258
        if len(shape) == 2:
            return v
        if len(shape) == 3:
            return v.rearrange("p (a b) -> p a b", b=shape[2])
        return v.rearrange("p (a b c) -> p a b c", b=shape[2], c=shape[3])

    def alias_barrier(new_bufs, old_bufs):
        merged = {}
        for ob in old_bufs:
            for dct in (ob.readers, ob.writers):
                for k, o in dct.items():
                    if k not in merged or merged[k].idx < o.idx:
                        merged[k] = o
        for nb in new_bufs:
            nb.readers.update(merged)
    b_KTaug = Buf()
    P.dve("memset", KT[64:65, 2, :], 1.0, writes=[b_KTaug])
    P.dve("memset", VP[:, :, 256:258], 1.0, writes=b_VP)

    modT = sb([128, 4, 8, 17]); b_modT = Buf()
    A1p = sb([128, 8]); B1p = sb([128, 8]); A2p = sb([128, 8]); B2p = sb([128, 8]); b_modp = Buf()
    A1s = sb([128, 8, 16]); A2s = sb([128, 8, 16]); b_mods = Buf()
    g1p = sb([128, D]); g2p = sb([128, D]); g1s = sb([128, D]); g2s = sb([128, D])
    b_g = {"g1p": Buf(), "g2p": Buf(), "g1s": Buf(), "g2s": Buf()}

    cct = xn[0][0:17, :]; b_cc = b_xn[0]; load(cct[:], cc, b_cc)
    sil = xn[1][0:17, :]; b_sil = b_xn[1]
    P.act("activation", out=sil[:], in_=cct[:], func=AF.Silu, reads=[b_cc], writes=[b_sil])
    silT = sb([128, 8, 17], BF16); b_silT = Buf()
    pt_, bp_ = psum()
    for k in range(8):
        P.pe("transpose", pt_[:, k * 17:(k + 1) * 17], sil[0:17, k * 128:(k + 1) * 128], ident_f[0:17, 0:17],
             reads=[b_sil, b_ident_f], writes=[bp_])
    P.dve("tensor_copy", silT[:].rearrange("p k s -> p (k s)"), pt_[:, 0:8 * 17], reads=[bp_], writes=[b_silT])
    nreg = 1024 // NG
    lhs_p = REG[:, 16:16 + nreg, :].rearrange("p a b -> p (a b)").rearrange("p (k c) -> p k c", c=128)
    lhs_s = REG[:, 16 + nreg:16 + 2 * nreg, :].rearrange("p a b -> p (a b)").rearrange("p (k c) -> p k c", c=128)
    b_lhs = Buf()
    b_lhs_all = [b_lhs] + b_REG[16:16 + 2 * nreg]
    P.dve("tensor_copy", lhs_p[:], silT[:, :, 0:1].to_broadcast([128, 8, 128]), reads=[b_silT], writes=b_lhs_all)
    for k in range(8):
        P.dve("tensor_copy", lhs_s[:, k, :].rearrange("p (s t) -> p s t", t=8),
                                           silT[:, k, 1:17].unsqueeze(2).to_broadcast([128, 16, 8]),
              reads=[b_silT, b_lhs], writes=b_lhs_all)

    for j in range(6):
        wv, wb = wload(w_ada[:, j * D:(j + 1) * D], 8, D)
        if j in (0, 1, 3, 4):
            jj = (0, 1, None, 2, 3)[j]
            po, bpo = psum()
            for m in range(8):
                for k in range(8):
                    P.pe("matmul", po[:, m * 17:(m + 1) * 17], lhsT=wv[:, k, m * 128:(m + 1) * 128],
                                                      rhs=silT[:, k, :], start=(k == 0), stop=(k == 7),
                         reads=wb + [b_silT], writes=[bpo])
            for m in range(8):
                P.dve("tensor_scalar", modT[:, jj, m, :], po[:, m * 17:(m + 1) * 17],
                                                                 bada_t[:, j * 8 + m:j * 8 + m + 1], None, op0=ALU.add,
                      reads=[bpo, b_bada, b_modT], writes=[b_modT])
        else:
            gi = 0 if j == 2 else 1
            for (lh, gt, bn) in ((lhs_p, g1p if gi == 0 else g2p, "g%dp" % (gi + 1)),
                                 (lhs_s, g1s if gi == 0 else g2s, "g%ds" % (gi + 1))):
                for n in range(2):
                    po, bpo = psum()
                    for k in range(8):
                        P.pe("matmul", po[:, :], lhsT=lh[:, k, :], rhs=wv[:, k, n * 512:(n + 1) * 512],
                                                                      start=(k == 0), stop=(k == 7),
                             reads=wb + b_lhs_all, writes=[bpo])
                    P.dve("tensor_tensor", out=gt[:, n * 512:(n + 1) * 512], in0=po[:, :],
                                                                              in1=BAD[gi][:, n * 512:(n + 1) * 512], op=ALU.add,
                          reads=[bpo, b_badag, b_badag1, b_g[bn]], writes=[b_g[bn]])
    convert_some(8)
    P.dve("scalar_tensor_tensor", out=A1p[:], in0=modT[:, 1, :, 0], scalar=1.0, in1=gattn_t[:], op0=ALU.add, op1=ALU.mult,
          reads=[b_modT, b_gattn], writes=[b_modp])
    P.dve("scalar_tensor_tensor", out=A2p[:], in0=modT[:, 3, :, 0], scalar=1.0, in1=gmlp_t[:], op0=ALU.add, op1=ALU.mult,
          reads=[b_modT, b_gmlp, b_modp], writes=[b_modp])
    P.dve("tensor_copy", B1p[:], modT[:, 0, :, 0], reads=[b_modT, b_modp], writes=[b_modp])
    P.dve("tensor_copy", B2p[:], modT[:, 2, :, 0], reads=[b_modT, b_modp], writes=[b_modp])
    P.dve("scalar_tensor_tensor", out=A1s[:], in0=modT[:, 1, :, 1:17], scalar=1.0,
                                           in1=gattn_t[:].unsqueeze(2).to_broadcast([128, 8, 16]), op0=ALU.add, op1=ALU.mult,
          reads=[b_modT, b_gattn], writes=[b_mods])
    P.dve("scalar_tensor_tensor", out=A2s[:], in0=modT[:, 3, :, 1:17], scalar=1.0,
                                           in1=gmlp_t[:].unsqueeze(2).to_broadcast([128, 8, 16]), op0=ALU.add, op1=ALU.mult,
          reads=[b_modT, b_gmlp, b_mods], writes=[b_mods])

    nj = D // NG
    JUNK = {"g": (REG[:, 0:nj, :].rearrange("p a b -> p (a b)"), b_REG[0:nj]),
            "k": (REG[:, 24:24 + nj, :].rearrange("p a b -> p (a b)"), b_REG[24:24 + nj])}
    TMP = [sb([128, 512]) for _ in range(2)]; b_TMP = [Buf(), Buf()]
    st_ss = sb([128, 8]); st_rs = sb([128, 8]); b_st = [Buf() for _ in range(8)]
    stc = {"n": 0}

    def rstd_of(src_ap, src_bufs, npart, ncols, inv_n, jsel="g"):
        i = stc["n"] % 8
        stc["n"] += 1
        junk, jb = JUNK[jsel]
        P.act("activation", out=junk[0:npart, 0:ncols], in_=src_ap, func=AF.Square, accum_out=st_ss[0:npart, i:i + 1],
              reads=list(src_bufs), writes=list(jb) + [b_st[i]])
        P.act("activation", out=st_rs[0:npart, i:i + 1], in_=st_ss[0:npart, i:i + 1], func=AF.Ln, scale=inv_n, bias=EPS,
              reads=[b_st[i]], writes=[b_st[i]])
        P.act("activation", out=st_rs[0:npart, i:i + 1], in_=st_rs[0:npart, i:i + 1], func=AF.Exp, scale=-0.5,
              reads=[b_st[i]], writes=[b_st[i]])
        return st_rs[0:npart, i:i + 1], b_st[i]

    xnc = {"n": 0}

    def make_uT(x_ap, x_bufs, npart, uT, uT_buf, col0, mod, jsel="g"):
        rs, rb = rstd_of(x_ap, x_bufs, npart, D, 1.0 / D, jsel)
        i = xnc["n"] % 2
        xnc["n"] += 1
        xnb = xn[i][:, :].bitcast(BF16)
        P.dve("tensor_scalar", xnb[0:npart, 0:D], x_ap, rs, None, op0=ALU.mult,
              reads=list(x_bufs) + [rb], writes=[b_xn[i]])
        for half in range(2):
            po_, bpo = psum()
            po = po_[:, :].bitcast(BF16)
            for kk in range(4):
                k = half * 4 + kk
                P.pe("transpose", po[:, kk * 128:kk * 128 + npart], xnb[0:npart, k * 128:(k + 1) * 128],
                                                              ident_b[0:npart, 0:npart],
                     reads=[b_xn[i], b_ident_b], writes=[bpo])
            for kk in range(4):
                k = half * 4 + kk
                if mod[0] == "p":
                    P.dve("tensor_scalar", uT[:, k, col0:col0 + npart], po[:, kk * 128:kk * 128 + npart],
                                                                       mod[1][:, k:k + 1], mod[2][:, k:k + 1], op0=ALU.mult, op1=ALU.add,
                          reads=[bpo, mod[3]], writes=[uT_buf])
                else:
                    tmp = sb_tmp_s
                    P.dve("tensor_tensor", out=tmp[:].rearrange("p (s t) -> p s t", t=8),
                                                                       in0=po[:, kk * 128:(kk + 1) * 128].rearrange("p (s t) -> p s t", t=8),
                                                                       in1=mod[1][:, k, :].unsqueeze(2).to_broadcast([128, 16, 8]), op=ALU.mult,
                          reads=[bpo, mod[3]], writes=[b_tmp_s])
                    P.dve("tensor_tensor", out=uT[:, k, col0:col0 + 128].rearrange("p (s t) -> p s t", t=8),
                                                         in0=tmp[:].rearrange("p (s t) -> p s t", t=8),
                                                         in1=mod[2][:, k, :].unsqueeze(2).to_broadcast([128, 16, 8]), op=ALU.add,
                          reads=[b_tmp_s, mod[3]], writes=[uT_buf])

    sb_tmp_s = TMP[0][:, 0:128]; b_tmp_s = b_TMP[0]

    kvf = [sb([128, 320]) for _ in range(2)]; b_kvf = [Buf(), Buf()]
    krb = [sb([128, 64], BF16) for _ in range(2)]; b_krb = [Buf(), Buf()]
    cst = [sb([128, 128]) for _ in range(2)]; b_cst = [Buf(), Buf()]
    krt = [sb([128, 128]) for _ in range(2)]; b_krt = [Buf(), Buf()]
    kvc = {"n": 0}

    def kv_build(uT, uT_buf, col0, cs_src, out_ckv, out_kr, vdst, vbuf, ktdst, ktbuf, split=False):
        i = kvc["n"] % 2
        kvc["n"] += 1
        load(cst[i][:], cs_src, b_cst[i])
        po, bpo = psum()
        for k in range(8):
            P.pe("matmul", po[:, 0:384], lhsT=uT[:, k, col0:col0 + 128], rhs=wkv[:, k, :], start=(k == 0), stop=(k == 7),
                 reads=[uT_buf, b_wkv], writes=[bpo])
        rs, rb = rstd_of(po[:, 0:256], [bpo], 128, 256, 1.0 / 256, "k")
        P.dve("scalar_tensor_tensor", out=kvf[i][:, 0:256], in0=po[:, 0:256], scalar=rs, in1=gkv_bc[:], op0=ALU.mult, op1=ALU.mult,
              reads=[bpo, rb, b_gkv], writes=[b_kvf[i]])
        P.dve("tensor_tensor", out=krt[i][:], in0=po[:, 256:384], in1=cst[i][:], op=ALU.mult,
              reads=[bpo, b_cst[i]], writes=[b_krt[i]])
        P.dve("tensor_tensor", out=kvf[i][:, 256:320], in0=krt[i][:, 0:64], in1=krt[i][:, 64:128], op=ALU.add,
              reads=[b_krt[i], b_kvf[i]], writes=[b_kvf[i]])
        P.dma("dma_start", out=out_ckv, in_=kvf[i][:, 0:256], reads=[b_kvf[i]])
        P.dma("dma_start", out=out_kr, in_=kvf[i][:, 256:320], reads=[b_kvf[i]])
        if split:
            return lambda: kv_build2(i, vdst, vbuf, ktdst, ktbuf)
        kv_build2(i, vdst, vbuf, ktdst, ktbuf)

    def kv_build2(i, vdst, vbuf, ktdst, ktbuf):
        P.act("activation", out=vdst, in_=kvf[i][:, 0:256], func=AF.Copy, reads=[b_kvf[i]], writes=[vbuf])
        P.act("activation", out=krb[i][:], in_=kvf[i][:, 256:320], func=AF.Copy, reads=[b_kvf[i]], writes=[b_krb[i]])
        pb, bpb = psum()
        pbb = pb[:, :].bitcast(BF16)
        P.pe("transpose", pbb[:, 0:128], vdst[:, 0:128], ident_b[:], reads=[vbuf, b_ident_b], writes=[bpb])
        P.pe("transpose", pbb[:, 128:256], vdst[:, 128:256], ident_b[:], reads=[vbuf, b_ident_b], writes=[bpb])
        P.pe("transpose", pbb[0:64, 256:384], krb[i][:], ident_b[:], reads=[b_krb[i], b_ident_b], writes=[bpb])
        P.act("activation", out=ktdst[:, 0:2, :], in_=pbb[:, 0:256].rearrange("p (c t) -> p c t", t=128), func=AF.Copy,
              reads=[bpb], writes=[ktbuf])
        P.act("activation", out=ktdst[0:64, 2, :], in_=pbb[0:64, 256:384], func=AF.Copy, reads=[bpb, ktbuf], writes=[ktbuf])

    uK = [UT[:, :, 0:128], UT[:, :, 128:256]]; b_uK = [b_UT[0], b_UT[1]]
    xk = [XG[:, 0, :], XG[:, 1, :]]; b_xk = [b_XG[0], b_XG[1]]
    modp = ("p", A1p, B1p, b_modp)
    NPRE = NT

    def stage_a(T):
        i = T % 2
        make_uT(xk[i], [b_xk[i]], 128, uK[i], b_uK[i], 0, modp, "k")


    load(xk[0], xs[0:128, :], b_xk[0])
    load(xk[1], xs[128:256, :], b_xk[1])
    stage_a(0)
    for T in range(NPRE):
        i = T % 2
        if T + 2 < NPRE:
            load(xk[i], xs[(T + 2) * 128:(T + 3) * 128, :], b_xk[i])
        part2 = kv_build(uK[i], b_uK[i], 0, cs_tok_p[T * 128:(T + 1) * 128, :], ckv_p[T * 128:(T + 1) * 128, :], kr_p[T * 128:(T + 1) * 128, :],
                         VP[:, T, 0:256], b_VP[T], KT[:, :, T * 128:(T + 1) * 128], b_KT[T], split=True)
        if T + 1 < NPRE:
            stage_a(T + 1)
        part2()
        convert_some(3)
    convert_some(1000)
    uH = sb([128, 8, 32], BF16); b_uH = Buf()
    load(xk[0][0:32, :], xh, b_xk[0])
    make_uT(xk[0][0:32, :], [b_xk[0]], 32, uH, b_uH, 0, modp, "k")

    GA = REG[:, 0:8, :]; b_GA = b_REG[0:8]
    GB = REG[:, 8:16, :]; b_GB = b_REG[8:16]
    MG = REG[:, 16:24, :]; b_MG = b_REG[16:24]
    HT = REG; b_HT = b_REG
    QA = sb([128, 3, NG], name="QA"); b_QA = Buf()
    SQ = sb([128, 3, NG], BF16, name="SQ"); b_SQ = Buf()
    RQ = sb([128, NG], name="RQ"); b_RQ = Buf()
    QN = sb([128, 3, NG], BF16, name="QN"); b_QN = Buf()
    QNOPE = [sb([128, NG], BF16) for _ in range(2)]; b_QNOPE = [Buf(), Buf()]
    QT = [sb([128, 3, NG], BF16) for _ in range(2)]; b_QT = [Buf(), Buf()]
    CSQ = sb([64, 2, NG], name="CSQ"); b_CSQ = Buf()
    RT = [sb([64, NG]) for _ in range(2)]; b_RT = [Buf(), Buf()]
    PT = [sb([128, NG], BF16) for _ in range(3)]; b_PT = [Buf() for _ in range(3)]
    OL = sb([128, NTG, 256], BF16, name="OL"); b_OL = [Buf() for _ in range(NTG)]
    OLT = sb([128, 2, NG], BF16, name="OLT"); b_OLT = Buf()
    RINV = sb([128, 4]); b_RINV = [Buf() for _ in range(4)]
    CG = [sb([128, NG + 8]) for _ in range(2)]; b_CG = [Buf(), Buf()]
    VPAD = [sb([128, max(NTG * 130, 160)]) for _ in range(2)]; b_VPAD = [Buf(), Buf()]
    ZC = [sb([128, NG]) for _ in range(2)]; b_ZC = [Buf(), Buf()]
    SG = [sb([128, NG]) for _ in range(2)]; b_SG = [Buf(), Buf()]
    VLAST = sb([128, 8, 32], name="VLAST"); b_VLAST = Buf()
    cnt = {"cg": 0, "tmp": 0, "pt": 0, "yo": 0, "h": 0}

    def group(N, ntile, x_src, mod1, mod2, g1t, bg1, g2t, bg2, csq_src, y_dst, attention, sample, preloaded=False, x_next=None, n_next=0):
        NW = N
        for t in range(ntile):
            if not preloaded:
                load(XG[:, t, :], x_src[t * 128:(t + 1) * 128, :], b_XG[t])
            make_uT(XG[:, t, :], [b_XG[t]], 128, UT, b_UT[t], t * 128, mod1)
        load(CSQ[:, :, 0:N], csq_src, b_CSQ)
        wv, wb = wload(w_main_b[:, 0:384], 8, 384, [b_wconv["main"]])
        for c in range(3):
            po, bpo = psum()
            for k in range(8):
                P.pe("matmul", po[:, 0:N], lhsT=wv[:, k, c * 128:(c + 1) * 128], rhs=UT[:, k, 0:N],
                                                        start=(k == 0), stop=(k == 7), reads=wb + b_UT[0:ntile], writes=[bpo])
            P.act("activation", out=QA[:, c, 0:N], in_=po[:, 0:N], func=AF.Copy, reads=[bpo, b_QA], writes=[b_QA])
            P.act("activation", out=SQ[:, c, 0:N], in_=po[:, 0:N], func=AF.Square, reads=[bpo, b_SQ], writes=[b_SQ])
        po, bpo = psum()
        for c in range(3):
            P.pe("matmul", po[:, 0:N], lhsT=ones_b[:], rhs=SQ[:, c, 0:N], start=(c == 0), stop=(c == 2),
                 reads=[b_ones, b_SQ], writes=[bpo])
        P.act("activation", out=RQ[:, 0:N], in_=po[:, 0:N], func=AF.Ln, scale=1.0 / 384, bias=EPS, reads=[bpo], writes=[b_RQ])
        P.act("activation", out=RQ[:, 0:N], in_=RQ[:, 0:N], func=AF.Exp, scale=-0.5, reads=[b_RQ], writes=[b_RQ])
        for c in range(3):
            P.dve("scalar_tensor_tensor", out=QN[:, c, 0:N], in0=QA[:, c, 0:N], scalar=gq_t[:, c:c + 1], in1=RQ[:, 0:N],
                                                        op0=ALU.mult, op1=ALU.mult, reads=[b_QA, b_gq, b_RQ, b_QN], writes=[b_QN])
        for fc in range(8):
            wv, wb = wload(w_main_b[:, 384 + fc * 640:384 + (fc + 1) * 640], 8, 640, [b_wconv["main"]])

            def proj(mc, po, bpo, rhs, ncols, c0=0):
                for k in range(8):
                    P.pe("matmul", po[:, c0:c0 + ncols], lhsT=wv[:, k, mc * 128:(mc + 1) * 128], rhs=rhs(k),
                                                 start=(k == 0), stop=(k == 7), reads=wb + [b_uH] + b_UT[0:ntile], writes=[bpo])

            ci = cnt["cg"] % 2
            cnt["cg"] += 1
            pcg, bpcg = psum()
            proj(1, pcg, bpcg, lambda k: UT[:, k, 0:N], N)
            P.act("activation", out=CG[ci][:, 0:N], in_=pcg[:, 0:N], func=AF.Copy, reads=[bpcg], writes=[b_CG[ci]])
            pxi, bpxi = psum()
            proj(2, pxi, bpxi, lambda k: UT[:, k, 0:N], N)
            vp = VPAD[ci][:, 0:ntile * 130].rearrange("p (t j) -> p t j", j=130)
            if not sample:
                ph, bph = psum()
                nh = 2 * ntile
                proj(1, ph, bph, lambda k: uH[:, k, cnt["g0"] * 2:cnt["g0"] * 2 + nh], nh, 0)
                proj(2, ph, bph, lambda k: uH[:, k, cnt["g0"] * 2:cnt["g0"] * 2 + nh], nh, 16)
                P.act("activation", out=CG[ci][:, NG:NG + nh], in_=ph[:, 0:nh], func=AF.Copy, reads=[bph, b_CG[ci]], writes=[b_CG[ci]])
                P.dve("tensor_tensor", out=vp[:, 0:ntile, 2:130], in0=pxi[:, 0:N].rearrange("p (t j) -> p t j", j=128),
                                                in1=CG[ci][:, 0:N].rearrange("p (t j) -> p t j", j=128), op=ALU.mult,
                      reads=[bpxi, b_CG[ci]], writes=[b_VPAD[ci]])
                P.dve("tensor_tensor", out=vp[:, 0:ntile, 0:2], in0=ph[:, 16:16 + nh].rearrange("p (t j) -> p t j", j=2),
                                                in1=CG[ci][:, NG:NG + nh].rearrange("p (t j) -> p t j", j=2), op=ALU.mult,
                      reads=[bph, b_CG[ci], b_VPAD[ci]], writes=[b_VPAD[ci]])
                g0 = cnt["g0"]
                P.dve("tensor_tensor", out=vp[:, 0:ntile, 0:2], in0=vp[:, 0:ntile, 0:2],
                                                in1=hmask[:, g0:g0 + ntile].unsqueeze(2).to_broadcast([128, ntile, 2]), op=ALU.mult,
                      reads=[b_VPAD[ci], b_hmask], writes=[b_VPAD[ci]])
                vin = [vp[:, 0:ntile, j:j + 128] for j in range(3)]
                zv = ZC[ci][:, 0:N].rearrange("p (t j) -> p t j", j=128)
                if cnt["g0"] + ntile == NOWN:
                    P.dve("tensor_copy", VLAST[:, fc, 0:2], vp[:, ntile - 1, 128:130], reads=[b_VPAD[ci], b_VLAST], writes=[b_VLAST])
            else:
                vps = VPAD[ci][:, 0:160].rearrange("p (s j) -> p s j", j=10)
                P.dve("tensor_tensor", out=vps[:, :, 2:10], in0=pxi[:, 0:128].rearrange("p (s j) -> p s j", j=8),
                                                in1=CG[ci][:, 0:128].rearrange("p (s j) -> p s j", j=8), op=ALU.mult,
                      reads=[bpxi, b_CG[ci]], writes=[b_VPAD[ci]])
                P.dve("tensor_copy", vps[:, :, 0:2], SCT[:, fc, :].rearrange("p (s j) -> p s j", j=2),
                      reads=[b_SCT, b_VPAD[ci]], writes=[b_VPAD[ci]])
                vin = [vps[:, :, j:j + 8] for j in range(3)]
                zv = ZC[ci][:, 0:128].rearrange("p (s j) -> p s j", j=8)
                P.dve("tensor_copy", VLAST[:, fc, :].rearrange("p (s j) -> p s j", j=2), vps[:, :, 8:10],
                      reads=[b_VPAD[ci], b_VLAST], writes=[b_VLAST])
            P.dve("tensor_scalar", zv, vin[0], convw[:, fc * 3:fc * 3 + 1], None, op0=ALU.mult,
                  reads=[b_VPAD[ci], b_convw], writes=[b_ZC[ci]])
            for j in (1, 2):
                P.dve("scalar_tensor_tensor", out=zv, in0=vin[j], scalar=convw[:, fc * 3 + j:fc * 3 + j + 1], in1=zv,
                                                            op0=ALU.mult, op1=ALU.add, reads=[b_VPAD[ci], b_convw, b_ZC[ci]], writes=[b_ZC[ci]])
            pbg, bpbg = psum()
            proj(0, pbg, bpbg, lambda k: UT[:, k, 0:N], N)
            P.dve("tensor_tensor", out=ZC[ci][:, 0:N], in0=pbg[:, 0:N], in1=ZC[ci][:, 0:N], op=ALU.mult,
                  reads=[bpbg, b_ZC[ci]], writes=[b_ZC[ci]])
            pga, bpga = psum()
            proj(3, pga, bpga, lambda k: UT[:, k, 0:N], N)
            P.act("activation", out=GA[:, fc, 0:N], in_=pga[:, 0:N], func=AF.Sigmoid, reads=[bpga], writes=[b_GA[fc]])
            pgb, bpgb = psum()
            proj(4, pgb, bpgb, lambda k: UT[:, k, 0:N], N)
            P.act("activation", out=SG[ci][:, 0:N], in_=pgb[:, 0:N], func=AF.Sigmoid, reads=[bpgb], writes=[b_SG[ci]])
            P.dve("tensor_tensor", out=GB[:, fc, 0:N], in0=SG[ci][:, 0:N], in1=ZC[ci][:, 0:N], op=ALU.mult,
                  reads=[b_SG[ci], b_ZC[ci]], writes=[b_GB[fc]])

        attention(N)

        wv, wb = wload(w_o_b, 8, D, [b_wconv["o"]])
        for t in range(ntile):
            for n in range(2):
                po, bpo = psum()
                for k in range(8):
                    P.pe("matmul", po[:, :], lhsT=MG[:, k, t * 128:(t + 1) * 128], rhs=wv[:, k, n * 512:(n + 1) * 512],
                                                                 start=(k == 0), stop=(k == 7), reads=wb + [b_MG[k]], writes=[bpo])
                i = cnt["tmp"] % 2
                cnt["tmp"] += 1
                P.dve("tensor_tensor", out=TMP[i][:], in0=po[:, :], in1=g1t[:, n * 512:(n + 1) * 512], op=ALU.mult,
                      reads=[bpo, bg1], writes=[b_TMP[i]])
                P.dve("tensor_tensor", out=XG[:, t, n * 512:(n + 1) * 512], in0=XG[:, t, n * 512:(n + 1) * 512],
                                                               in1=TMP[i][:], op=ALU.add, reads=[b_TMP[i], b_XG[t]], writes=[b_XG[t]])
        for t in range(ntile):
            make_uT(XG[:, t, :], [b_XG[t]], 128, UT, b_UT[t], t * 128, mod2)
        for j2 in range(8):
            wv, wb = wload(w_1_b[:, j2 * 512:(j2 + 1) * 512], 8, 512, [b_wconv["1"]])
            for m in range(4):
                po, bpo = psum()
                for k in range(8):
                    P.pe("matmul", po[:, 0:N], lhsT=wv[:, k, m * 128:(m + 1) * 128], rhs=UT[:, k, 0:N],
                                                            start=(k == 0), stop=(k == 7), reads=wb + b_UT[0:ntile], writes=[bpo])
                hc = j2 * 4 + m
                P.act("activation", out=TMP[hc % 2][:, 0:N], in_=po[:, 0:N], func=AF.Relu,
                      reads=[bpo], writes=[b_TMP[hc % 2]])
                P.dve("tensor_tensor", out=HT[:, hc, 0:N], in0=TMP[hc % 2][:, 0:N], in1=TMP[hc % 2][:, 0:N], op=ALU.mult,
                      reads=[b_TMP[hc % 2]], writes=[b_HT[hc]])
        for j in range(4):
            wva, wba = wload(w_2_b[0:2048, j * 256:(j + 1) * 256], 16, 256, [b_wconv["2"]])
            wvb, wbb = wload(w_2_b[2048:4096, j * 256:(j + 1) * 256], 16, 256, [b_wconv["2"]])
            for t in range(ntile):
                po, bpo = psum()
                for k in range(32):
                    wv, wb = (wva, wba) if k < 16 else (wvb, wbb)
                    P.pe("matmul", po[:, 0:256], lhsT=HT[:, k, t * 128:(t + 1) * 128], rhs=wv[:, k % 16, :],
                         start=(k == 0), stop=(k == 31), reads=wb + [b_HT[k]], writes=[bpo])
                i2 = cnt["tmp"] % 2
                cnt["tmp"] += 1
                P.dve("tensor_tensor", out=TMP[i2][:, 0:256], in0=po[:, 0:256], in1=g2t[:, j * 256:(j + 1) * 256], op=ALU.mult,
                      reads=[bpo, bg2], writes=[b_TMP[i2]])
                P.dve("tensor_tensor", out=XG[:, t, j * 256:(j + 1) * 256], in0=XG[:, t, j * 256:(j + 1) * 256],
                                                                 in1=TMP[i2][:, 0:256], op=ALU.add, reads=[b_TMP[i2], b_XG[t]], writes=[b_XG[t]])
        for t in range(ntile):
            rs, rb = rstd_of(XG[:, t, :], [b_XG[t]], 128, D, 1.0 / D)
            i = cnt["yo"] % 2
            cnt["yo"] += 1
            P.dve("scalar_tensor_tensor", out=YO[i][:], in0=XG[:, t, :], scalar=rs, in1=gfin_bc[:], op0=ALU.mult, op1=ALU.mult,
                  reads=[b_XG[t], rb, b_gfin], writes=[b_YO[i]])
            if x_next is not None and t < n_next:
                load(XG[:, t, :], x_next[t * 128:(t + 1) * 128, :], b_XG[t])
            P.dma("dma_start", out=y_dst[t * 128:(t + 1) * 128, :], in_=YO[i][:], reads=[b_YO[i]])

    def head_q(h, N, ktcol_lhsT):
        qi = h % 2
        po, bpo = psum()
        for c in range(3):
            P.pe("matmul", po[:, 0:N], lhsT=wq[:, c, h * 256:h * 256 + 128], rhs=QN[:, c, 0:N], start=(c == 0), stop=(c == 2),
                 reads=[b_wq, b_QN], writes=[bpo])
        P.act("activation", out=QNOPE[qi][:, 0:N], in_=po[:, 0:N], func=AF.Copy, reads=[bpo], writes=[b_QNOPE[qi]])
        pr, bpr = psum()
        for c in range(3):
            P.pe("matmul", pr[0:64, 0:N], lhsT=wq[:, c, h * 256 + 128:h * 256 + 192], rhs=QN[:, c, 0:N], start=(c == 0), stop=(c == 2),
                 reads=[b_wq, b_QN], writes=[bpr])
        pw, bpw = psum()
        for c in range(3):
            P.pe("matmul", pw[0:64, 0:N], lhsT=wq[:, c, h * 256 + 192:h * 256 + 256], rhs=QN[:, c, 0:N], start=(c == 0), stop=(c == 2),
                 reads=[b_wq, b_QN], writes=[bpw])
        P.dve("tensor_tensor", out=RT[0][:, 0:N], in0=pr[0:64, 0:N], in1=CSQ[:, 0, 0:N], op=ALU.mult, reads=[bpr, b_CSQ], writes=[b_RT[0]])
        P.dve("tensor_tensor", out=RT[1][:, 0:N], in0=pw[0:64, 0:N], in1=CSQ[:, 1, 0:N], op=ALU.mult, reads=[bpw, b_CSQ], writes=[b_RT[1]])
        P.dve("tensor_tensor", out=QT[qi][0:64, 2, 0:N], in0=RT[0][:, 0:N], in1=RT[1][:, 0:N], op=ALU.add,
              reads=[b_RT[0], b_RT[1], b_QT[qi]], writes=[b_QT[qi]])
        for m in range(2):
            pl, bpl = psum()
            P.pe("matmul", pl[:, 0:N], lhsT=wuk[:, h, m * 128:(m + 1) * 128], rhs=QNOPE[qi][:, 0:N], start=True, stop=True,
                 reads=[b_wuk, b_QNOPE[qi]], writes=[bpl])
            P.act("activation", out=QT[qi][:, m, 0:N], in_=pl[:, 0:N], func=AF.Copy, reads=[bpl, b_QT[qi]], writes=[b_QT[qi]])
        return qi

    def merge_head(h, N, pa, bpa):
        i = cnt["tmp"] % 2
        cnt["tmp"] += 1
        P.dve("tensor_tensor", out=TMP[i][:, 0:N], in0=pa[:, 0:N], in1=GA[:, h, 0:N], op=ALU.mult, reads=[bpa, b_GA[h]], writes=[b_TMP[i]])
        P.dve("tensor_tensor", out=MG[:, h, 0:N], in0=TMP[i][:, 0:N], in1=GB[:, h, 0:N], op=ALU.add, reads=[b_TMP[i], b_GB[h]], writes=[b_MG[h]])

    SBANK = (0, 1)
    OBANK = (2, 3, 4, 5)
    GBANK = [6, 7]

    def prompt_attention_factory(i0, ntile):
        def attention(N):
            nk = 2 * (i0 + ntile)
            g8 = 2 * i0
            DEFB[:] = [6, 7]
            GBANK[:] = [6, 7]

            def prep(h):
                qi = head_q(h, N, None)
                pm, bpm = psum(GBANK, "gb")
                for c in range(3):
                    kc = 128 if c < 2 else 64
                    P.pe("matmul", pm[0:1, 0:N], lhsT=KT[0:kc, c, 0:1], rhs=QT[qi][0:kc, c, 0:N], start=(c == 0), stop=(c == 2),
                         reads=[b_KT[0], b_QT[qi]], writes=[bpm])
                P.act("activation", out=QT[qi][64:65, 2, 0:N], in_=pm[0:1, 0:N], func=AF.Copy, scale=-1.0, reads=[bpm, b_QT[qi]], writes=[b_QT[qi]])
                return qi

            qis = {0: prep(0)}
            deferred = []

            for h in range(8):
                if h + 1 < 8:
                    qis[h + 1] = prep(h + 1)
                qi = qis[h]
                obk = OBANK[0:ntile] if (ntile > 2 or h % 2 == 0) else OBANK[2:2 + ntile]

                def stage_S(kt):
                    j = max(0, (kt - g8) // 2)
                    q0 = j * 128
                    e_ = (kt - g8) % 2 if kt >= g8 else None
                    pS, bpS = psum(SBANK, "sb")
                    for c in range(3):
                        kc = 128 if c < 2 else 65
                        P.pe("matmul", pS[:, q0:N], lhsT=KT[0:kc, c, kt * 128:(kt + 1) * 128], rhs=QT[qi][0:kc, c, q0:N],
                             start=(c == 0), stop=(c == 2), reads=[b_KT[kt], b_KTaug, b_QT[qi]], writes=[bpS])
                    pi = cnt["pt"] % 3
                    cnt["pt"] += 1
                    P.act("activation", out=PT[pi][:, q0:N], in_=pS[:, q0:N], func=AF.Exp, scale=SCALE, reads=[bpS], writes=[b_PT[pi]])
                    if e_ is not None:
                        P.dve("tensor_tensor", out=PT[pi][:, q0:q0 + 128], in0=PT[pi][:, q0:q0 + 128],
                              in1=masks[:, e_ * 128:(e_ + 1) * 128], op=ALU.mult, reads=[b_PT[pi], b_masks], writes=[b_PT[pi]])
                    return (kt, j, pi)

                def stage_V(kt, j, pi):
                    for jq in range(j, ntile):
                        last = g8 + 2 * jq + 1
                        P.pe("matmul", ps[obk[jq]][:, 0:257], lhsT=PT[pi][:, jq * 128:(jq + 1) * 128], rhs=VP[:, kt, 0:257],
                             start=(kt == 0), stop=(kt == last), reads=[b_PT[pi], b_VP[kt]], writes=[bps[obk[jq]]])

                pending = None
                for kt in range(nk):
                    info = stage_S(kt)
                    if pending is not None:
                        stage_V(*pending)
                    pending = info
                    if kt == min(2, nk - 1) and deferred:
                        deferred.pop()()
                stage_V(*pending)

                def epilogue(h=h, obk=obk):
                    for jq in range(ntile):
                        ob = ps[obk[jq]]; bob = bps[obk[jq]]
                        P.dve("reciprocal", RINV[:, jq:jq + 1], ob[:, 256:257], reads=[bob], writes=[b_RINV[jq]])
                        P.act("activation", out=OL[:, jq, :], in_=ob[:, 0:256], func=AF.Copy, scale=RINV[:, jq:jq + 1],
                              reads=[bob, b_RINV[jq]], writes=[b_OL[jq]])
                    pb, bpb = psum(GBANK, "gb")
                    pbb = pb[:, :].bitcast(BF16)
                    for jq in range(ntile):
                        for m in range(2):
                            P.pe("transpose", pbb[:, m * N + jq * 128:m * N + (jq + 1) * 128], OL[:, jq, m * 128:(m + 1) * 128], ident_b[:],
                                 reads=[b_OL[jq], b_ident_b], writes=[bpb])
                    P.act("activation", out=OLT[:].rearrange("p m q -> p (m q)"), in_=pbb[:, 0:2 * N], func=AF.Copy, reads=[bpb], writes=[b_OLT])
                    pa, bpa = psum(GBANK, "gb")
                    for m in range(2):
                        P.pe("matmul", pa[:, 0:N], lhsT=wuv[:, m, h, :], rhs=OLT[:, m, 0:N], start=(m == 0), stop=(m == 1),
                             reads=[b_wuv, b_OLT], writes=[bpa])
                    merge_head(h, N, pa, bpa)

                if ntile <= 2:
                    deferred.append(epilogue)
                else:
                    epilogue()
            while deferred:
                deferred.pop()()
            DEFB[:] = [0, 1, 2, 3, 4, 5, 6, 7]
        return attention

    if STAGE >= 2:
        for g in range(NOWN // NTG):
            cnt["g0"] = NTG * g
            last_g = (g == NOWN // NTG - 1)
            group(NG, NTG, xo[g * NG:(g + 1) * NG, :], modp, ("p", A2p, B2p, b_modp), g1p, b_g["g1p"], g2p, b_g["g2p"],
                  cs_feat_p.rearrange("p (a t) -> p a t", a=2)[:, :, g * NG:(g + 1) * NG], y_p[g * NG:(g + 1) * NG, :],
                  prompt_attention_factory(NTG * g, NTG), False, preloaded=(g > 0),
                  x_next=(xd if last_g else xo[(g + 1) * NG:(g + 2) * NG, :]) if STAGE >= 3 or not last_g else None,
                  n_next=(1 if last_g else NTG))
        for c8 in range(8):
            P.dma("dma_start", out=conv_p[:, c8 * 128:(c8 + 1) * 128].rearrange("t p -> p t"), in_=VLAST[:, c8, 0:2],
                  allow_slow_non_contiguous=True, reads=[b_VLAST])

    SCT = sb([128, 8, 32], name="SCT"); b_SCT = Buf()
    if STAGE >= 3:
        sct = xn[1][0:32, :]; b_sct = b_xn[1]; load(sct[:], sconv, b_sct)
        po, bpo = psum()
        for k in range(8):
            P.pe("transpose", po[:, k * 32:(k + 1) * 32], sct[0:32, k * 128:(k + 1) * 128], ident_f[0:32, 0:32],
                 reads=[b_sct, b_ident_f], writes=[bpo])
        P.dve("tensor_copy", SCT[:].rearrange("p k s -> p (k s)"), po[:, 0:256], reads=[bpo], writes=[b_SCT])

        KTN = carve([128, 3, 128]); b_KTN = Buf()
        VN = carve([128, 258]); b_VN = Buf()
        kind_f = sb([2, 128]); b_kind = Buf(); load(kind_f[:], kind_d, b_kind)
        augc = sb([2, 256]); b_augc = Buf(); load(augc[:], augc_d, b_augc)
        smask = carve([128, 8, 128]); b_smask = Buf()
        ptt = sb([128, 8], I32); b_ptt = Buf(); load(ptt[:], pt, b_ptt)
        QS = carve([128, 3, 8, 128]); b_QS = Buf()
        QP = [carve([128, 3, 128]) for _ in range(2)]; b_QP = [Buf(), Buf()]
        OTS = carve([128, 2, 8, 128]); b_OTS = Buf()
        R = 8
        KVT = [carve([128, R * 256]) for i in range(3)]; b_KVT = [Buf() for _ in range(3)]
        KRT = [carve([128, 16 * 64]) for i in range(2)]; b_KRT = [Buf() for _ in range(2)]
        KTS = [carve([128, 2, 3, 128]) for i in range(3)]; b_KTS = [Buf() for _ in range(3)]
        PS4 = [carve([128, 4, 128]) for _ in range(2)]; b_PS4 = [Buf(), Buf()]
        PN = carve([128, 128]); b_PN = Buf()
        OLS = carve([128, 256]); b_OLS = Buf()
        alias_barrier([b_KTN, b_VN, b_smask, b_QS, b_OTS, b_PN, b_OLS] + b_QP + b_KVT + b_KRT + b_KTS + b_PS4, b_KT + b_VP + [b_KTaug])
        P.dve("memset", VN[:, 256:258], 1.0, writes=[b_VN])
        P.dve("tensor_copy", KTN[64:66, 2, :], kind_f[:], reads=[b_kind], writes=[b_KTN])
        load(smask.rearrange("p a b -> p (a b)"), smask_d, b_smask, q=POOL)
        for i in range(3):
            P.dve("tensor_copy", KTS[i][64:66, :, 2, 0:64], kind_f[:, 0:1].unsqueeze(1).to_broadcast([2, 2, 64]),
                  reads=[b_kind], writes=[b_KTS[i]])
            P.dve("tensor_copy", KTS[i][64:66, :, 2, 64:128], kind_f[:, 8:9].unsqueeze(1).to_broadcast([2, 2, 64]),
                  reads=[b_kind, b_KTS[i]], writes=[b_KTS[i]])
        RIS = sb([128, 1]); b_RIS = Buf()
        VT = xn[0][0:32, :]; b_VT = b_xn[0]

        mods1 = ("s", A1s, modT[:, 0, :, 1:17], b_mods)
        mods2 = ("s", A2s, modT[:, 2, :, 1:17], b_mods)

        def sample_attention(N):
            DEFB[:] = [4, 5, 6, 7]
            GBANK[:] = [4, 5, 6, 7]
            kv_build(UT, b_UT[0], 0, cs_tok_s, ckv_s, kr_s, VN[:, 0:256], b_VN, KTN[:, :, :], b_KTN)
            for h in range(8):
                qi = head_q(h, N, None)
                P.dve("tensor_copy", QS[:, 0:2, h, :], QT[qi][:, 0:2, 0:128], reads=[b_QT[qi], b_QS], writes=[b_QS])
                P.dve("tensor_copy", QS[0:64, 2, h, :], QT[qi][0:64, 2, 0:128], reads=[b_QT[qi], b_QS], writes=[b_QS])
            def issue_kv(n):
                if n >= 128:
                    return
                jj, rbb = n // 16, n % 16
                if rbb % 2 == 0:
                    gk = n // 2
                    P.dma("indirect_dma_start", out=KRT[gk % 2][:, :], out_offset=None, in_=kr_pool,
                          in_offset=bass.IndirectOffsetOnAxis(ap=ptt[:, jj:jj + 1], axis=0),
                          element_offset=(rbb // 2) * 16 * 64, reads=[b_ptt], writes=[b_KRT[gk % 2]], q=POOL)
                P.dma("indirect_dma_start", out=KVT[n % 3][:, :], out_offset=None, in_=ckv_pool,
                      in_offset=bass.IndirectOffsetOnAxis(ap=ptt[:, jj:jj + 1], axis=0),
                      element_offset=rbb * R * 256, reads=[b_ptt], writes=[b_KVT[n % 3]], q=POOL)

            issue_kv(0)
            for j in range(8):
                qp = QP[j % 2]; bqp = b_QP[j % 2]
                for c in range(3):
                    kc = 128 if c < 2 else 64
                    P.dve("tensor_copy",
                          qp[0:kc, c, :].rearrange("p (s h t) -> p s h t", s=2, h=8),
                          QS[0:kc, c, :, 16 * j:16 * j + 16].rearrange("p h (s t) -> p s h t", t=8), reads=[b_QS, bqp], writes=[bqp])
                pm, bpm = psum(GBANK, "gb")
                for c in range(3):
                    kc = 128 if c < 2 else 64
                    P.pe("matmul", pm[0:2, 0:128], lhsT=KTN[0:kc, c, 16 * j:16 * j + 16:8], rhs=qp[0:kc, c, :],
                         start=(c == 0), stop=(c == 2), reads=[b_KTN, bqp], writes=[bpm])
                P.dve("tensor_tensor", out=augt[:], in0=pm[0:2, 0:128], in1=augc[:, 0:128], op=ALU.mult, reads=[bpm, b_augc], writes=[b_augt])
                P.dve("tensor_tensor", out=qp[64:66, 2, :], in0=augt[:], in1=augc[:, 128:256], op=ALU.add, reads=[b_augt, b_augc, bqp], writes=[bqp])
                ob = ps[OBANK[j % 2]]; bob = bps[OBANK[j % 2]]
                pS, bpS = psum(SBANK, "sb")
                for c in range(3):
                    kc = 128 if c < 2 else 66
                    P.pe("matmul", pS[:, 0:128], lhsT=KTN[0:kc, c, :], rhs=qp[0:kc, c, :], start=(c == 0), stop=(c == 2),
                         reads=[b_KTN, bqp], writes=[bpS])
                P.act("activation", out=PN[:], in_=pS[:, 0:128], func=AF.Exp, scale=SCALE, reads=[bpS], writes=[b_PN])
                P.dve("tensor_tensor", out=PN[:], in0=PN[:], in1=smask[:, j, :], op=ALU.mult, reads=[b_PN, b_smask], writes=[b_PN])
                P.pe("matmul", ob[:, 0:257], lhsT=PN[:], rhs=VN[:, 0:257], start=True, stop=False, reads=[b_PN, b_VN], writes=[bob])

                def stage_T(st):
                    n = 16 * j + st // 4
                    r0 = (st % 4) * 2
                    kvv = KVT[n % 3][:, :].rearrange("p (r f) -> p r f", f=256)
                    krv = KRT[(n // 2) % 2][:, :].rearrange("p (r f) -> p r f", f=64)
                    ti = cnt["h"] % 3
                    cnt["h"] += 1
                    pb, bpb = psum(GBANK, "gb")
                    pbb = pb[:, :].bitcast(BF16)
                    for rr_ in range(2):
                        r = r0 + rr_
                        P.pe("transpose", pbb[:, (rr_ * 2) * 128:(rr_ * 2 + 1) * 128], kvv[:, r, 0:128], ident_b[:],
                             reads=[b_KVT[n % 3], b_ident_b], writes=[bpb])
                        P.pe("transpose", pbb[:, (rr_ * 2 + 1) * 128:(rr_ * 2 + 2) * 128], kvv[:, r, 128:256], ident_b[:],
                             reads=[b_KVT[n % 3], b_ident_b], writes=[bpb])
                    rk = (n % 2) * 8 + r0
                    P.pe("transpose", pbb[:, 512:640], KRT[(n // 2) % 2][:, rk * 64:(rk + 2) * 64], ident_b[:],
                         reads=[b_KRT[(n // 2) % 2], b_ident_b], writes=[bpb])
                    pv = pbb[:, 0:512].rearrange("p (a c t) -> p a c t", a=2, c=2)
                    P.dve("tensor_copy", KTS[ti][:, :, 0:2, :], pv, reads=[bpb, b_KTS[ti]], writes=[b_KTS[ti]])
                    P.dve("tensor_copy", KTS[ti][0:64, 0, 2, :], pbb[0:64, 512:640], reads=[bpb, b_KTS[ti]], writes=[b_KTS[ti]])
                    P.dve("tensor_copy", KTS[ti][0:64, 1, 2, :], pbb[64:128, 512:640], reads=[bpb, b_KTS[ti]], writes=[b_KTS[ti]])
                    return ti

                def stage_S(st, ti, pS, bpS):
                    for rr_ in range(2):
                        col = ((st % 2) * 2 + rr_) * 128
                        for c in range(3):
                            kc = 128 if c < 2 else 66
                            P.pe("matmul", pS[:, col:col + 128], lhsT=KTS[ti][0:kc, rr_, c, :], rhs=qp[0:kc, c, :],
                                 start=(c == 0), stop=(c == 2), reads=[b_KTS[ti], bqp], writes=[bpS])

                def stage_V(q, pi):
                    n = 16 * j + q // 2
                    kvv = KVT[n % 3][:, :].rearrange("p (r f) -> p r f", f=256)
                    for r_ in range(4):
                        r = (q % 2) * 4 + r_
                        lastmm = (q == 31 and r_ == 3)
                        P.pe("matmul", ob[:, 0:256], lhsT=PS4[pi][:, r_, :], rhs=kvv[:, r, :], start=False, stop=lastmm, skip_group_check=True,
                             reads=[b_PS4[pi], b_KVT[n % 3]], writes=[bob])
                        P.pe("matmul", ob[:, 256:257], lhsT=PS4[pi][:, r_, :], rhs=ones_b[:, 0:1], start=False, stop=lastmm, skip_group_check=True,
                             reads=[b_PS4[pi], b_ones], writes=[bob])

                tis = {0: stage_T(0)}
                prev = None
                for st in range(64):
                    if st == 1 and j > 0:
                        issue_kv(16 * j + 2)
                    if st == 0 and j == 0:
                        issue_kv(1)
                        issue_kv(2)
                    if st + 1 < 64:
                        tis[st + 1] = stage_T(st + 1)
                    if st % 2 == 0:
                        pS, bpS = psum(SBANK, "sb")
                    stage_S(st, tis[st], pS, bpS)
                    if st % 2 == 1:
                        q = st // 2
                        pi = q % 2
                        P.act("activation", out=PS4[pi][:].rearrange("p a b -> p (a b)"), in_=pS[:, :], func=AF.Exp, scale=SCALE,
                              reads=[bpS], writes=[b_PS4[pi]])
                        if prev is not None:
                            stage_V(*prev)
                            if prev[0] % 2 == 1:
                                issue_kv(16 * j + prev[0] // 2 + 3)
                        prev = (q, pi)
                stage_V(*prev)
                P.dve("reciprocal", RIS[:], ob[:, 256:257], reads=[bob], writes=[b_RIS])
                P.act("activation", out=OLS[:], in_=ob[:, 0:256], func=AF.Copy, scale=RIS[:, 0:1], reads=[bob, b_RIS], writes=[b_OLS])
                pb, bpb = psum(GBANK, "gb")
                pbb = pb[:, :].bitcast(BF16)
                for m in range(2):
                    P.pe("transpose", pbb[:, m * 128:(m + 1) * 128], OLS[:, m * 128:(m + 1) * 128], ident_b[:],
                         reads=[b_OLS, b_ident_b], writes=[bpb])
                for m in range(2):
                    P.dve("tensor_copy", OTS[:, m, :, 16 * j:16 * j + 16].rearrange("p h (s t) -> p s h t", t=8),
                                                                pbb[:, m * 128:(m + 1) * 128].rearrange("p (s h t) -> p s h t", s=2, h=8),
                          reads=[bpb, b_OTS], writes=[b_OTS])
            for h in range(8):
                pa, bpa = psum(GBANK, "gb")
                for m in range(2):
                    P.pe("matmul", pa[:, 0:N], lhsT=wuv[:, m, h, :], rhs=OTS[:, m, h, :], start=(m == 0), stop=(m == 1),
                         reads=[b_wuv, b_OTS], writes=[bpa])
                merge_head(h, N, pa, bpa)
            DEFB[:] = [0, 1, 2, 3, 4, 5, 6, 7]

        augt = sb([2, 128]); b_augt = Buf()
        cnt["g0"] = 0
        group(128, 1, xd, mods1, mods2, g1s, b_g["g1s"], g2s, b_g["g2s"],
              cs_feat_s.rearrange("p (a t) -> p a t", a=2), y_s, sample_attention, True, preloaded=True)
        for hh in range(2):
            po, bpo = psum()
            for kk in range(4):
                k = hh * 4 + kk
                P.pe("transpose", po[0:32, kk * 128:(kk + 1) * 128], VLAST[:, k, :], ident_f[:], reads=[b_VLAST, b_ident_f], writes=[bpo])
            P.dve("tensor_copy", VT[:, hh * 512:(hh + 1) * 512], po[0:32, :], reads=[bpo, b_VT], writes=[b_VT])
        P.dma("dma_start", out=conv_s, in_=VT[:], reads=[b_VT])

    P.emit()
    return nc


_NC_CACHE = {}


def _rope_tables(pos):
    inv = (np.float32(10000.0) ** (-np.arange(0, 64, 2, dtype=np.float32) / np.float32(64))).astype(np.float32)
    ang = pos.astype(np.float32)[:, None] * inv[None, :]
    return np.cos(ang).astype(np.float32), np.sin(ang).astype(np.float32)


def kernel(x_prompt, x_sample, cache_ckv, cache_krope, state_conv, page_table, c_prompt, c_sample,
           w_ada, b_ada, g_attn, w_in, g_q, w_q_b, g_kv, w_kv_b, conv_w, w_o, g_mlp, w_1, w_2, g_final):
    f32 = np.float32
    A = lambda a: np.ascontiguousarray(np.asarray(a))
    x_prompt = np.asarray(x_prompt, f32); x_sample = np.asarray(x_sample, f32)
    if "nc" not in _NC_CACHE:
        _NC_CACHE["nc"] = build_program()
    nc = _NC_CACHE["nc"]

    ident = np.eye(128, dtype=f32)
    tri = (np.arange(128)[:, None] <= np.arange(128)[None, :]).astype(f32)
    cos_p, sin_p = _rope_tables(np.arange(SEQ))
    cs_tok_p = np.concatenate([cos_p, cos_p, -sin_p, sin_p], axis=1)
    pos_s = 8192 + (np.arange(128) % 8)
    cos_s, sin_s = _rope_tables(pos_s)
    cs_tok_s = np.concatenate([cos_s, cos_s, -sin_s, sin_s], axis=1)
    cs_feat_s = np.concatenate([np.concatenate([cos_s, cos_s], 1).T, np.concatenate([-sin_s, sin_s], 1).T], axis=1)
    smask = np.zeros((128, 8, 128), f32)
    kk_s = np.arange(128) // 8; kk_t = np.arange(128) % 8
    qc = np.arange(128); q_sl = qc // 64; q_t = qc % 8
    for j in range(8):
        smask[:, j, :] = ((kk_s[:, None] == (2 * j + q_sl)[None, :]) & (kk_t[:, None] <= q_t[None, :])).astype(f32)
    augc = np.zeros((2, 256), f32)
    augc[0, 0:64] = -1.0; augc[1, 64:128] = -1.0
    augc[0, 128 + 64:256] = -BIG; augc[1, 128:128 + 64] = -BIG
    kind = np.zeros((2, 128), f32)
    kind[0, :] = ((np.arange(128) // 8) % 2 == 0); kind[1, :] = ((np.arange(128) // 8) % 2 == 1)

    w_in0 = np.asarray(w_in[0], f32)
    q_a_w = w_in0[:, 0:384]; ckv_w = w_in0[:, 384:640]; kr_w = w_in0[:, 640:704]
    bg_w = w_in0[:, 704:1728]; cg_w = w_in0[:, 1728:2752]; xin_w = w_in0[:, 2752:3776]
    ga_w = w_in0[:, 3776:4800]; gb_w = w_in0[:, 4800:5824]
    kr_sw = np.concatenate([kr_w[:, 32:64], kr_w[:, 0:32]], axis=1)
    w_kvin = A(np.concatenate([ckv_w, kr_w, kr_sw], axis=1))
    parts = [q_a_w]
    for fc in range(8):
        sl = slice(fc * 128, (fc + 1) * 128)
        parts += [bg_w[:, sl], cg_w[:, sl], xin_w[:, sl], ga_w[:, sl], gb_w[:, sl]]
    w_main = A(np.concatenate(parts, axis=1))
    wqb = np.asarray(w_q_b[0], f32).reshape(384, 8, 192)
    w_q = A(np.concatenate([wqb[:, :, 0:128], wqb[:, :, 128:192], wqb[:, :, 160:192], wqb[:, :, 128:160]], axis=2).reshape(384, 2048))
    wkvb = np.asarray(w_kv_b[0], f32).reshape(256, 8, 256)
    w_ukT = A(wkvb[:, :, 0:128].transpose(2, 1, 0).reshape(128, 2048))
    w_uv = A(wkvb[:, :, 128:256].reshape(256, 1024))
    convT = A(np.asarray(conv_w[0], f32).T.reshape(8, 128, 3).transpose(1, 0, 2).reshape(128, 24))
    fm = lambda v, n: A(np.asarray(v, f32).reshape(n, 128).T)
    badaT = fm(b_ada[0], 48)
    bada_g = A(np.stack([np.asarray(b_ada[0], f32)[2048:3072], np.asarray(b_ada[0], f32)[5120:6144]]))
    shared = {
        "w_ada": A(np.asarray(w_ada[0], f32)), "badaT": badaT, "bada_g": bada_g,
        "gattnT": fm(g_attn[0], 8), "gmlpT": fm(g_mlp[0], 8), "gqT": fm(g_q[0], 3),
        "gkv": A(np.asarray(g_kv[0], f32)), "gfin": A(np.asarray(g_final, f32)),
        "w_kvin": w_kvin, "w_main": w_main, "w_q": w_q, "w_ukT": w_ukT, "w_uv": w_uv, "convT": convT,
        "w_o": A(np.asarray(w_o[0], f32)), "w_1": A(np.asarray(w_1[0], f32)), "w_2": A(np.asarray(w_2[0], f32)),
        "ident": ident, "cs_tok_p": A(cs_tok_p), "cs_tok_s": A(cs_tok_s), "cs_feat_s": A(cs_feat_s),
        "smask": A(smask.reshape(128, 1024)), "augc": augc, "kind": kind,
        "ckv_pool": np.asarray(cache_ckv[0], f32).reshape(NPOOL, 128 * 256),
        "kr_pool": np.asarray(cache_krope[0], f32).reshape(NPOOL, 128 * 64),
    }
    page_table = np.asarray(page_table, np.int32)
    in_maps = []
    own_rows = []
    for c in range(8):
        b, half = c // 2, c % 2
        tiles = [2 * i + half for i in range(NOWN)]
        rows = np.concatenate([np.arange(t * 128, (t + 1) * 128) for t in tiles])
        own_rows.append(rows)
        xb = x_prompt[b]
        xh = np.zeros((32, D), f32)
        hm = np.ones((128, 16), f32)
        for i, t in enumerate(tiles):
            if t == 0:
                hm[:, i] = 0.0
            else:
                xh[2 * i:2 * i + 2] = xb[t * 128 - 2:t * 128]
        masks = np.concatenate([tri, np.zeros((128, 128), f32)], 1) if half == 0 else np.concatenate([np.ones((128, 128), f32), tri], 1)
        cosq = np.concatenate([cos_p[rows], cos_p[rows]], 1).T
        sinq = np.concatenate([-sin_p[rows], sin_p[rows]], 1).T
        cs_feat_p = np.concatenate([cosq, sinq], axis=1)
        seqs = np.arange(16 * c, 16 * c + 16)
        ptc = np.zeros((128, 8), np.int32)
        for j in range(8):
            ptc[0:64, j] = page_table[seqs[2 * j]]
            ptc[64:128, j] = page_table[seqs[2 * j + 1]]
        m = dict(shared)
        m.update({
            "xs": A(xb), "xo": A(xb[rows]), "xh": xh, "xd": A(x_sample[seqs].reshape(128, D)),
            "cc": A(np.concatenate([np.asarray(c_prompt, f32)[b:b + 1], np.asarray(c_sample, f32)[seqs]], 0)),
            "pt": ptc, "sconv": A(np.asarray(state_conv[0], f32)[seqs].reshape(32, D)),
            "masks": A(masks), "hmask": hm, "cs_feat_p": A(cs_feat_p),
        })
        in_maps.append(m)

    res = run_bass_kernel_spmd(nc, in_maps, core_ids=list(range(8)))
    R = res.results
    y_prompt = np.zeros((4, SEQ, D), f32)
    y_sample = np.zeros((128, 8, D), f32)
    ckv_pr = np.zeros((1, 4, SEQ, 256), f32); kr_pr = np.zeros((1, 4, SEQ, 64), f32); conv_pr = np.zeros((1, 4, 2, D), f32)
    ckv_sm = np.zeros((1, 128, 8, 256), f32); kr_sm = np.zeros((1, 128, 8, 64), f32); conv_sm = np.zeros((1, 128, 2, D), f32)
    for c in range(8):
        b, half = c // 2, c % 2
        r = R[c]
        y_prompt[b, own_rows[c]] = r["y_p"]
        y_sample[16 * c:16 * c + 16] = r["y_s"].reshape(16, 8, D)
        if half == 0:
            ckv_pr[0, b] = r["ckv_p"]; kr_pr[0, b] = r["kr_p"]
        else:
            conv_pr[0, b] = r["conv_p"]
        ckv_sm[0, 16 * c:16 * c + 16] = r["ckv_s"].reshape(16, 8, 256)
        kr_sm[0, 16 * c:16 * c + 16] = r["kr_s"].reshape(16, 8, 64)
        conv_sm[0, 16 * c:16 * c + 16] = r["conv_s"].reshape(16, 2, D)
    return (y_prompt, y_sample, ckv_pr, kr_pr, conv_pr, ckv_sm, kr_sm, conv_sm)
```

```python
import contextlib
import numpy as np
import concourse.bass as bass
import concourse.mybir as mybir
from concourse.bass_utils import run_bass_kernel_spmd

F32 = mybir.dt.float32
BF16 = mybir.dt.bfloat16
I32 = mybir.dt.int32
AF = mybir.ActivationFunctionType
ALU = mybir.AluOpType

PE, ACT, DVE, POOL, SP = "pe", "act", "dve", "pool", "sp"
COMPUTE = (PE, ACT, DVE, POOL)
NQ = 8

D = 1024
SEQ = 4096
NT = 32
NOWN = 16
NPOOL = 10240
EPS = 1e-6
SCALE = float((128 + 64) ** -0.5)
BIG = 30000.0
STAGE = 3
NTG = 2
NG = 128 * NTG


class Buf:
    __slots__ = ("name", "writers", "readers")

    def __init__(self, name=""):
        self.name = name
        self.writers = {}
        self.readers = {}


class Op:
    __slots__ = ("eng", "fn", "deps", "idx", "is_dma", "signal", "sem", "val", "qslot")

    def __init__(self, eng, fn, is_dma):
        self.eng = eng
        self.fn = fn
        self.is_dma = is_dma
        self.deps = []
        self.signal = False
        self.sem = None
        self.val = None
        self.qslot = 0


class Prog:
    def __init__(self, nc):
        self.nc = nc
        self.ops = {e: [] for e in (PE, ACT, DVE, POOL, SP)}
        self.ndma = {e: 0 for e in (PE, ACT, DVE, POOL, SP)}

    def _key(self, op):
        return ("dma", id(op)) if op.is_dma else op.eng

    def op(self, eng, meth, args, kwargs, reads=(), writes=(), dma=False):
        fn = (lambda e: getattr(e, meth)(*args, **kwargs))
        o = Op(eng, fn, dma)
        o.idx = len(self.ops[eng])
        deps = []
        for b in reads:
            for w in b.writers.values():
                deps.append(w)
        for b in writes:
            for w in b.writers.values():
                if w.is_dma or dma or w.eng != eng:
                    deps.append(w)
            for r in b.readers.values():
                if r.is_dma or dma or r.eng != eng:
                    deps.append(r)
        o.deps = [d for d in deps if not (d.eng == PE and eng == PE and not d.is_dma and not dma)]
        for b in reads:
            b.readers[self._key(o)] = o
        for b in writes:
            b.writers = {self._key(o): o}
            b.readers = {}
        if dma:
            o.qslot = self.ndma[eng]
            self.ndma[eng] += 1
        self.ops[eng].append(o)
        return o

    def pe(self, meth, *args, reads=(), writes=(), **kw):
        return self.op(PE, meth, args, kw, reads, writes)

    def act(self, meth, *args, reads=(), writes=(), **kw):
        return self.op(ACT, meth, args, kw, reads, writes)

    def dve(self, meth, *args, reads=(), writes=(), **kw):
        return self.op(DVE, meth, args, kw, reads, writes)

    def pool(self, meth, *args, reads=(), writes=(), **kw):
        return self.op(POOL, meth, args, kw, reads, writes)

    def dma(self, meth, *args, reads=(), writes=(), q=SP, **kw):
        return self.op(q, meth, args, kw, reads, writes, dma=True)

    def emit(self):
        nc = self.nc
        for e in self.ops:
            for o in self.ops[e]:
                for d in o.deps:
                    d.signal = True
        with contextlib.ExitStack() as st:
            csem = {e: st.enter_context(nc.semaphore("s_" + e)) for e in COMPUTE}
            qsem = {}
            for e in self.ops:
                if self.ndma[e]:
                    qsem[e] = [st.enter_context(nc.semaphore("q_%s_%d" % (e, i))) for i in range(NQ)]
            for e in self.ops:
                c = 0
                for o in self.ops[e]:
                    if o.is_dma:
                        o.sem = qsem[e][o.qslot % NQ]
                        o.val = 16 * (o.qslot // NQ + 1)
                    elif o.signal:
                        c += 1
                        o.sem, o.val = csem[e], c
            block = st.enter_context(nc.Block())
            handles = {PE: block.tensor, ACT: block.scalar, DVE: block.vector, POOL: block.gpsimd, SP: block.sync}

            def make(e):
                def body(eng):
                    known = {}
                    for o in self.ops[e]:
                        waits = {}
                        for d in o.deps:
                            kk = id(d.sem)
                            if kk not in waits or waits[kk][1] < d.val:
                                waits[kk] = (d.sem, d.val)
                        if o.is_dma and o.qslot >= NQ:
                            s = qsem[e][o.qslot % NQ]
                            v = 16 * (o.qslot // NQ)
                            kk = id(s)
                            if kk not in waits or waits[kk][1] < v:
                                waits[kk] = (s, v)
                        for kk, (s, v) in waits.items():
                            if known.get(kk, 0) >= v:
                                continue
                            eng.wait_ge(s, v)
                            known[kk] = v
                        inst = o.fn(eng)
                        if o.is_dma:
                            inst.then_inc(o.sem, 16)
                        elif o.signal:
                            inst.then_inc(o.sem, 1)
                    if e == SP:
                        for qe in qsem:
                            n = self.ndma[qe]
                            for slot in range(min(NQ, n)):
                                last = ((n - 1 - slot) // NQ) + 1
                                eng.wait_ge(qsem[qe][slot], 16 * last)
                return body

            for e in (SP, POOL, ACT, DVE, PE):
                if self.ops[e] or e == SP:
                    handles[e](make(e))


def build_program():
    nc = bass.Bass("TRN2", target_bir_lowering=False)
    P = Prog(nc)

    def din(name, shape, dt=F32):
        return nc.dram_tensor(name, list(shape), dt, kind="ExternalInput").ap()

    def dout(name, shape):
        return nc.dram_tensor(name, list(shape), F32, kind="ExternalOutput").ap()

    _n = [0]

    def sb(shape, dt=F32, name=None):
        _n[0] += 1
        return nc.alloc_sbuf_tensor(name or ("t%d" % _n[0]), list(shape), dt)

    xs = din("xs", [SEQ, D])
    xo = din("xo", [NOWN * 128, D])
    xh = din("xh", [32, D])
    xd = din("xd", [128, D])
    cc = din("cc", [17, D])
    pt = din("pt", [128, 8], I32)
    ckv_pool = din("ckv_pool", [NPOOL, 128 * 256])
    kr_pool = din("kr_pool", [NPOOL, 128 * 64])
    sconv = din("sconv", [32, D])
    w_ada = din("w_ada", [D, 6 * D])
    badaT = din("badaT", [128, 48])
    bada_g = din("bada_g", [2, D])
    gattnT = din("gattnT", [128, 8])
    gmlpT = din("gmlpT", [128, 8])
    gqT = din("gqT", [128, 3])
    gkv = din("gkv", [256])
    gfin = din("gfin", [D])
    w_kvin = din("w_kvin", [D, 384])
    w_main = din("w_main", [D, 5504])
    w_q = din("w_q", [384, 2048])
    w_ukT = din("w_ukT", [128, 8 * 256])
    w_uv = din("w_uv", [256, 8 * 128])
    convT = din("convT", [128, 8 * 3])
    w_o = din("w_o", [D, D])
    w_1 = din("w_1", [D, 4 * D])
    w_2 = din("w_2", [4 * D, D])
    ident_d = din("ident", [128, 128])
    masks_d = din("masks", [128, 2 * 128])
    hmask_d = din("hmask", [128, 16])
    cs_tok_p = din("cs_tok_p", [SEQ, 128])
    cs_tok_s = din("cs_tok_s", [128, 128])
    cs_feat_p = din("cs_feat_p", [64, 2 * NOWN * 128])
    cs_feat_s = din("cs_feat_s", [64, 2 * 128])
    smask_d = din("smask", [128, 8 * 128])
    augc_d = din("augc", [2, 2 * 128])
    kind_d = din("kind", [2, 128])

    y_p = dout("y_p", [NOWN * 128, D])
    y_s = dout("y_s", [128, D])
    ckv_p = dout("ckv_p", [SEQ, 256])
    kr_p = dout("kr_p", [SEQ, 64])
    conv_p = dout("conv_p", [2, D])
    ckv_s = dout("ckv_s", [128, 256])
    kr_s = dout("kr_s", [128, 64])
    conv_s = dout("conv_s", [32, D])

    ps = [nc.alloc_psum_tensor("ps%d" % i, [128, 512], F32) for i in range(8)]
    bps = [Buf("ps%d" % i) for i in range(8)]
    rr = {"g": 0}

    DEFB = [0, 1, 2, 3, 4, 5, 6, 7]

    def psum(group=DEFB, key="g"):
        i = group[rr.get(key, 0) % len(group)]
        rr[key] = rr.get(key, 0) + 1
        return ps[i], bps[i]

    def load(dst_ap, src_ap, buf, q=SP, reads=()):
        P.dma("dma_start", out=dst_ap, in_=src_ap, reads=list(reads), writes=[buf], q=q)

    ident_f = sb([128, 128]); b_ident_f = Buf()
    ident_b = sb([128, 128], BF16); b_ident_b = Buf()
    ones_b = sb([128, 128], BF16); b_ones = Buf()
    load(ident_f[:], ident_d, b_ident_f)
    P.dve("tensor_copy", ident_b[:], ident_f[:], reads=[b_ident_f], writes=[b_ident_b])
    P.dve("memset", ones_b[:], 1.0, writes=[b_ones])
    masks_f = sb([128, 256]); b_masks_f = Buf()
    masks = sb([128, 256], BF16); b_masks = Buf()
    load(masks_f[:], masks_d, b_masks_f)
    P.dve("tensor_copy", masks[:], masks_f[:], reads=[b_masks_f], writes=[b_masks])
    hmask = sb([128, 16]); b_hmask = Buf()
    load(hmask[:], hmask_d, b_hmask)
    gattn_t = sb([128, 8]); b_gattn = Buf(); load(gattn_t[:], gattnT, b_gattn)
    gmlp_t = sb([128, 8]); b_gmlp = Buf(); load(gmlp_t[:], gmlpT, b_gmlp)
    gq_t = sb([128, 3]); b_gq = Buf(); load(gq_t[:], gqT, b_gq)
    bada_t = sb([128, 48]); b_bada = Buf(); load(bada_t[:], badaT, b_bada)
    convw = sb([128, 24]); b_convw = Buf(); load(convw[:], convT, b_convw)
    gkv_bc = sb([128, 256]); b_gkv = Buf(); load(gkv_bc[:], gkv.partition_broadcast(128), b_gkv)
    gfin_bc = sb([128, D]); b_gfin = Buf(); load(gfin_bc[:], gfin.partition_broadcast(128), b_gfin)
    xn = [sb([128, D]) for _ in range(2)]; b_xn = [Buf(), Buf()]
    YO = xn; b_YO = b_xn
    XG = sb([128, NTG, D], name="XG"); b_XG = [Buf("XG%d" % t) for t in range(NTG)]
    UT = sb([128, 8, NG], BF16, name="UT"); b_UT = [Buf("UT%d" % t) for t in range(NTG)]
    REG = sb([128, 32, NG], BF16, name="REG"); b_REG = [Buf() for _ in range(32)]
    BAD = [XG[:, 0, :], XG[:, 1, :]]
    b_badag, b_badag1 = b_XG[0], b_XG[1]
    load(BAD[0], bada_g[0].partition_broadcast(128), b_badag)
    load(BAD[1], bada_g[1].partition_broadcast(128), b_badag1)

    wkv = sb([128, 8, 384], BF16); b_wkv = Buf()
    load(wkv[:], w_kvin.rearrange("(k p) c -> p k c", p=128), b_wkv, q=POOL)
    wq = sb([128, 3, 2048], BF16); b_wq = Buf()
    load(wq[:], w_q.rearrange("(k p) c -> p k c", p=128), b_wq, q=POOL)
    wuk = sb([128, 8, 256], BF16); b_wuk = Buf()
    load(wuk[:].rearrange("p h c -> p (h c)"), w_ukT, b_wuk, q=POOL)
    wuv = sb([128, 2, 8, 128], BF16); b_wuv = Buf()
    load(wuv[:].rearrange("p m h v -> p m (h v)"), w_uv.rearrange("(m p) c -> p m c", p=128), b_wuv, q=POOL)

    WUNIT = 1024
    NUNIT = 18
    wring = sb([128, NUNIT * WUNIT], BF16, name="wring")
    bunits = [Buf("wu%d" % i) for i in range(NUNIT)]
    wstate = {"pos": 0}

    def wload(src2d, kch, cols, reads=()):
        n = kch * cols
        nu = (n + WUNIT - 1) // WUNIT
        if wstate["pos"] + nu > NUNIT:
            wstate["pos"] = 0
        u0 = wstate["pos"]
        wstate["pos"] += nu
        bufs = bunits[u0:u0 + nu]
        view = wring[:, u0 * WUNIT:u0 * WUNIT + n].rearrange("p (k c) -> p k c", c=cols)
        P.dma("dma_start", out=view, in_=src2d.rearrange("(k p) c -> p k c", p=128),
              reads=list(reads), writes=bufs, q=POOL)
        return view, bufs

    w_main_b = nc.dram_tensor("w_main_b", [D, 5504], BF16, kind="Internal").ap()
    w_o_b = nc.dram_tensor("w_o_b", [D, D], BF16, kind="Internal").ap()
    w_1_b = nc.dram_tensor("w_1_b", [D, 4 * D], BF16, kind="Internal").ap()
    w_2_b = nc.dram_tensor("w_2_b", [4 * D, D], BF16, kind="Internal").ap()
    b_wconv = {"main": [], "o": [], "1": [], "2": []}

    def convert_weights():
        for (src, dst, key) in ((w_main, w_main_b, "main"), (w_o, w_o_b, "o"), (w_1, w_1_b, "1"), (w_2, w_2_b, "2")):
            rows, cols = src.shape
            for r0 in range(0, rows, 128):
                for c0 in range(0, cols, 2048):
                    c1 = min(cols, c0 + 2048)
                    bb = Buf()
                    b_wconv[key].append(bb)
                    P.dma("dma_start", out=dst[r0:r0 + 128, c0:c1], in_=src[r0:r0 + 128, c0:c1],
                          writes=[bb], q=POOL)
                    yield None

    conv_gen = convert_weights()

    def convert_some(n):
        for _ in range(n):
            try:
                next(conv_gen)
            except StopIteration:
                return

    RA = sb([128, 3 * SEQ + NT * 258 + 8], BF16, name="RA")
    KT = RA[:, 0:3 * SEQ].rearrange("p (c t) -> p c t", c=3); b_KT = [Buf("KT%d" % t) for t in range(NT)]
    VP = RA[:, 3 * SEQ:3 * SEQ + NT * 258].rearrange("p (t f) -> p t f", f=258); b_VP = [Buf("VP%d" % t) for t in range(NT)]
    ra_off = [0]

    def carve(shape):
        n = 1
        for d_ in shape[1:]:
            n *= d_
        v = RA[:, ra_off[0]:ra_off[0] + n]
        ra_off[0] += n + (n % 2)
        assert ra_off[0] <= 3 * SEQ + NT * 258
        if len(shape) == 2:
            return v
        if len(shape) == 3:
            return v.rearrange("p (a b) -> p a b", b=shape[2])
        return v.rearrange("p (a b c) -> p a b c", b=shape[2], c=shape[3])

    def alias_barrier(new_bufs, old_bufs):
        merged = {}
        for ob in old_bufs:
            for dct in (ob.readers, ob.writers):
                for k, o in dct.items():
                    if k not in merged or merged[k].idx < o.idx:
                        merged[k] = o
        for nb in new_bufs:
            nb.readers.update(merged)
    b_KTaug = Buf()
    P.dve("memset", KT[64:65, 2, :], 1.0, writes=[b_KTaug])
    P.dve("memset", VP[:, :, 256:258], 1.0, writes=b_VP)

    modT = sb([128, 4, 8, 17]); b_modT = Buf()
    A1p = sb([128, 8]); B1p = sb([128, 8]); A2p = sb([128, 8]); B2p = sb([128, 8]); b_modp = Buf()
    A1s = sb([128, 8, 16]); A2s = sb([128, 8, 16]); b_mods = Buf()
    g1p = sb([128, D]); g2p = sb([128, D]); g1s = sb([128, D]); g2s = sb([128, D])
    b_g = {"g1p": Buf(), "g2p": Buf(), "g1s": Buf(), "g2s": Buf()}

    cct = xn[0][0:17, :]; b_cc = b_xn[0]; load(cct[:], cc, b_cc)
    sil = xn[1][0:17, :]; b_sil = b_xn[1]
    P.act("activation", out=sil[:], in_=cct[:], func=AF.Silu, reads=[b_cc], writes=[b_sil])
    silT = sb([128, 8, 17], BF16); b_silT = Buf()
    pt_, bp_ = psum()
    for k in range(8):
        P.pe("transpose", pt_[:, k * 17:(k + 1) * 17], sil[0:17, k * 128:(k + 1) * 128], ident_f[0:17, 0:17],
             reads=[b_sil, b_ident_f], writes=[bp_])
    P.dve("tensor_copy", silT[:].rearrange("p k s -> p (k s)"), pt_[:, 0:8 * 17], reads=[bp_], writes=[b_silT])
    nreg = 1024 // NG
    lhs_p = REG[:, 16:16 + nreg, :].rearrange("p a b -> p (a b)").rearrange("p (k c) -> p k c", c=128)
    lhs_s = REG[:, 16 + nreg:16 + 2 * nreg, :].rearrange("p a b -> p (a b)").rearrange("p (k c) -> p k c", c=128)
    b_lhs = Buf()
    b_lhs_all = [b_lhs] + b_REG[16:16 + 2 * nreg]
    P.dve("tensor_copy", lhs_p[:], silT[:, :, 0:1].to_broadcast([128, 8, 128]), reads=[b_silT], writes=b_lhs_all)
    for k in range(8):
        P.dve("tensor_copy", lhs_s[:, k, :].rearrange("p (s t) -> p s t", t=8),
                                           silT[:, k, 1:17].unsqueeze(2).to_broadcast([128, 16, 8]),
              reads=[b_silT, b_lhs], writes=b_lhs_all)

    for j in range(6):
        wv, wb = wload(w_ada[:, j * D:(j + 1) * D], 8, D)
        if j in (0, 1, 3, 4):
            jj = (0, 1, None, 2, 3)[j]
            po, bpo = psum()
            for m in range(8):
                for k in range(8):
                    P.pe("matmul", po[:, m * 17:(m + 1) * 17], lhsT=wv[:, k, m * 128:(m + 1) * 128],
                                                      rhs=silT[:, k, :], start=(k == 0), stop=(k == 7),
                         reads=wb + [b_silT], writes=[bpo])
            for m in range(8):
                P.dve("tensor_scalar", modT[:, jj, m, :], po[:, m * 17:(m + 1) * 17],
                                                                 bada_t[:, j * 8 + m:j * 8 + m + 1], None, op0=ALU.add,
                      reads=[bpo, b_bada, b_modT], writes=[b_modT])
        else:
            gi = 0 if j == 2 else 1
            for (lh, gt, bn) in ((lhs_p, g1p if gi == 0 else g2p, "g%dp" % (gi + 1)),
                                 (lhs_s, g1s if gi == 0 else g2s, "g%ds" % (gi + 1))):
                for n in range(2):
                    po, bpo = psum()
                    for k in range(8):
                        P.pe("matmul", po[:, :], lhsT=lh[:, k, :], rhs=wv[:, k, n * 512:(n + 1) * 512],
                                                                      start=(k == 0), stop=(k == 7),
                             reads=wb + b_lhs_all, writes=[bpo])
                    P.dve("tensor_tensor", out=gt[:, n * 512:(n + 1) * 512], in0=po[:, :],
                                                                              in1=BAD[gi][:, n * 512:(n + 1) * 512], op=ALU.add,
                          reads=[bpo, b_badag, b_badag1, b_g[bn]], writes=[b_g[bn]])
    convert_some(8)
    P.dve("scalar_tensor_tensor", out=A1p[:], in0=modT[:, 1, :, 0], scalar=1.0, in1=gattn_t[:], op0=ALU.add, op1=ALU.mult,
          reads=[b_modT, b_gattn], writes=[b_modp])
    P.dve("scalar_tensor_tensor", out=A2p[:], in0=modT[:, 3, :, 0], scalar=1.0, in1=gmlp_t[:], op0=ALU.add, op1=ALU.mult,
          reads=[b_modT, b_gmlp, b_modp], writes=[b_modp])
    P.dve("tensor_copy", B1p[:], modT[:, 0, :, 0], reads=[b_modT, b_modp], writes=[b_modp])
    P.dve("tensor_copy", B2p[:], modT[:, 2, :, 0], reads=[b_modT, b_modp], writes=[b_modp])
    P.dve("scalar_tensor_tensor", out=A1s[:], in0=modT[:, 1, :, 1:17], scalar=1.0,
                                           in1=gattn_t[:].unsqueeze(2).to_broadcast([128, 8, 16]), op0=ALU.add, op1=ALU.mult,
          reads=[b_modT, b_gattn], writes=[b_mods])
    P.dve("scalar_tensor_tensor", out=A2s[:], in0=modT[:, 3, :, 1:17], scalar=1.0,
                                           in1=gmlp_t[:].unsqueeze(2).to_broadcast([128, 8, 16]), op0=ALU.add, op1=ALU.mult,
          reads=[b_modT, b_gmlp, b_mods], writes=[b_mods])

    nj = D // NG
    JUNK = {"g": (REG[:, 0:nj, :].rearrange("p a b -> p (a b)"), b_REG[0:nj]),
            "k": (REG[:, 24:24 + nj, :].rearrange("p a b -> p (a b)"), b_REG[24:24 + nj])}
    TMP = [sb([128, 512]) for _ in range(2)]; b_TMP = [Buf(), Buf()]
    st_ss = sb([128, 8]); st_rs = sb([128, 8]); b_st = [Buf() for _ in range(8)]
    stc = {"n": 0}

    def rstd_of(src_ap, src_bufs, npart, ncols, inv_n, jsel="g"):
        i = stc["n"] % 8
        stc["n"] += 1
        junk, jb = JUNK[jsel]
        P.act("activation", out=junk[0:npart, 0:ncols], in_=src_ap, func=AF.Square, accum_out=st_ss[0:npart, i:i + 1],
              reads=list(src_bufs), writes=list(jb) + [b_st[i]])
        P.act("activation", out=st_rs[0:npart, i:i + 1], in_=st_ss[0:npart, i:i + 1], func=AF.Ln, scale=inv_n, bias=EPS,
              reads=[b_st[i]], writes=[b_st[i]])
        P.act("activation", out=st_rs[0:npart, i:i + 1], in_=st_rs[0:npart, i:i + 1], func=AF.Exp, scale=-0.5,
              reads=[b_st[i]], writes=[b_st[i]])
        return st_rs[0:npart, i:i + 1], b_st[i]

    xnc = {"n": 0}

    def make_uT(x_ap, x_bufs, npart, uT, uT_buf, col0, mod, jsel="g"):
        rs, rb = rstd_of(x_ap, x_bufs, npart, D, 1.0 / D, jsel)
        i = xnc["n"] % 2
        xnc["n"] += 1
        xnb = xn[i][:, :].bitcast(BF16)
        P.dve("tensor_scalar", xnb[0:npart, 0:D], x_ap, rs, None, op0=ALU.mult,
              reads=list(x_bufs) + [rb], writes=[b_xn[i]])
        for half in range(2):
            po_, bpo = psum()
            po = po_[:, :].bitcast(BF16)
            for kk in range(4):
                k = half * 4 + kk
                P.pe("transpose", po[:, kk * 128:kk * 128 + npart], xnb[0:npart, k * 128:(k + 1) * 128],
                                                              ident_b[0:npart, 0:npart],
                     reads=[b_xn[i], b_ident_b], writes=[bpo])
            for kk in range(4):
                k = half * 4 + kk
                if mod[0] == "p":
                    P.dve("tensor_scalar", uT[:, k, col0:col0 + npart], po[:, kk * 128:kk * 128 + npart],
                                                                       mod[1][:, k:k + 1], mod[2][:, k:k + 1], op0=ALU.mult, op1=ALU.add,
                          reads=[bpo, mod[3]], writes=[uT_buf])
                else:
                    tmp = sb_tmp_s
                    P.dve("tensor_tensor", out=tmp[:].rearrange("p (s t) -> p s t", t=8),
                                                                       in0=po[:, kk * 128:(kk + 1) * 128].rearrange("p (s t) -> p s t", t=8),
                                                                       in1=mod[1][:, k, :].unsqueeze(2).to_broadcast([128, 16, 8]), op=ALU.mult,
                          reads=[bpo, mod[3]], writes=[b_tmp_s])
                    P.dve("tensor_tensor", out=uT[:, k, col0:col0 + 128].rearrange("p (s t) -> p s t", t=8),
                                                         in0=tmp[:].rearrange("p (s t) -> p s t", t=8),
                                                         in1=mod[2][:, k, :].unsqueeze(2).to_broadcast([128, 16, 8]), op=ALU.add,
                          reads=[b_tmp_s, mod[3]], writes=[uT_buf])

    sb_tmp_s = TMP[0][:, 0:128]; b_tmp_s = b_TMP[0]

    kvf = [sb([128, 320]) for _ in range(2)]; b_kvf = [Buf(), Buf()]
    krb = [sb([128, 64], BF16) for _ in range(2)]; b_krb = [Buf(), Buf()]
    cst = [sb([128, 128]) for _ in range(2)]; b_cst = [Buf(), Buf()]
    krt = [sb([128, 128]) for _ in range(2)]; b_krt = [Buf(), Buf()]
    kvc = {"n": 0}

    def kv_build(uT, uT_buf, col0, cs_src, out_ckv, out_kr, vdst, vbuf, ktdst, ktbuf, split=False):
        i = kvc["n"] % 2
        kvc["n"] += 1
        load(cst[i][:], cs_src, b_cst[i])
        po, bpo = psum()
        for k in range(8):
            P.pe("matmul", po[:, 0:384], lhsT=uT[:, k, col0:col0 + 128], rhs=wkv[:, k, :], start=(k == 0), stop=(k == 7),
                 reads=[uT_buf, b_wkv], writes=[bpo])
        rs, rb = rstd_of(po[:, 0:256], [bpo], 128, 256, 1.0 / 256, "k")
        P.dve("scalar_tensor_tensor", out=kvf[i][:, 0:256], in0=po[:, 0:256], scalar=rs, in1=gkv_bc[:], op0=ALU.mult, op1=ALU.mult,
              reads=[bpo, rb, b_gkv], writes=[b_kvf[i]])
        P.dve("tensor_tensor", out=krt[i][:], in0=po[:, 256:384], in1=cst[i][:], op=ALU.mult,
              reads=[bpo, b_cst[i]], writes=[b_krt[i]])
        P.dve("tensor_tensor", out=kvf[i][:, 256:320], in0=krt[i][:, 0:64], in1=krt[i][:, 64:128], op=ALU.add,
              reads=[b_krt[i], b_kvf[i]], writes=[b_kvf[i]])
        P.dma("dma_start", out=out_ckv, in_=kvf[i][:, 0:256], reads=[b_kvf[i]])
        P.dma("dma_start", out=out_kr, in_=kvf[i][:, 256:320], reads=[b_kvf[i]])
        if split:
            return lambda: kv_build2(i, vdst, vbuf, ktdst, ktbuf)
        kv_build2(i, vdst, vbuf, ktdst, ktbuf)

    def kv_build2(i, vdst, vbuf, ktdst, ktbuf):
        P.act("activation", out=vdst, in_=kvf[i][:, 0:256], func=AF.Copy, reads=[b_kvf[i]], writes=[vbuf])
        P.act("activation", out=krb[i][:], in_=kvf[i][:, 256:320], func=AF.Copy, reads=[b_kvf[i]], writes=[b_krb[i]])
        pb, bpb = psum()
        pbb = pb[:, :].bitcast(BF16)
        P.pe("transpose", pbb[:, 0:128], vdst[:, 0:128], ident_b[:], reads=[vbuf, b_ident_b], writes=[bpb])
        P.pe("transpose", pbb[:, 128:256], vdst[:, 128:256], ident_b[:], reads=[vbuf, b_ident_b], writes=[bpb])
        P.pe("transpose", pbb[0:64, 256:384], krb[i][:], ident_b[:], reads=[b_krb[i], b_ident_b], writes=[bpb])
        P.act("activation", out=ktdst[:, 0:2, :], in_=pbb[:, 0:256].rearrange("p (c t) -> p c t", t=128), func=AF.Copy,
              reads=[bpb], writes=[ktbuf])
        P.act("activation", out=ktdst[0:64, 2, :], in_=pbb[0:64, 256:384], func=AF.Copy, reads=[bpb, ktbuf], writes=[ktbuf])

    uK = [UT[:, :, 0:128], UT[:, :, 128:256]]; b_uK = [b_UT[0], b_UT[1]]
    xk = [XG[:, 0, :], XG[:, 1, :]]; b_xk = [b_XG[0], b_XG[1]]
    modp = ("p", A1p, B1p, b_modp)
    NPRE = NT

    def stage_a(T):
        i = T % 2
        make_uT(xk[i], [b_xk[i]], 128, uK[i], b_uK[i], 0, modp, "k")


    load(xk[0], xs[0:128, :], b_xk[0])
    load(xk[1], xs[128:256, :], b_xk[1])
    stage_a(0)
    for T in range(NPRE):
        i = T % 2
        if T + 2 < NPRE:
            load(xk[i], xs[(T + 2) * 128:(T + 3) * 128, :], b_xk[i])
        part2 = kv_build(uK[i], b_uK[i], 0, cs_tok_p[T * 128:(T + 1) * 128, :], ckv_p[T * 128:(T + 1) * 128, :], kr_p[T * 128:(T + 1) * 128, :],
                         VP[:, T, 0:256], b_VP[T], KT[:, :, T * 128:(T + 1) * 128], b_KT[T], split=True)
        if T + 1 < NPRE:
            stage_a(T + 1)
        part2()
        convert_some(3)
    convert_some(1000)
    uH = sb([128, 8, 32], BF16); b_uH = Buf()
    load(xk[0][0:32, :], xh, b_xk[0])
    make_uT(xk[0][0:32, :], [b_xk[0]], 32, uH, b_uH, 0, modp, "k")

    GA = REG[:, 0:8, :]; b_GA = b_REG[0:8]
    GB = REG[:, 8:16, :]; b_GB = b_REG[8:16]
    MG = REG[:, 16:24, :]; b_MG = b_REG[16:24]
    HT = REG; b_HT = b_REG
    QA = sb([128, 3, NG], name="QA"); b_QA = Buf()
    SQ = sb([128, 3, NG], BF16, name="SQ"); b_SQ = Buf()
    RQ = sb([128, NG], name="RQ"); b_RQ = Buf()
    QN = sb([128, 3, NG], BF16, name="QN"); b_QN = Buf()
    QNOPE = [sb([128, NG], BF16) for _ in range(2)]; b_QNOPE = [Buf(), Buf()]
    QT = [sb([128, 3, NG], BF16) for _ in range(2)]; b_QT = [Buf(), Buf()]
    CSQ = sb([64, 2, NG], name="CSQ"); b_CSQ = Buf()
    RT = [sb([64, NG]) for _ in range(2)]; b_RT = [Buf(), Buf()]
    PT = [sb([128, NG], BF16) for _ in range(3)]; b_PT = [Buf() for _ in range(3)]
    OL = sb([128, NTG, 256], BF16, name="OL"); b_OL = [Buf() for _ in range(NTG)]
    OLT = sb([128, 2, NG], BF16, name="OLT"); b_OLT = Buf()
    RINV = sb([128, 4]); b_RINV = [Buf() for _ in range(4)]
    CG = [sb([128, NG + 8]) for _ in range(2)]; b_CG = [Buf(), Buf()]
    VPAD = [sb([128, max(NTG * 130, 160)]) for _ in range(2)]; b_VPAD = [Buf(), Buf()]
    ZC = [sb([128, NG]) for _ in range(2)]; b_ZC = [Buf(), Buf()]
    SG = [sb([128, NG]) for _ in range(2)]; b_SG = [Buf(), Buf()]
    VLAST = sb([128, 8, 32], name="VLAST"); b_VLAST = Buf()
    cnt = {"cg": 0, "tmp": 0, "pt": 0, "yo": 0, "h": 0}

    def group(N, ntile, x_src, mod1, mod2, g1t, bg1, g2t, bg2, csq_src, y_dst, attention, sample, preloaded=False, x_next=None, n_next=0):
        NW = N
        for t in range(ntile):
            if not preloaded:
                load(XG[:, t, :], x_src[t * 128:(t + 1) * 128, :], b_XG[t])
            make_uT(XG[:, t, :], [b_XG[t]], 128, UT, b_UT[t], t * 128, mod1)
        load(CSQ[:, :, 0:N], csq_src, b_CSQ)
        wv, wb = wload(w_main_b[:, 0:384], 8, 384, b_wconv["main"])
        for c in range(3):
            po, bpo = psum()
            for k in range(8):
                P.pe("matmul", po[:, 0:N], lhsT=wv[:, k, c * 128:(c + 1) * 128], rhs=UT[:, k, 0:N],
                                                        start=(k == 0), stop=(k == 7), reads=wb + b_UT[0:ntile], writes=[bpo])
            P.act("activation", out=QA[:, c, 0:N], in_=po[:, 0:N], func=AF.Copy, reads=[bpo, b_QA], writes=[b_QA])
            P.act("activation", out=SQ[:, c, 0:N], in_=po[:, 0:N], func=AF.Square, reads=[bpo, b_SQ], writes=[b_SQ])
        po, bpo = psum()
        for c in range(3):
            P.pe("matmul", po[:, 0:N], lhsT=ones_b[:], rhs=SQ[:, c, 0:N], start=(c == 0), stop=(c == 2),
                 reads=[b_ones, b_SQ], writes=[bpo])
        P.act("activation", out=RQ[:, 0:N], in_=po[:, 0:N], func=AF.Ln, scale=1.0 / 384, bias=EPS, reads=[bpo], writes=[b_RQ])
        P.act("activation", out=RQ[:, 0:N], in_=RQ[:, 0:N], func=AF.Exp, scale=-0.5, reads=[b_RQ], writes=[b_RQ])
        for c in range(3):
            P.dve("scalar_tensor_tensor", out=QN[:, c, 0:N], in0=QA[:, c, 0:N], scalar=gq_t[:, c:c + 1], in1=RQ[:, 0:N],
                                                        op0=ALU.mult, op1=ALU.mult, reads=[b_QA, b_gq, b_RQ, b_QN], writes=[b_QN])
        for fc in range(8):
            wv, wb = wload(w_main_b[:, 384 + fc * 640:384 + (fc + 1) * 640], 8, 640, b_wconv["main"])

            def proj(mc, po, bpo, rhs, ncols, c0=0):
                for k in range(8):
                    P.pe("matmul", po[:, c0:c0 + ncols], lhsT=wv[:, k, mc * 128:(mc + 1) * 128], rhs=rhs(k),
                                                 start=(k == 0), stop=(k == 7), reads=wb + [b_uH] + b_UT[0:ntile], writes=[bpo])

            ci = cnt["cg"] % 2
            cnt["cg"] += 1
            pcg, bpcg = psum()
            proj(1, pcg, bpcg, lambda k: UT[:, k, 0:N], N)
            P.act("activation", out=CG[ci][:, 0:N], in_=pcg[:, 0:N], func=AF.Copy, reads=[bpcg], writes=[b_CG[ci]])
            pxi, bpxi = psum()
            proj(2, pxi, bpxi, lambda k: UT[:, k, 0:N], N)
            vp = VPAD[ci][:, 0:ntile * 130].rearrange("p (t j) -> p t j", j=130)
            if not sample:
                ph, bph = psum()
                nh = 2 * ntile
                proj(1, ph, bph, lambda k: uH[:, k, cnt["g0"] * 2:cnt["g0"] * 2 + nh], nh, 0)
                proj(2, ph, bph, lambda k: uH[:, k, cnt["g0"] * 2:cnt["g0"] * 2 + nh], nh, 16)
                P.act("activation", out=CG[ci][:, NG:NG + nh], in_=ph[:, 0:nh], func=AF.Copy, reads=[bph, b_CG[ci]], writes=[b_CG[ci]])
                P.dve("tensor_tensor", out=vp[:, 0:ntile, 2:130], in0=pxi[:, 0:N].rearrange("p (t j) -> p t j", j=128),
                                                in1=CG[ci][:, 0:N].rearrange("p (t j) -> p t j", j=128), op=ALU.mult,
                      reads=[bpxi, b_CG[ci]], writes=[b_VPAD[ci]])
                P.dve("tensor_tensor", out=vp[:, 0:ntile, 0:2], in0=ph[:, 16:16 + nh].rearrange("p (t j) -> p t j", j=2),
                                                in1=CG[ci][:, NG:NG + nh].rearrange("p (t j) -> p t j", j=2), op=ALU.mult,
                      reads=[bph, b_CG[ci], b_VPAD[ci]], writes=[b_VPAD[ci]])
                g0 = cnt["g0"]
                P.dve("tensor_tensor", out=vp[:, 0:ntile, 0:2], in0=vp[:, 0:ntile, 0:2],
                                                in1=hmask[:, g0:g0 + ntile].unsqueeze(2).to_broadcast([128, ntile, 2]), op=ALU.mult,
                      reads=[b_VPAD[ci], b_hmask], writes=[b_VPAD[ci]])
                vin = [vp[:, 0:ntile, j:j + 128] for j in range(3)]
                zv = ZC[ci][:, 0:N].rearrange("p (t j) -> p t j", j=128)
                if cnt["g0"] + ntile == NOWN:
                    P.dve("tensor_copy", VLAST[:, fc, 0:2], vp[:, ntile - 1, 128:130], reads=[b_VPAD[ci], b_VLAST], writes=[b_VLAST])
            else:
                vps = VPAD[ci][:, 0:160].rearrange("p (s j) -> p s j", j=10)
                P.dve("tensor_tensor", out=vps[:, :, 2:10], in0=pxi[:, 0:128].rearrange("p (s j) -> p s j", j=8),
                                                in1=CG[ci][:, 0:128].rearrange("p (s j) -> p s j", j=8), op=ALU.mult,
                      reads=[bpxi, b_CG[ci]], writes=[b_VPAD[ci]])
                P.dve("tensor_copy", vps[:, :, 0:2], SCT[:, fc, :].rearrange("p (s j) -> p s j", j=2),
                      reads=[b_SCT, b_VPAD[ci]], writes=[b_VPAD[ci]])
                vin = [vps[:, :, j:j + 8] for j in range(3)]
                zv = ZC[ci][:, 0:128].rearrange("p (s j) -> p s j", j=8)
                P.dve("tensor_copy", VLAST[:, fc, :].rearrange("p (s j) -> p s j", j=2), vps[:, :, 8:10],
                      reads=[b_VPAD[ci], b_VLAST], writes=[b_VLAST])
            P.dve("tensor_scalar", zv, vin[0], convw[:, fc * 3:fc * 3 + 1], None, op0=ALU.mult,
                  reads=[b_VPAD[ci], b_convw], writes=[b_ZC[ci]])
            for j in (1, 2):
                P.dve("scalar_tensor_tensor", out=zv, in0=vin[j], scalar=convw[:, fc * 3 + j:fc * 3 + j + 1], in1=zv,
                                                            op0=ALU.mult, op1=ALU.add, reads=[b_VPAD[ci], b_convw, b_ZC[ci]], writes=[b_ZC[ci]])
            pbg, bpbg = psum()
            proj(0, pbg, bpbg, lambda k: UT[:, k, 0:N], N)
            P.dve("tensor_tensor", out=ZC[ci][:, 0:N], in0=pbg[:, 0:N], in1=ZC[ci][:, 0:N], op=ALU.mult,
                  reads=[bpbg, b_ZC[ci]], writes=[b_ZC[ci]])
            pga, bpga = psum()
            proj(3, pga, bpga, lambda k: UT[:, k, 0:N], N)
            P.act("activation", out=GA[:, fc, 0:N], in_=pga[:, 0:N], func=AF.Sigmoid, reads=[bpga], writes=[b_GA[fc]])
            pgb, bpgb = psum()
            proj(4, pgb, bpgb, lambda k: UT[:, k, 0:N], N)
            P.act("activation", out=SG[ci][:, 0:N], in_=pgb[:, 0:N], func=AF.Sigmoid, reads=[bpgb], writes=[b_SG[ci]])
            P.dve("tensor_tensor", out=GB[:, fc, 0:N], in0=SG[ci][:, 0:N], in1=ZC[ci][:, 0:N], op=ALU.mult,
                  reads=[b_SG[ci], b_ZC[ci]], writes=[b_GB[fc]])

        attention(N)

        wv, wb = wload(w_o_b, 8, D, b_wconv["o"])
        for t in range(ntile):
            for n in range(2):
                po, bpo = psum()
                for k in range(8):
                    P.pe("matmul", po[:, :], lhsT=MG[:, k, t * 128:(t + 1) * 128], rhs=wv[:, k, n * 512:(n + 1) * 512],
                                                                 start=(k == 0), stop=(k == 7), reads=wb + [b_MG[k]], writes=[bpo])
                i = cnt["tmp"] % 2
                cnt["tmp"] += 1
                P.dve("tensor_tensor", out=TMP[i][:], in0=po[:, :], in1=g1t[:, n * 512:(n + 1) * 512], op=ALU.mult,
                      reads=[bpo, bg1], writes=[b_TMP[i]])
                P.dve("tensor_tensor", out=XG[:, t, n * 512:(n + 1) * 512], in0=XG[:, t, n * 512:(n + 1) * 512],
                                                               in1=TMP[i][:], op=ALU.add, reads=[b_TMP[i], b_XG[t]], writes=[b_XG[t]])
        for t in range(ntile):
            make_uT(XG[:, t, :], [b_XG[t]], 128, UT, b_UT[t], t * 128, mod2)
        for j2 in range(8):
            wv, wb = wload(w_1_b[:, j2 * 512:(j2 + 1) * 512], 8, 512, b_wconv["1"])
            for m in range(4):
                po, bpo = psum()
                for k in range(8):
                    P.pe("matmul", po[:, 0:N], lhsT=wv[:, k, m * 128:(m + 1) * 128], rhs=UT[:, k, 0:N],
                                                            start=(k == 0), stop=(k == 7), reads=wb + b_UT[0:ntile], writes=[bpo])
                hc = j2 * 4 + m
                P.act("activation", out=TMP[hc % 2][:, 0:N], in_=po[:, 0:N], func=AF.Relu,
                      reads=[bpo], writes=[b_TMP[hc % 2]])
                P.dve("tensor_tensor", out=HT[:, hc, 0:N], in0=TMP[hc % 2][:, 0:N], in1=TMP[hc % 2][:, 0:N], op=ALU.mult,
                      reads=[b_TMP[hc % 2]], writes=[b_HT[hc]])
        for j in range(4):
            wva, wba = wload(w_2_b[0:2048, j * 256:(j + 1) * 256], 16, 256, b_wconv["2"])
            wvb, wbb = wload(w_2_b[2048:4096, j * 256:(j + 1) * 256], 16, 256, b_wconv["2"])
            for t in range(ntile):
                po, bpo = psum()
                for k in range(32):
                    wv, wb = (wva, wba) if k < 16 else (wvb, wbb)
                    P.pe("matmul", po[:, 0:256], lhsT=HT[:, k, t * 128:(t + 1) * 128], rhs=wv[:, k % 16, :],
                         start=(k == 0), stop=(k == 31), reads=wb + [b_HT[k]], writes=[bpo])
                i2 = cnt["tmp"] % 2
                cnt["tmp"] += 1
                P.dve("tensor_tensor", out=TMP[i2][:, 0:256], in0=po[:, 0:256], in1=g2t[:, j * 256:(j + 1) * 256], op=ALU.mult,
                      reads=[bpo, bg2], writes=[b_TMP[i2]])
                P.dve("tensor_tensor", out=XG[:, t, j * 256:(j + 1) * 256], in0=XG[:, t, j * 256:(j + 1) * 256],
                                                                 in1=TMP[i2][:, 0:256], op=ALU.add, reads=[b_TMP[i2], b_XG[t]], writes=[b_XG[t]])
        for t in range(ntile):
            rs, rb = rstd_of(XG[:, t, :], [b_XG[t]], 128, D, 1.0 / D)
            i = cnt["yo"] % 2
            cnt["yo"] += 1
            P.dve("scalar_tensor_tensor", out=YO[i][:], in0=XG[:, t, :], scalar=rs, in1=gfin_bc[:], op0=ALU.mult, op1=ALU.mult,
                  reads=[b_XG[t], rb, b_gfin], writes=[b_YO[i]])
            if x_next is not None and t < n_next:
                load(XG[:, t, :], x_next[t * 128:(t + 1) * 128, :], b_XG[t])
            P.dma("dma_start", out=y_dst[t * 128:(t + 1) * 128, :], in_=YO[i][:], reads=[b_YO[i]])

    def head_q(h, N, ktcol_lhsT):
        qi = h % 2
        po, bpo = psum()
        for c in range(3):
            P.pe("matmul", po[:, 0:N], lhsT=wq[:, c, h * 256:h * 256 + 128], rhs=QN[:, c, 0:N], start=(c == 0), stop=(c == 2),
                 reads=[b_wq, b_QN], writes=[bpo])
        P.act("activation", out=QNOPE[qi][:, 0:N], in_=po[:, 0:N], func=AF.Copy, reads=[bpo], writes=[b_QNOPE[qi]])
        pr, bpr = psum()
        for c in range(3):
            P.pe("matmul", pr[0:64, 0:N], lhsT=wq[:, c, h * 256 + 128:h * 256 + 192], rhs=QN[:, c, 0:N], start=(c == 0), stop=(c == 2),
                 reads=[b_wq, b_QN], writes=[bpr])
        pw, bpw = psum()
        for c in range(3):
            P.pe("matmul", pw[0:64, 0:N], lhsT=wq[:, c, h * 256 + 192:h * 256 + 256], rhs=QN[:, c, 0:N], start=(c == 0), stop=(c == 2),
                 reads=[b_wq, b_QN], writes=[bpw])
        P.dve("tensor_tensor", out=RT[0][:, 0:N], in0=pr[0:64, 0:N], in1=CSQ[:, 0, 0:N], op=ALU.mult, reads=[bpr, b_CSQ], writes=[b_RT[0]])
        P.dve("tensor_tensor", out=RT[1][:, 0:N], in0=pw[0:64, 0:N], in1=CSQ[:, 1, 0:N], op=ALU.mult, reads=[bpw, b_CSQ], writes=[b_RT[1]])
        P.dve("tensor_tensor", out=QT[qi][0:64, 2, 0:N], in0=RT[0][:, 0:N], in1=RT[1][:, 0:N], op=ALU.add,
              reads=[b_RT[0], b_RT[1], b_QT[qi]], writes=[b_QT[qi]])
        for m in range(2):
            pl, bpl = psum()
            P.pe("matmul", pl[:, 0:N], lhsT=wuk[:, h, m * 128:(m + 1) * 128], rhs=QNOPE[qi][:, 0:N], start=True, stop=True,
                 reads=[b_wuk, b_QNOPE[qi]], writes=[bpl])
            P.act("activation", out=QT[qi][:, m, 0:N], in_=pl[:, 0:N], func=AF.Copy, reads=[bpl, b_QT[qi]], writes=[b_QT[qi]])
        return qi

    def merge_head(h, N, pa, bpa):
        i = cnt["tmp"] % 2
        cnt["tmp"] += 1
        P.dve("tensor_tensor", out=TMP[i][:, 0:N], in0=pa[:, 0:N], in1=GA[:, h, 0:N], op=ALU.mult, reads=[bpa, b_GA[h]], writes=[b_TMP[i]])
        P.dve("tensor_tensor", out=MG[:, h, 0:N], in0=TMP[i][:, 0:N], in1=GB[:, h, 0:N], op=ALU.add, reads=[b_TMP[i], b_GB[h]], writes=[b_MG[h]])

    SBANK = (0, 1)
    OBANK = (2, 3, 4, 5)
    GBANK = [6, 7]

    def prompt_attention_factory(i0, ntile):
        def attention(N):
            nk = 2 * (i0 + ntile)
            g8 = 2 * i0
            DEFB[:] = [6, 7]
            GBANK[:] = [6, 7]

            def prep(h):
                qi = head_q(h, N, None)
                pm, bpm = psum(GBANK, "gb")
                for c in range(3):
                    kc = 128 if c < 2 else 64
                    P.pe("matmul", pm[0:1, 0:N], lhsT=KT[0:kc, c, 0:1], rhs=QT[qi][0:kc, c, 0:N], start=(c == 0), stop=(c == 2),
                         reads=[b_KT[0], b_QT[qi]], writes=[bpm])
                P.act("activation", out=QT[qi][64:65, 2, 0:N], in_=pm[0:1, 0:N], func=AF.Copy, scale=-1.0, reads=[bpm, b_QT[qi]], writes=[b_QT[qi]])
                return qi

            qis = {0: prep(0)}
            deferred = []

            for h in range(8):
                if h + 1 < 8:
                    qis[h + 1] = prep(h + 1)
                qi = qis[h]
                obk = OBANK[0:ntile] if (ntile > 2 or h % 2 == 0) else OBANK[2:2 + ntile]

                def stage_S(kt):
                    j = max(0, (kt - g8) // 2)
                    q0 = j * 128
                    e_ = (kt - g8) % 2 if kt >= g8 else None
                    pS, bpS = psum(SBANK, "sb")
                    for c in range(3):
                        kc = 128 if c < 2 else 65
                        P.pe("matmul", pS[:, q0:N], lhsT=KT[0:kc, c, kt * 128:(kt + 1) * 128], rhs=QT[qi][0:kc, c, q0:N],
                             start=(c == 0), stop=(c == 2), reads=[b_KT[kt], b_KTaug, b_QT[qi]], writes=[bpS])
                    pi = cnt["pt"] % 3
                    cnt["pt"] += 1
                    P.act("activation", out=PT[pi][:, q0:N], in_=pS[:, q0:N], func=AF.Exp, scale=SCALE, reads=[bpS], writes=[b_PT[pi]])
                    if e_ is not None:
                        P.dve("tensor_tensor", out=PT[pi][:, q0:q0 + 128], in0=PT[pi][:, q0:q0 + 128],
                              in1=masks[:, e_ * 128:(e_ + 1) * 128], op=ALU.mult, reads=[b_PT[pi], b_masks], writes=[b_PT[pi]])
                    return (kt, j, pi)

                def stage_V(kt, j, pi):
                    for jq in range(j, ntile):
                        last = g8 + 2 * jq + 1
                        P.pe("matmul", ps[obk[jq]][:, 0:257], lhsT=PT[pi][:, jq * 128:(jq + 1) * 128], rhs=VP[:, kt, 0:257],
                             start=(kt == 0), stop=(kt == last), reads=[b_PT[pi], b_VP[kt]], writes=[bps[obk[jq]]])

                pending = None
                for kt in range(nk):
                    info = stage_S(kt)
                    if pending is not None:
                        stage_V(*pending)
                    pending = info
                    if kt == min(2, nk - 1) and deferred:
                        deferred.pop()()
                stage_V(*pending)

                def epilogue(h=h, obk=obk):
                    for jq in range(ntile):
                        ob = ps[obk[jq]]; bob = bps[obk[jq]]
                        P.dve("reciprocal", RINV[:, jq:jq + 1], ob[:, 256:257], reads=[bob], writes=[b_RINV[jq]])
                        P.act("activation", out=OL[:, jq, :], in_=ob[:, 0:256], func=AF.Copy, scale=RINV[:, jq:jq + 1],
                              reads=[bob, b_RINV[jq]], writes=[b_OL[jq]])
                    pb, bpb = psum(GBANK, "gb")
                    pbb = pb[:, :].bitcast(BF16)
                    for jq in range(ntile):
                        for m in range(2):
                            P.pe("transpose", pbb[:, m * N + jq * 128:m * N + (jq + 1) * 128], OL[:, jq, m * 128:(m + 1) * 128], ident_b[:],
                                 reads=[b_OL[jq], b_ident_b], writes=[bpb])
                    P.act("activation", out=OLT[:].rearrange("p m q -> p (m q)"), in_=pbb[:, 0:2 * N], func=AF.Copy, reads=[bpb], writes=[b_OLT])
                    pa, bpa = psum(GBANK, "gb")
                    for m in range(2):
                        P.pe("matmul", pa[:, 0:N], lhsT=wuv[:, m, h, :], rhs=OLT[:, m, 0:N], start=(m == 0), stop=(m == 1),
                             reads=[b_wuv, b_OLT], writes=[bpa])
                    merge_head(h, N, pa, bpa)

                if ntile <= 2:
                    deferred.append(epilogue)
                else:
                    epilogue()
            while deferred:
                deferred.pop()()
            DEFB[:] = [0, 1, 2, 3, 4, 5, 6, 7]
        return attention

    if STAGE >= 2:
        for g in range(NOWN // NTG):
            cnt["g0"] = NTG * g
            last_g = (g == NOWN // NTG - 1)
            group(NG, NTG, xo[g * NG:(g + 1) * NG, :], modp, ("p", A2p, B2p, b_modp), g1p, b_g["g1p"], g2p, b_g["g2p"],
                  cs_feat_p.rearrange("p (a t) -> p a t", a=2)[:, :, g * NG:(g + 1) * NG], y_p[g * NG:(g + 1) * NG, :],
                  prompt_attention_factory(NTG * g, NTG), False, preloaded=(g > 0),
                  x_next=(xd if last_g else xo[(g + 1) * NG:(g + 2) * NG, :]) if STAGE >= 3 or not last_g else None,
                  n_next=(1 if last_g else NTG))
        for c8 in range(8):
            P.dma("dma_start", out=conv_p[:, c8 * 128:(c8 + 1) * 128].rearrange("t p -> p t"), in_=VLAST[:, c8, 0:2],
                  allow_slow_non_contiguous=True, reads=[b_VLAST])

    SCT = sb([128, 8, 32], name="SCT"); b_SCT = Buf()
    if STAGE >= 3:
        sct = xn[1][0:32, :]; b_sct = b_xn[1]; load(sct[:], sconv, b_sct)
        po, bpo = psum()
        for k in range(8):
            P.pe("transpose", po[:, k * 32:(k + 1) * 32], sct[0:32, k * 128:(k + 1) * 128], ident_f[0:32, 0:32],
                 reads=[b_sct, b_ident_f], writes=[bpo])
        P.dve("tensor_copy", SCT[:].rearrange("p k s -> p (k s)"), po[:, 0:256], reads=[bpo], writes=[b_SCT])

        KTN = carve([128, 3, 128]); b_KTN = Buf()
        VN = carve([128, 258]); b_VN = Buf()
        kind_f = sb([2, 128]); b_kind = Buf(); load(kind_f[:], kind_d, b_kind)
        augc = sb([2, 256]); b_augc = Buf(); load(augc[:], augc_d, b_augc)
        smask = carve([128, 8, 128]); b_smask = Buf()
        ptt = sb([128, 8], I32); b_ptt = Buf(); load(ptt[:], pt, b_ptt)
        QS = carve([128, 3, 8, 128]); b_QS = Buf()
        QP = [carve([128, 3, 128]) for _ in range(2)]; b_QP = [Buf(), Buf()]
        OTS = carve([128, 2, 8, 128]); b_OTS = Buf()
        R = 8
        KVT = [carve([128, R * 256]) for i in range(3)]; b_KVT = [Buf() for _ in range(3)]
        KRT = [carve([128, 16 * 64]) for i in range(2)]; b_KRT = [Buf() for _ in range(2)]
        KTS = [carve([128, 2, 3, 128]) for i in range(3)]; b_KTS = [Buf() for _ in range(3)]
        PS4 = [carve([128, 4, 128]) for _ in range(2)]; b_PS4 = [Buf(), Buf()]
        PN = carve([128, 128]); b_PN = Buf()
        OLS = carve([128, 256]); b_OLS = Buf()
        alias_barrier([b_KTN, b_VN, b_smask, b_QS, b_OTS, b_PN, b_OLS] + b_QP + b_KVT + b_KRT + b_KTS + b_PS4, b_KT + b_VP + [b_KTaug])
        P.dve("memset", VN[:, 256:258], 1.0, writes=[b_VN])
        P.dve("tensor_copy", KTN[64:66, 2, :], kind_f[:], reads=[b_kind], writes=[b_KTN])
        load(smask.rearrange("p a b -> p (a b)"), smask_d, b_smask, q=POOL)
        for i in range(3):
            P.dve("tensor_copy", KTS[i][64:66, :, 2, 0:64], kind_f[:, 0:1].unsqueeze(1).to_broadcast([2, 2, 64]),
                  reads=[b_kind], writes=[b_KTS[i]])
            P.dve("tensor_copy", KTS[i][64:66, :, 2, 64:128], kind_f[:, 8:9].unsqueeze(1).to_broadcast([2, 2, 64]),
                  reads=[b_kind, b_KTS[i]], writes=[b_KTS[i]])
        RIS = sb([128, 1]); b_RIS = Buf()
        VT = xn[0][0:32, :]; b_VT = b_xn[0]

        mods1 = ("s", A1s, modT[:, 0, :, 1:17], b_mods)
        mods2 = ("s", A2s, modT[:, 2, :, 1:17], b_mods)

        def sample_attention(N):
            DEFB[:] = [4, 5, 6, 7]
            GBANK[:] = [4, 5, 6, 7]
            kv_build(UT, b_UT[0], 0, cs_tok_s, ckv_s, kr_s, VN[:, 0:256], b_VN, KTN[:, :, :], b_KTN)
            for h in range(8):
                qi = head_q(h, N, None)
                P.dve("tensor_copy", QS[:, 0:2, h, :], QT[qi][:, 0:2, 0:128], reads=[b_QT[qi], b_QS], writes=[b_QS])
                P.dve("tensor_copy", QS[0:64, 2, h, :], QT[qi][0:64, 2, 0:128], reads=[b_QT[qi], b_QS], writes=[b_QS])
            def issue_kv(n):
                if n >= 128:
                    return
                jj, rbb = n // 16, n % 16
                if rbb % 2 == 0:
                    gk = n // 2
                    P.dma("indirect_dma_start", out=KRT[gk % 2][:, :], out_offset=None, in_=kr_pool,
                          in_offset=bass.IndirectOffsetOnAxis(ap=ptt[:, jj:jj + 1], axis=0),
                          element_offset=(rbb // 2) * 16 * 64, reads=[b_ptt], writes=[b_KRT[gk % 2]], q=POOL)
                P.dma("indirect_dma_start", out=KVT[n % 3][:, :], out_offset=None, in_=ckv_pool,
                      in_offset=bass.IndirectOffsetOnAxis(ap=ptt[:, jj:jj + 1], axis=0),
                      element_offset=rbb * R * 256, reads=[b_ptt], writes=[b_KVT[n % 3]], q=POOL)

            issue_kv(0)
            for j in range(8):
                qp = QP[j % 2]; bqp = b_QP[j % 2]
                for c in range(3):
                    kc = 128 if c < 2 else 64
                    P.dve("tensor_copy",
                          qp[0:kc, c, :].rearrange("p (s h t) -> p s h t", s=2, h=8),
                          QS[0:kc, c, :, 16 * j:16 * j + 16].rearrange("p h (s t) -> p s h t", t=8), reads=[b_QS, bqp], writes=[bqp])
                pm, bpm = psum(GBANK, "gb")
                for c in range(3):
                    kc = 128 if c < 2 else 64
                    P.pe("matmul", pm[0:2, 0:128], lhsT=KTN[0:kc, c, 16 * j:16 * j + 16:8], rhs=qp[0:kc, c, :],
                         start=(c == 0), stop=(c == 2), reads=[b_KTN, bqp], writes=[bpm])
                P.dve("tensor_tensor", out=augt[:], in0=pm[0:2, 0:128], in1=augc[:, 0:128], op=ALU.mult, reads=[bpm, b_augc], writes=[b_augt])
                P.dve("tensor_tensor", out=qp[64:66, 2, :], in0=augt[:], in1=augc[:, 128:256], op=ALU.add, reads=[b_augt, b_augc, bqp], writes=[bqp])
                ob = ps[OBANK[j % 2]]; bob = bps[OBANK[j % 2]]
                pS, bpS = psum(SBANK, "sb")
                for c in range(3):
                    kc = 128 if c < 2 else 66
                    P.pe("matmul", pS[:, 0:128], lhsT=KTN[0:kc, c, :], rhs=qp[0:kc, c, :], start=(c == 0), stop=(c == 2),
                         reads=[b_KTN, bqp], writes=[bpS])
                P.act("activation", out=PN[:], in_=pS[:, 0:128], func=AF.Exp, scale=SCALE, reads=[bpS], writes=[b_PN])
                P.dve("tensor_tensor", out=PN[:], in0=PN[:], in1=smask[:, j, :], op=ALU.mult, reads=[b_PN, b_smask], writes=[b_PN])
                P.pe("matmul", ob[:, 0:257], lhsT=PN[:], rhs=VN[:, 0:257], start=True, stop=False, reads=[b_PN, b_VN], writes=[bob])

                def stage_T(st):
                    n = 16 * j + st // 4
                    r0 = (st % 4) * 2
                    kvv = KVT[n % 3][:, :].rearrange("p (r f) -> p r f", f=256)
                    krv = KRT[(n // 2) % 2][:, :].rearrange("p (r f) -> p r f", f=64)
                    ti = cnt["h"] % 3
                    cnt["h"] += 1
                    pb, bpb = psum(GBANK, "gb")
                    pbb = pb[:, :].bitcast(BF16)
                    for rr_ in range(2):
                        r = r0 + rr_
                        P.pe("transpose", pbb[:, (rr_ * 2) * 128:(rr_ * 2 + 1) * 128], kvv[:, r, 0:128], ident_b[:],
                             reads=[b_KVT[n % 3], b_ident_b], writes=[bpb])
                        P.pe("transpose", pbb[:, (rr_ * 2 + 1) * 128:(rr_ * 2 + 2) * 128], kvv[:, r, 128:256], ident_b[:],
                             reads=[b_KVT[n % 3], b_ident_b], writes=[bpb])
                    rk = (n % 2) * 8 + r0
                    P.pe("transpose", pbb[:, 512:640], KRT[(n // 2) % 2][:, rk * 64:(rk + 2) * 64], ident_b[:],
                         reads=[b_KRT[(n // 2) % 2], b_ident_b], writes=[bpb])
                    pv = pbb[:, 0:512].rearrange("p (a c t) -> p a c t", a=2, c=2)
                    P.dve("tensor_copy", KTS[ti][:, :, 0:2, :], pv, reads=[bpb, b_KTS[ti]], writes=[b_KTS[ti]])
                    P.dve("tensor_copy", KTS[ti][0:64, 0, 2, :], pbb[0:64, 512:640], reads=[bpb, b_KTS[ti]], writes=[b_KTS[ti]])
                    P.dve("tensor_copy", KTS[ti][0:64, 1, 2, :], pbb[64:128, 512:640], reads=[bpb, b_KTS[ti]], writes=[b_KTS[ti]])
                    return ti

                def stage_S(st, ti, pS, bpS):
                    for rr_ in range(2):
                        col = ((st % 2) * 2 + rr_) * 128
                        for c in range(3):
                            kc = 128 if c < 2 else 66
                            P.pe("matmul", pS[:, col:col + 128], lhsT=KTS[ti][0:kc, rr_, c, :], rhs=qp[0:kc, c, :],
                                 start=(c == 0), stop=(c == 2), reads=[b_KTS[ti], bqp], writes=[bpS])

                def stage_V(q, pi):
                    n = 16 * j + q // 2
                    kvv = KVT[n % 3][:, :].rearrange("p (r f) -> p r f", f=256)
                    for r_ in range(4):
                        r = (q % 2) * 4 + r_
                        lastmm = (q == 31 and r_ == 3)
                        P.pe("matmul", ob[:, 0:256], lhsT=PS4[pi][:, r_, :], rhs=kvv[:, r, :], start=False, stop=lastmm, skip_group_check=True,
                             reads=[b_PS4[pi], b_KVT[n % 3]], writes=[bob])
                        P.pe("matmul", ob[:, 256:257], lhsT=PS4[pi][:, r_, :], rhs=ones_b[:, 0:1], start=False, stop=lastmm, skip_group_check=True,
                             reads=[b_PS4[pi], b_ones], writes=[bob])

                tis = {0: stage_T(0)}
                prev = None
                for st in range(64):
                    if st == 1 and j > 0:
                        issue_kv(16 * j + 2)
                    if st == 0 and j == 0:
                        issue_kv(1)
                        issue_kv(2)
                    if st + 1 < 64:
                        tis[st + 1] = stage_T(st + 1)
                    if st % 2 == 0:
                        pS, bpS = psum(SBANK, "sb")
                    stage_S(st, tis[st], pS, bpS)
                    if st % 2 == 1:
                        q = st // 2
                        pi = q % 2
                        P.act("activation", out=PS4[pi][:].rearrange("p a b -> p (a b)"), in_=pS[:, :], func=AF.Exp, scale=SCALE,
                              reads=[bpS], writes=[b_PS4[pi]])
                        if prev is not None:
                            stage_V(*prev)
                            if prev[0] % 2 == 1:
                                issue_kv(16 * j + prev[0] // 2 + 3)
                        prev = (q, pi)
                stage_V(*prev)
                P.dve("reciprocal", RIS[:], ob[:, 256:257], reads=[bob], writes=[b_RIS])
                P.act("activation", out=OLS[:], in_=ob[:, 0:256], func=AF.Copy, scale=RIS[:, 0:1], reads=[bob, b_RIS], writes=[b_OLS])
                pb, bpb = psum(GBANK, "gb")
                pbb = pb[:, :].bitcast(BF16)
                for m in range(2):
                    P.pe("transpose", pbb[:, m * 128:(m + 1) * 128], OLS[:, m * 128:(m + 1) * 128], ident_b[:],
                         reads=[b_OLS, b_ident_b], writes=[bpb])
                for m in range(2):
                    P.dve("tensor_copy", OTS[:, m, :, 16 * j:16 * j + 16].rearrange("p h (s t) -> p s h t", t=8),
                                                                pbb[:, m * 128:(m + 1) * 128].rearrange("p (s h t) -> p s h t", s=2, h=8),
                          reads=[bpb, b_OTS], writes=[b_OTS])
            for h in range(8):
                pa, bpa = psum(GBANK, "gb")
                for m in range(2):
                    P.pe("matmul", pa[:, 0:N], lhsT=wuv[:, m, h, :], rhs=OTS[:, m, h, :], start=(m == 0), stop=(m == 1),
                         reads=[b_wuv, b_OTS], writes=[bpa])
                merge_head(h, N, pa, bpa)
            DEFB[:] = [0, 1, 2, 3, 4, 5, 6, 7]

        augt = sb([2, 128]); b_augt = Buf()
        cnt["g0"] = 0
        group(128, 1, xd, mods1, mods2, g1s, b_g["g1s"], g2s, b_g["g2s"],
              cs_feat_s.rearrange("p (a t) -> p a t", a=2), y_s, sample_attention, True, preloaded=True)
        for hh in range(2):
            po, bpo = psum()
            for kk in range(4):
                k = hh * 4 + kk
                P.pe("transpose", po[0:32, kk * 128:(kk + 1) * 128], VLAST[:, k, :], ident_f[:], reads=[b_VLAST, b_ident_f], writes=[bpo])
            P.dve("tensor_copy", VT[:, hh * 512:(hh + 1) * 512], po[0:32, :], reads=[bpo, b_VT], writes=[b_VT])
        P.dma("dma_start", out=conv_s, in_=VT[:], reads=[b_VT])

    P.emit()
    return nc


_NC_CACHE = {}


def _rope_tables(pos):
    inv = (np.float32(10000.0) ** (-np.arange(0, 64, 2, dtype=np.float32) / np.float32(64))).astype(np.float32)
    ang = pos.astype(np.float32)[:, None] * inv[None, :]
    return np.cos(ang).astype(np.float32), np.sin(ang).astype(np.float32)


def kernel(x_prompt, x_sample, cache_ckv, cache_krope, state_conv, page_table, c_prompt, c_sample,
           w_ada, b_ada, g_attn, w_in, g_q, w_q_b, g_kv, w_kv_b, conv_w, w_o, g_mlp, w_1, w_2, g_final):
    f32 = np.float32
    A = lambda a: np.ascontiguousarray(np.asarray(a))
    x_prompt = np.asarray(x_prompt, f32); x_sample = np.asarray(x_sample, f32)
    if "nc" not in _NC_CACHE:
        _NC_CACHE["nc"] = build_program()
    nc = _NC_CACHE["nc"]

    ident = np.eye(128, dtype=f32)
    tri = (np.arange(128)[:, None] <= np.arange(128)[None, :]).astype(f32)
    cos_p, sin_p = _rope_tables(np.arange(SEQ))
    cs_tok_p = np.concatenate([cos_p, cos_p, -sin_p, sin_p], axis=1)
    pos_s = 8192 + (np.arange(128) % 8)
    cos_s, sin_s = _rope_tables(pos_s)
    cs_tok_s = np.concatenate([cos_s, cos_s, -sin_s, sin_s], axis=1)
    cs_feat_s = np.concatenate([np.concatenate([cos_s, cos_s], 1).T, np.concatenate([-sin_s, sin_s], 1).T], axis=1)
    smask = np.zeros((128, 8, 128), f32)
    kk_s = np.arange(128) // 8; kk_t = np.arange(128) % 8
    qc = np.arange(128); q_sl = qc // 64; q_t = qc % 8
    for j in range(8):
        smask[:, j, :] = ((kk_s[:, None] == (2 * j + q_sl)[None, :]) & (kk_t[:, None] <= q_t[None, :])).astype(f32)
    augc = np.zeros((2, 256), f32)
    augc[0, 0:64] = -1.0; augc[1, 64:128] = -1.0
    augc[0, 128 + 64:256] = -BIG; augc[1, 128:128 + 64] = -BIG
    kind = np.zeros((2, 128), f32)
    kind[0, :] = ((np.arange(128) // 8) % 2 == 0); kind[1, :] = ((np.arange(128) // 8) % 2 == 1)

    w_in0 = np.asarray(w_in[0], f32)
    q_a_w = w_in0[:, 0:384]; ckv_w = w_in0[:, 384:640]; kr_w = w_in0[:, 640:704]
    bg_w = w_in0[:, 704:1728]; cg_w = w_in0[:, 1728:2752]; xin_w = w_in0[:, 2752:3776]
    ga_w = w_in0[:, 3776:4800]; gb_w = w_in0[:, 4800:5824]
    kr_sw = np.concatenate([kr_w[:, 32:64], kr_w[:, 0:32]], axis=1)
    w_kvin = A(np.concatenate([ckv_w, kr_w, kr_sw], axis=1))
    parts = [q_a_w]
    for fc in range(8):
        sl = slice(fc * 128, (fc + 1) * 128)
        parts += [bg_w[:, sl], cg_w[:, sl], xin_w[:, sl], ga_w[:, sl], gb_w[:, sl]]
    w_main = A(np.concatenate(parts, axis=1))
    wqb = np.asarray(w_q_b[0], f32).reshape(384, 8, 192)
    w_q = A(np.concatenate([wqb[:, :, 0:128], wqb[:, :, 128:192], wqb[:, :, 160:192], wqb[:, :, 128:160]], axis=2).reshape(384, 2048))
    wkvb = np.asarray(w_kv_b[0], f32).reshape(256, 8, 256)
    w_ukT = A(wkvb[:, :, 0:128].transpose(2, 1, 0).reshape(128, 2048))
    w_uv = A(wkvb[:, :, 128:256].reshape(256, 1024))
    convT = A(np.asarray(conv_w[0], f32).T.reshape(8, 128, 3).transpose(1, 0, 2).reshape(128, 24))
    fm = lambda v, n: A(np.asarray(v, f32).reshape(n, 128).T)
    badaT = fm(b_ada[0], 48)
    bada_g = A(np.stack([np.asarray(b_ada[0], f32)[2048:3072], np.asarray(b_ada[0], f32)[5120:6144]]))
    shared = {
        "w_ada": A(np.asarray(w_ada[0], f32)), "badaT": badaT, "bada_g": bada_g,
        "gattnT": fm(g_attn[0], 8), "gmlpT": fm(g_mlp[0], 8), "gqT": fm(g_q[0], 3),
        "gkv": A(np.asarray(g_kv[0], f32)), "gfin": A(np.asarray(g_final, f32)),
        "w_kvin": w_kvin, "w_main": w_main, "w_q": w_q, "w_ukT": w_ukT, "w_uv": w_uv, "convT": convT,
        "w_o": A(np.asarray(w_o[0], f32)), "w_1": A(np.asarray(w_1[0], f32)), "w_2": A(np.asarray(w_2[0], f32)),
        "ident": ident, "cs_tok_p": A(cs_tok_p), "cs_tok_s": A(cs_tok_s), "cs_feat_s": A(cs_feat_s),
        "smask": A(smask.reshape(128, 1024)), "augc": augc, "kind": kind,
        "ckv_pool": np.asarray(cache_ckv[0], f32).reshape(NPOOL, 128 * 256),
        "kr_pool": np.asarray(cache_krope[0], f32).reshape(NPOOL, 128 * 64),
    }
    page_table = np.asarray(page_table, np.int32)
    in_maps = []
    own_rows = []
    for c in range(8):
        b, half = c // 2, c % 2
        tiles = [2 * i + half for i in range(NOWN)]
        rows = np.concatenate([np.arange(t * 128, (t + 1) * 128) for t in tiles])
        own_rows.append(rows)
        xb = x_prompt[b]
        xh = np.zeros((32, D), f32)
        hm = np.ones((128, 16), f32)
        for i, t in enumerate(tiles):
            if t == 0:
                hm[:, i] = 0.0
            else:
                xh[2 * i:2 * i + 2] = xb[t * 128 - 2:t * 128]
        masks = np.concatenate([tri, np.zeros((128, 128), f32)], 1) if half == 0 else np.concatenate([np.ones((128, 128), f32), tri], 1)
        cosq = np.concatenate([cos_p[rows], cos_p[rows]], 1).T
        sinq = np.concatenate([-sin_p[rows], sin_p[rows]], 1).T
        cs_feat_p = np.concatenate([cosq, sinq], axis=1)
        seqs = np.arange(16 * c, 16 * c + 16)
        ptc = np.zeros((128, 8), np.int32)
        for j in range(8):
            ptc[0:64, j] = page_table[seqs[2 * j]]
            ptc[64:128, j] = page_table[seqs[2 * j + 1]]
        m = dict(shared)
        m.update({
            "xs": A(xb), "xo": A(xb[rows]), "xh": xh, "xd": A(x_sample[seqs].reshape(128, D)),
            "cc": A(np.concatenate([np.asarray(c_prompt, f32)[b:b + 1], np.asarray(c_sample, f32)[seqs]], 0)),
            "pt": ptc, "sconv": A(np.asarray(state_conv[0], f32)[seqs].reshape(32, D)),
            "masks": A(masks), "hmask": hm, "cs_feat_p": A(cs_feat_p),
        })
        in_maps.append(m)

    res = run_bass_kernel_spmd(nc, in_maps, core_ids=list(range(8)))
    R = res.results
    y_prompt = np.zeros((4, SEQ, D), f32)
    y_sample = np.zeros((128, 8, D), f32)
    ckv_pr = np.zeros((1, 4, SEQ, 256), f32); kr_pr = np.zeros((1, 4, SEQ, 64), f32); conv_pr = np.zeros((1, 4, 2, D), f32)
    ckv_sm = np.zeros((1, 128, 8, 256), f32); kr_sm = np.zeros((1, 128, 8, 64), f32); conv_sm = np.zeros((1, 128, 2, D), f32)
    for c in range(8):
        b, half = c // 2, c % 2
        r = R[c]
        y_prompt[b, own_rows[c]] = r["y_p"]
        y_sample[16 * c:16 * c + 16] = r["y_s"].reshape(16, 8, D)
        if half == 0:
            ckv_pr[0, b] = r["ckv_p"]; kr_pr[0, b] = r["kr_p"]
        else:
            conv_pr[0, b] = r["conv_p"]
        ckv_sm[0, 16 * c:16 * c + 16] = r["ckv_s"].reshape(16, 8, 256)
        kr_sm[0, 16 * c:16 * c + 16] = r["kr_s"].reshape(16, 8, 64)
        conv_sm[0, 16 * c:16 * c + 16] = r["conv_s"].reshape(16, 2, D)
    return (y_prompt, y_sample, ckv_pr, kr_pr, conv_pr, ckv_sm, kr_sm, conv_sm)
```
